# Optimizing a Trainium2 kernel written in Bass

```python
import math
import jax, jax.numpy as jnp
from jax import lax
import numpy as np

D_MODEL = 1024
BATCH = 1
SEQ = 16384
DEPTH = 1

N_HEADS = 8
N_KV_GROUPS = 2
GROUP_SIZE = N_HEADS // N_KV_GROUPS
HEAD_DIM = 64
NSA_WIDTH = N_HEADS * HEAD_DIM
KV_WIDTH = N_KV_GROUPS * HEAD_DIM
CMP_STRIDE = 16
CMP_BLOCK = 2 * CMP_STRIDE
CMP_HIDDEN = 256
SEL_BLOCK = 64
SEL_PER_CMP = SEL_BLOCK // CMP_STRIDE
N_SELECT = 16
WINDOW = 512
Q_BLOCK = 128
LRU_WIDTH = 512
LRU_BLOCKS = 8
LRU_BLOCK_DIM = LRU_WIDTH // LRU_BLOCKS
CONV_WIDTH = 4
LRU_C = 8.0
N_BUCKETS = 32
MAX_DISTANCE = 128
N_BRANCHES = 2
EPS = 1e-6
NEG = -1e30
FORCE_SCORE = 1e9
IN_SIZES = (NSA_WIDTH, 6 * KV_WIDTH, NSA_WIDTH, 3 * N_HEADS, LRU_WIDTH, LRU_WIDTH, N_BRANCHES * D_MODEL)
D_IN = NSA_WIDTH + 6 * KV_WIDTH + NSA_WIDTH + 3 * N_HEADS + LRU_WIDTH + LRU_WIDTH + N_BRANCHES * D_MODEL

kernel_name = "hybrid_nsa_rglru_gated_block"


def rms_norm(x, g):
    xf = x.astype(jnp.float32)
    y = xf * lax.rsqrt(jnp.mean(xf * xf, axis=-1, keepdims=True) + EPS)
    return (y * g.astype(jnp.float32)).astype(x.dtype)


def t5_bucket(dist):
    n = jnp.maximum(dist, 0)
    max_exact = N_BUCKETS // 2
    nf = jnp.maximum(n, 1).astype(jnp.float32)
    large = max_exact + (jnp.log(nf / max_exact) / math.log(MAX_DISTANCE / max_exact)
                         * (N_BUCKETS - max_exact)).astype(jnp.int32)
    large = jnp.minimum(large, N_BUCKETS - 1)
    return jnp.where(n < max_exact, n, large)


def masked_softmax(s, mask):
    s = jnp.where(mask, s.astype(jnp.float32), NEG)
    return jax.nn.softmax(s, axis=-1) * mask


def compress_blocks(kv, pe, w1, b1, w2):
    B, S, G, dh = kv.shape
    ch = kv.reshape(B, S // CMP_STRIDE, CMP_STRIDE, G, dh)
    blocks = jnp.concatenate([ch[:, :-1], ch[:, 1:]], axis=2)
    blocks = blocks + pe[None, None, :, None, :]
    nc = blocks.shape[1]
    flat = blocks.transpose(0, 1, 3, 2, 4).reshape(B, nc, G, CMP_BLOCK * dh)
    return jax.nn.silu(flat @ w1 + b1) @ w2


def setup_inputs(seed: int = 0) -> dict:
    key = jax.random.key(seed)
    ks = jax.random.split(key, 24)
    nrm = lambda k, shape, s: jax.random.normal(k, shape, jnp.float32) * s
    u = jax.random.uniform(ks[17], (LRU_WIDTH,), jnp.float32, 0.9, 0.999)
    sig = u ** (1.0 / LRU_C)
    lru_lambda = jnp.log(sig) - jnp.log1p(-sig)
    return {
        "x": nrm(ks[0], (BATCH, SEQ, D_MODEL), 1.0),
        "norm_gain": 1.0 + nrm(ks[1], (D_MODEL,), 0.1),
        "w_in": nrm(ks[2], (D_MODEL, D_IN), D_MODEL ** -0.5),
        "q_norm_gain": 1.0 + nrm(ks[3], (HEAD_DIM,), 0.1),
        "k_norm_gain": 1.0 + nrm(ks[4], (3, HEAD_DIM), 0.1),
        "cmp_pe": nrm(ks[5], (2, CMP_BLOCK, HEAD_DIM), 0.5),
        "cmp_w1": nrm(ks[6], (2, CMP_BLOCK * HEAD_DIM, CMP_HIDDEN), (CMP_BLOCK * HEAD_DIM) ** -0.5),
        "cmp_b1": nrm(ks[7], (2, CMP_HIDDEN), 0.02),
        "cmp_w2": nrm(ks[8], (2, CMP_HIDDEN, HEAD_DIM), CMP_HIDDEN ** -0.5),
        "rel_bias": nrm(ks[9], (N_BUCKETS, N_HEADS), 0.5),
        "conv_w": nrm(ks[10], (CONV_WIDTH, LRU_WIDTH), CONV_WIDTH ** -0.5),
        "conv_b": nrm(ks[11], (LRU_WIDTH,), 0.02),
        "lru_wa": nrm(ks[12], (LRU_BLOCKS, LRU_BLOCK_DIM, LRU_BLOCK_DIM), LRU_BLOCK_DIM ** -0.5),
        "lru_ba": nrm(ks[13], (LRU_WIDTH,), 0.02),
        "lru_wx": nrm(ks[14], (LRU_BLOCKS, LRU_BLOCK_DIM, LRU_BLOCK_DIM), LRU_BLOCK_DIM ** -0.5),
        "lru_bx": nrm(ks[15], (LRU_WIDTH,), 0.02),
        "lru_lambda": lru_lambda,
        "w_proj_a": nrm(ks[18], (NSA_WIDTH, D_MODEL), NSA_WIDTH ** -0.5),
        "w_proj_b": nrm(ks[19], (LRU_WIDTH, D_MODEL), LRU_WIDTH ** -0.5),
        "w_out": nrm(ks[20], (D_MODEL, D_MODEL), D_MODEL ** -0.5),
    }


def nsa_mixer(q, k_cmp, v_cmp, k_slc, v_slc, k_win, v_win, br_gate, q_norm_gain, k_norm_gain,
              cmp_pe, cmp_w1, cmp_b1, cmp_w2, rel_bias):
    B, S, _ = q.shape
    G, R, dh = N_KV_GROUPS, GROUP_SIZE, HEAD_DIM
    nsb = S // SEL_BLOCK
    n_sel = min(N_SELECT, nsb)
    scale = dh ** -0.5

    q = (rms_norm(q.reshape(B, S, G, R, dh), q_norm_gain) * scale).transpose(0, 2, 3, 1, 4)
    kc = rms_norm(compress_blocks(k_cmp, cmp_pe[0], cmp_w1[0], cmp_b1[0], cmp_w2[0]), k_norm_gain[0])
    vc = compress_blocks(v_cmp, cmp_pe[1], cmp_w1[1], cmp_b1[1], cmp_w2[1])
    kc = kc.transpose(0, 2, 1, 3)
    vc = vc.transpose(0, 2, 1, 3)
    nc = kc.shape[2]
    k_blocks = rms_norm(k_slc, k_norm_gain[1]).transpose(0, 2, 1, 3).reshape(B, G, nsb, SEL_BLOCK, dh)
    v_blocks = v_slc.transpose(0, 2, 1, 3).reshape(B, G, nsb, SEL_BLOCK, dh)
    pad_w = ((0, 0), (0, 0), (WINDOW, 0), (0, 0))
    k_wp = jnp.pad(rms_norm(k_win, k_norm_gain[2]).transpose(0, 2, 1, 3), pad_w)
    v_wp = jnp.pad(v_win.transpose(0, 2, 1, 3), pad_w)

    bias_gr = rel_bias.reshape(N_BUCKETS, G, R)
    def shared_bias(dist):
        return rel_bias[t5_bucket(dist)].transpose(2, 0, 1).reshape(G, R, *dist.shape)
    def group_bias(dist):
        b = jax.vmap(lambda tab, bk: tab[bk], in_axes=(1, 1), out_axes=1)(bias_gr, t5_bucket(dist))
        return jnp.moveaxis(b, -1, 2)

    c_end = jnp.arange(nc, dtype=jnp.int32) * CMP_STRIDE + CMP_BLOCK - 1
    blk = jnp.arange(nsb, dtype=jnp.int32)
    bi = jnp.arange(B)[:, None, None, None]
    gi = jnp.arange(G)[None, :, None, None]
    right_pad = SEL_PER_CMP * nsb + SEL_PER_CMP - 1 - nc

    def block_fn(q0):
        qc = lax.dynamic_slice_in_dim(q, q0, Q_BLOCK, axis=3)
        t = q0 + jnp.arange(Q_BLOCK, dtype=jnp.int32)
        dist = t[:, None] - c_end[None, :]
        s = jnp.einsum('bgrqd,bgcd->bgrqc', qc, kc) + shared_bias(dist)
        p_cmp = masked_softmax(s, dist >= 0)
        o_cmp = jnp.einsum('bgrqc,bgcd->bgrqd', p_cmp.astype(vc.dtype), vc)
        imp = jnp.pad(p_cmp.sum(axis=2), ((0, 0), (0, 0), (0, 0), (1, right_pad)))
        imp_blk = (imp[..., :SEL_PER_CMP * nsb].reshape(B, G, Q_BLOCK, nsb, SEL_PER_CMP).sum(-1)
                   + imp[..., SEL_PER_CMP:SEL_PER_CMP * nsb + SEL_PER_CMP:SEL_PER_CMP])
        cur = (t // SEL_BLOCK)[:, None]
        valid = blk[None, :] <= cur
        force = (blk[None, :] == 0) | (blk[None, :] == cur) | (blk[None, :] == cur - 1)
        score = jnp.where(valid, jnp.where(force, FORCE_SCORE, imp_blk), -1.0)
        _, idx = lax.top_k(score, n_sel)
        kb = k_blocks[bi, gi, idx].reshape(B, G, Q_BLOCK, n_sel * SEL_BLOCK, dh)
        vb = v_blocks[bi, gi, idx].reshape(B, G, Q_BLOCK, n_sel * SEL_BLOCK, dh)
        pos = (idx[..., None] * SEL_BLOCK + jnp.arange(SEL_BLOCK, dtype=jnp.int32)).reshape(B, G, Q_BLOCK, -1)
        dist = t[None, None, :, None] - pos
        s = jnp.einsum('bgrqd,bgqkd->bgrqk', qc, kb) + group_bias(dist)
        p = masked_softmax(s, (dist >= 0)[:, :, None])
        o_slc = jnp.einsum('bgrqk,bgqkd->bgrqd', p.astype(vb.dtype), vb)
        kw = lax.dynamic_slice_in_dim(k_wp, q0, WINDOW + Q_BLOCK, axis=2)
        vw = lax.dynamic_slice_in_dim(v_wp, q0, WINDOW + Q_BLOCK, axis=2)
        pos = q0 - WINDOW + jnp.arange(WINDOW + Q_BLOCK, dtype=jnp.int32)
        dist = t[:, None] - pos[None, :]
        mask = (dist >= 0) & (dist < WINDOW) & (pos[None, :] >= 0)
        s = jnp.einsum('bgrqd,bgkd->bgrqk', qc, kw) + shared_bias(dist)
        p = masked_softmax(s, mask)
        o_win = jnp.einsum('bgrqk,bgkd->bgrqd', p.astype(vw.dtype), vw)
        return o_cmp, o_slc, o_win

    starts = jnp.arange(0, S, Q_BLOCK, dtype=jnp.int32)
    outs = lax.map(block_fn, starts)
    o = jnp.stack([ob.transpose(1, 0, 4, 2, 3, 5).reshape(B, S, N_HEADS, dh) for ob in outs], axis=2)
    g = jax.nn.sigmoid(br_gate.reshape(B, S, 3, N_HEADS, 1))
    return (g * o).sum(axis=2).reshape(B, S, NSA_WIDTH)


def rglru_mixer(u, conv_w, conv_b, lru_wa, lru_ba, lru_wx, lru_bx, lru_lambda):
    B, S, W = u.shape
    up = jnp.pad(u, ((0, 0), (CONV_WIDTH - 1, 0), (0, 0)))
    uc = conv_b + sum(up[:, k:k + S] * conv_w[k] for k in range(CONV_WIDTH))
    ub = uc.reshape(B, S, LRU_BLOCKS, LRU_BLOCK_DIM)
    r = jax.nn.sigmoid(jnp.einsum('bsnd,nde->bsne', ub, lru_wa).reshape(B, S, W) + lru_ba)
    i = jax.nn.sigmoid(jnp.einsum('bsnd,nde->bsne', ub, lru_wx).reshape(B, S, W) + lru_bx)
    log_a = (-LRU_C * r.astype(jnp.float32)) * jax.nn.softplus(-lru_lambda.astype(jnp.float32))
    a = jnp.exp(log_a)
    b = jnp.sqrt(-jnp.expm1(2.0 * log_a)) * (i * uc).astype(jnp.float32)
    def comb(left, right):
        a1, b1 = left
        a2, b2 = right
        return a1 * a2, a2 * b1 + b2
    _, h = lax.associative_scan(comb, (a, b), axis=1)
    return h.astype(u.dtype)


def reference(x, norm_gain, w_in, q_norm_gain, k_norm_gain, cmp_pe, cmp_w1, cmp_b1, cmp_w2, rel_bias,
              conv_w, conv_b, lru_wa, lru_ba, lru_wx, lru_bx, lru_lambda, w_proj_a, w_proj_b, w_out):
    B, S, _ = x.shape
    split_idx = np.cumsum(IN_SIZES)[:-1].tolist()
    for _layer in range(DEPTH):
        h = rms_norm(x, norm_gain)
        proj = h @ w_in
        q, kv_all, g_nsa, br_gate, u_lru, g_lru, merge_g = jnp.split(proj, split_idx, axis=-1)
        kv_all = kv_all.reshape(B, S, 6, N_KV_GROUPS, HEAD_DIM)
        y_a = nsa_mixer(q, kv_all[:, :, 0], kv_all[:, :, 1], kv_all[:, :, 2], kv_all[:, :, 3],
                        kv_all[:, :, 4], kv_all[:, :, 5], br_gate, q_norm_gain, k_norm_gain,
                        cmp_pe, cmp_w1, cmp_b1, cmp_w2, rel_bias)
        y_a = (y_a * jax.nn.silu(g_nsa)) @ w_proj_a
        y_b = rglru_mixer(u_lru, conv_w, conv_b, lru_wa, lru_ba, lru_wx, lru_bx, lru_lambda)
        y_b = (y_b * jax.nn.silu(g_lru)) @ w_proj_b
        gate_a, gate_b = jnp.split(merge_g, N_BRANCHES, axis=-1)
        m = jax.nn.sigmoid(gate_a) * y_a + jax.nn.sigmoid(gate_b) * y_b
        x = x + m @ w_out
    return x
```

```python
import numpy as np
import ml_dtypes
import concourse.bass as bass
import concourse.mybir as mybir
from concourse.bass_utils import run_bass_kernel_spmd

F32 = mybir.dt.float32
BF16 = mybir.dt.bfloat16
AF = mybir.ActivationFunctionType
ALU = mybir.AluOpType
NEG = -30000.0
S = 16384
NCHUNK = 128
EPS = 1e-6


class Buf:
    __slots__ = ("w", "r")

    def __init__(self):
        self.w = None
        self.r = {}


class Eng:
    def __init__(self, name, handle, sem):
        self.name = name
        self.h = handle
        self.sem = sem
        self.count = 0
        self.waited = {}


class Ctx:
    NDSEM = 24

    def __init__(self, nc):
        self.nc = nc
        self.E = {}
        for name, h in (("pe", nc.tensor), ("act", nc.scalar), ("dve", nc.vector), ("pool", nc.gpsimd), ("sp", nc.sync)):
            self.E[name] = Eng(name, h, nc.alloc_semaphore(name="sem_" + name))
        self.dsems = [nc.alloc_semaphore(name=f"dsem{i}") for i in range(self.NDSEM)]
        self.dcnt = [0] * self.NDSEM
        self.dnext = 0

    def _wait(self, E, tok):
        sem, val, owner = tok
        if owner == "pe" and E.name == "pe":
            return
        key = id(sem)
        if E.waited.get(key, 0) >= val:
            return
        E.h.wait_ge(sem, val)
        E.waited[key] = val

    def _deps(self, E, reads, writes):
        for b in reads:
            if b.w is not None:
                self._wait(E, b.w)
        for b in writes:
            if b.w is not None:
                self._wait(E, b.w)
            for t in b.r.values():
                self._wait(E, t)

    def _mark(self, tok, reads, writes):
        key = tok[2] if tok[2] != "dma" else id(tok[0])
        for b in reads:
            b.r[key] = tok
        for b in writes:
            b.w = tok
            b.r = {}

    def op(self, eng, fn, reads=(), writes=()):
        E = self.E[eng]
        self._deps(E, reads, writes)
        ins = fn(E.h)
        E.count += 1
        ins.then_inc(E.sem, 1)
        self._mark((E.sem, E.count, eng), reads, writes)

    def dma(self, out, in_, reads=(), writes=(), queue="sp", **kw):
        E = self.E[queue]
        k = self.dnext
        self.dnext = (self.dnext + 1) % self.NDSEM
        sem = self.dsems[k]
        if self.dcnt[k] > 0:
            self._wait(E, (sem, 16 * self.dcnt[k], "dma"))
        self._deps(E, reads, writes)
        ins = E.h.dma_start(out=out, in_=in_, **kw)
        self.dcnt[k] += 1
        ins.then_inc(sem, 16)
        self._mark((sem, 16 * self.dcnt[k], "dma"), reads, writes)

    def barrier(self):
        toks = [(e.sem, e.count, e.name) for e in self.E.values() if e.count > 0]
        toks += [(self.dsems[k], 16 * self.dcnt[k], "dma") for k in range(self.NDSEM) if self.dcnt[k] > 0]
        for e in self.E.values():
            for t in toks:
                if t[2] == e.name:
                    continue
                key = id(t[0])
                if e.waited.get(key, 0) >= t[1]:
                    continue
                e.h.wait_ge(t[0], t[1])
                e.waited[key] = t[1]


def t5_bucket_np(dist):
    n = np.maximum(dist, 0)
    nf = np.maximum(n, 1).astype(np.float32)
    large = 16 + (np.log(nf / np.float32(16)) / np.float32(np.log(128 / 16)) * np.float32(16)).astype(np.int32)
    large = np.minimum(large, 31)
    return np.where(n < 16, n, large)


W_SEGS = [("kc0", 512, 64), ("vc0", 640, 64), ("kc1", 576, 64), ("vc1", 704, 64), ("ks", 768, 128),
          ("kw", 1024, 128), ("vs", 896, 128), ("vw", 1152, 128), ("u", 1816, 512),
          ("q", 0, 512), ("bg", 1792, 24)]
W_OFF = {}
_o = 0
for _n, _c, _w in W_SEGS:
    W_OFF[_n] = _o
    _o += _w
NB = _o


def build_program():
    nc = bass.Bass("TRN2", target_bir_lowering=False)
    C = Ctx(nc)

    import os
    STOP = os.environ.get("KSTOP", "")

    class _Stop(Exception):
        pass

    def ck(name):
        if STOP == name:
            raise _Stop()

    try:
        def din(name, shape, dt=F32):
            return nc.dram_tensor(name, list(shape), dt, kind="ExternalInput").ap()

        x_d = din("x", [S, 1024])
        w_in = din("w_in", [1024, 4888])
        norm_gain = din("norm_gain", [1024])
        q_norm_gain = din("q_norm_gain", [64])
        k_norm_gain = din("k_norm_gain", [3, 64])
        cmp_pe = din("cmp_pe", [2, 32, 64])
        cmp_w1 = din("cmp_w1", [2, 2048, 256])
        cmp_b1 = din("cmp_b1", [2, 256])
        cmp_w2 = din("cmp_w2", [2, 256, 64])
        conv_w = din("conv_w", [4, 512])
        conv_b = din("conv_b", [512])
        lru_wa = din("lru_wa", [8, 64, 64])
        lru_ba = din("lru_ba", [512])
        lru_wx = din("lru_wx", [8, 64, 64])
        lru_bx = din("lru_bx", [512])
        lru_lambda = din("lru_lambda", [512])
        w_proj_a = din("w_proj_a", [512, 1024])
        w_proj_b = din("w_proj_b", [512, 1024])
        w_out = din("w_out", [1024, 1024])
        c_identb = din("c_identb", [128, 128], BF16)
        c_identf = din("c_identf", [128, 128])
        c_onesbd = din("c_onesbd", [128, 128], BF16)
        c_td = din("c_td", [128, 1024])
        c_tp = din("c_tp", [128, 1024])
        c_tw0 = din("c_tw0", [128, 512])
        c_gc = din("c_gc", [128, 1024])
        c_b31 = din("c_b31", [128, 8])
        c_bigsh = din("c_bigsh", [128, 384])
        c_ind = din("c_ind", [64, 4096], BF16)
        c_pool = din("c_pool", [128, 2048], BF16)
        c_wrel = din("c_wrel", [128, 512])
        c_wcore = din("c_wcore", [128, 256])
        c_valid = din("c_valid", [128, 128])
        c_cvalid = din("c_cvalid", [128, 8])
        c_lmask = din("c_lmask", [128, 1024], BF16)
        out_d = nc.dram_tensor("out", [16, 128, 1024], F32, kind="ExternalOutput").ap()
        kcmp_d = nc.dram_tensor("kcmp_scr", [2, 128, 16400], BF16, kind="Internal").ap()
        _dk = "ExternalOutput" if os.environ.get("KDEBUG", "") else "Internal"
        ow_d = nc.dram_tensor("ow_scr", [16, 128, 512], F32, kind=_dk).ap()
        oa_d = nc.dram_tensor("oa_scr", [16, 128, 512], F32, kind=_dk).ap()
        hl_d = nc.dram_tensor("hl_scr", [16, 4, 128, 128], F32, kind="Internal").ap()
        u_d = nc.dram_tensor("u_scr", [4, 128, 16387], F32, kind="Internal").ap()
        b_ud = [Buf() for _ in range(4)]
        b_kcmp = [Buf(), Buf()]
        b_ow = [Buf() for _ in range(16)]
        b_oa = [Buf() for _ in range(16)]
        b_hl = [Buf() for _ in range(16)]
        b_out = [Buf() for _ in range(16)]

        import contextlib
        cur_stack = [None]

        def sb(name, shape, dt=F32):
            if cur_stack[0] is None:
                return nc.alloc_sbuf_tensor(name, list(shape), dt), Buf()
            return cur_stack[0].enter_context(nc.sbuf_tensor(name, list(shape), dt)), Buf()

        def pe_mm(out, lhsT, rhs, start, stop, reads, writes):
            C.op("pe", lambda e: e.matmul(out, lhsT=lhsT, rhs=rhs, start=start, stop=stop, skip_group_check=True), reads, writes)

        def act(out, in_, func, reads, writes, **kw):
            C.op("act", lambda e: e.activation(out=out, in_=in_, func=func, **kw), reads, writes)

        def ts(eng, out, in0, s1, s2, op0, op1, reads, writes):
            if op1 is None:
                C.op(eng, lambda e: e.tensor_scalar(out=out, in0=in0, scalar1=s1, scalar2=None, op0=op0), reads, writes)
            else:
                C.op(eng, lambda e: e.tensor_scalar(out=out, in0=in0, scalar1=s1, scalar2=s2, op0=op0, op1=op1), reads, writes)

        def stt(out, in0, scalar, in1, op0, op1, reads, writes):
            C.op("dve", lambda e: e.scalar_tensor_tensor(out=out, in0=in0, scalar=scalar, in1=in1, op0=op0, op1=op1), reads, writes)

        def tt(eng, out, in0, in1, op, reads, writes):
            C.op(eng, lambda e: e.tensor_tensor(out=out, in0=in0, in1=in1, op=op), reads, writes)

        def cp(eng, out, in_, reads, writes):
            if eng == "act":
                C.op("act", lambda e: e.copy(out=out, in_=in_), reads, writes)
            else:
                C.op(eng, lambda e: e.tensor_copy(out=out, in_=in_), reads, writes)

        ps_tp = nc.alloc_psum_tensor("ps_tp", [128, 1024], BF16)
        b_ps_tp = Buf()
        PS = []
        for j in range(7):
            PS.append((nc.alloc_psum_tensor(f"ps{j}", [128, 512], F32), Buf()))

        identb, b_identb = sb("identb", [128, 128], BF16)
        identf, b_identf = sb("identf", [128, 128])
        onesbd, b_onesbd = sb("onesbd", [128, 128], BF16)
        gain_bc, b_gain = sb("gain_bc", [128, 1024])
        sg_all, b_sg = sb("sg_all", [128, 16 * 24])
        kgain, b_kgain = sb("kgain", [128, 3])
        qgain, b_qgain = sb("qgain", [128, 1])
        stage = [sb(f"stage{j}", [128, 512]) for j in range(2)]
        stage_i = [0]
        xring = [sb(f"xr{j}", [128, 1024]) for j in range(4)]
        xn, b_xn = sb("xn", [128, 1024], BF16)
        xn2, b_xn2 = sb("xn2", [128, 1024], BF16)
        xn_l = [(xn, b_xn), (xn2, b_xn2)]
        tp_l = [(ps_tp, b_ps_tp), (ps_tp, b_ps_tp)]
        sqjunk, b_sqjunk = sb("sqjunk", [128, 1024], BF16)
        hT, b_hT = sb("hT", [128, 8 * 512], BF16)
        smalls_l = [sb(f"smalls{j}", [128, 4]) for j in range(4)]

        C.dma(identb[:], c_identb, writes=[b_identb])
        C.dma(identf[:], c_identf, writes=[b_identf])
        C.dma(onesbd[:], c_onesbd, writes=[b_onesbd])
        C.dma(gain_bc[:], norm_gain.partition_broadcast(128), writes=[b_gain])
        for half in range(2):
            C.dma(kgain[half * 64:(half + 1) * 64, :], k_norm_gain.rearrange("j d -> d j"), writes=[b_kgain],
                  allow_slow_non_contiguous=True)
            C.dma(qgain[half * 64:(half + 1) * 64, :], q_norm_gain.rearrange("(d o) -> d o", o=1), writes=[b_qgain],
                  allow_slow_non_contiguous=True)

        def load_w_bf16(dst, src, ncols, b_dst):
            st, b_st = stage[stage_i[0] % len(stage)]
            stage_i[0] += 1
            C.dma(st[:, 0:ncols], src, writes=[b_st])
            eng = "dve" if stage_i[0] % 2 else "act"
            cp(eng, dst, st[:, 0:ncols], [b_st], [b_dst])

        def norm_chunk_a(xt, b_xt, slot):
            sm, b_sm = smalls_l[slot]
            act(sqjunk[:], xt[:], AF.Square, [b_xt], [b_sqjunk, b_sm], accum_out=sm[:, 0:1])
            act(sm[:, 1:2], sm[:, 0:1], AF.Ln, [b_sm], [b_sm], scale=1.0 / 1024, bias=EPS)
            act(sm[:, 2:3], sm[:, 1:2], AF.Exp, [b_sm], [b_sm], scale=-0.5)

        def norm_chunk_b1(xt, b_xt, slot, par=0):
            sm, b_sm = smalls_l[slot]
            xn_, b_xn_ = xn_l[par]
            tp_, b_tp_ = tp_l[par]
            stt(xn_[:], xt[:], sm[:, 2:3], gain_bc[:], ALU.mult, ALU.mult, [b_xt, b_sm, b_gain], [b_xn_])
            for k in range(8):
                C.op("pe", lambda e: e.transpose(out=tp_[:, k * 128:(k + 1) * 128], in_=xn_[:, k * 128:(k + 1) * 128],
                                                 identity=identb[:]), [b_xn_, b_identb], [b_tp_])

        def norm_chunk_b2(col0, hT_=None, b_hT_=None, par=0):
            if hT_ is None:
                hT_, b_hT_ = hT, b_hT
            tp_, b_tp_ = tp_l[par]
            for k in range(8):
                cp("act" if k % 2 else "dve", hT_[:, k * 512 + col0: k * 512 + col0 + 128], tp_[:, k * 128:(k + 1) * 128],
                   [b_tp_], [b_hT_])

        def norm_chunk_b(xt, b_xt, col0, slot, hT_=None, b_hT_=None, par=0):
            norm_chunk_b1(xt, b_xt, slot, par)
            norm_chunk_b2(col0, hT_, b_hT_, par)

        def norm_chunk(xt, b_xt, col0, hT_=None, b_hT_=None, par=0):
            norm_chunk_a(xt, b_xt, par)
            norm_chunk_b(xt, b_xt, col0, par, hT_, b_hT_, par)

        def head_norm1(src_aps, n, tmp, b_tmp, reads):
            sqb, rt = tmp
            for (src, lo, hi) in src_aps:
                act(sqb[lo:hi, 0:n], src, AF.Square, reads, [b_tmp])

        def head_norm2(src_aps, gain_ap, out_ap, n, sc, bs, ps_n, b_ps_n, tmp, b_tmp, reads, writes):
            sqb, rt = tmp
            pe_mm(ps_n[:, 0:n], onesbd[:], sqb[:, 0:n], True, True, [b_onesbd, b_tmp], [b_ps_n])
            act(rt[:, 0:n], ps_n[:, 0:n], AF.Ln, [b_ps_n], [b_tmp], scale=sc, bias=bs)
            act(rt[:, 0:n], rt[:, 0:n], AF.Exp, [b_tmp], [b_tmp], scale=-0.5)
            for (src, lo, hi) in src_aps:
                stt(out_ap[lo:hi, :], src, gain_ap[lo:hi, :], rt[lo:hi, 0:n], ALU.mult, ALU.mult,
                    reads + [b_tmp, b_kgain, b_qgain], writes)

        def head_norm(src_aps, gain_ap, out_ap, n, sc, bs, ps_n, b_ps_n, tmp, b_tmp, reads, writes):
            head_norm1(src_aps, n, tmp, b_tmp, reads)
            head_norm2(src_aps, gain_ap, out_ap, n, sc, bs, ps_n, b_ps_n, tmp, b_tmp, reads, writes)

        sc1 = contextlib.ExitStack()
        cur_stack[0] = sc1
        Ks, b_Ks = sb("Ks", [128, S], BF16)
        Vs, b_Vs = sb("Vs", [128, NCHUNK * 130], BF16)
        kcT, b_kcT = sb("kcT", [128, 1024], BF16)
        vcx, b_vcx = sb("vcx", [128, 8 * 130], BF16)
        q_all, b_q = sb("q_all", [128, 16 * 512], BF16)
        TDp, b_TDp = sb("TDp", [128, 1024])
        TPp, b_TPp = sb("TPp", [128, 1024])
        b31, b_b31 = sb("b31", [128, 8])
        valid, b_valid = sb("valid", [128, 128])
        C.dma(TDp[:], c_td, writes=[b_TDp])
        C.dma(TPp[:], c_tp, writes=[b_TPp])
        C.dma(b31[:], c_b31, writes=[b_b31])
        C.dma(valid[:], c_valid, writes=[b_valid])
        for h in range(8):
            for (T_, bT) in ((TDp, b_TDp), (TPp, b_TPp)):
                ts("dve", T_[:, h * 128:(h + 1) * 128], T_[:, h * 128:(h + 1) * 128], b31[:, h:h + 1], None, ALU.subtract, None,
                   [bT, b_b31], [bT])

        ck('setup0')
        phB = contextlib.ExitStack()
        cur_stack[0] = phB
        for j in range(4):
            stage.append(sb(f"stageB{j}", [128, 512]))
        WB, b_WB = sb("WB", [128, 8 * NB], BF16)
        for k in range(8):
            for (nm, c0, w) in W_SEGS:
                o = k * NB + W_OFF[nm]
                load_w_bf16(WB[:, o:o + w], w_in[k * 128:(k + 1) * 128, c0:c0 + w], w, b_WB)
        TW0, b_TW0 = sb("TW0", [128, 512])
        C.dma(TW0[:], c_tw0, writes=[b_TW0])
        hT2, b_hT2 = sb("hT2", [128, 8 * 512], BF16)
        hT_l = [(hT, b_hT), (hT2, b_hT2)]
        tp2 = PS[5][0][:].bitcast(BF16)
        tp_l[1] = (tp2, PS[5][1])
        ust = [sb(f"ust{j}", [128, 512]) for j in range(2)]
        zt3, b_zt3 = sb("zt3", [128, 3])
        C.op("dve", lambda e: e.memset(zt3[:], 0.0), [], [b_zt3])
        for ct in range(4):
            C.dma(u_d[ct, :, 0:3], zt3[:], reads=[b_zt3], writes=[b_ud[ct]])
        nt_sq, _ = sb("nt_sq", [128, 512], BF16)
        nt_rt, _ = sb("nt_rt", [128, 512])
        b_nt = Buf()
        nt2_sq, _ = sb("nt2_sq", [128, 512], BF16)
        nt2_rt, _ = sb("nt2_rt", [128, 512])
        b_nt2 = Buf()
        cst = [sb(f"cst{g}", [128, 512], BF16) for g in range(2)]
        Kw, b_Kw = sb("Kw", [128, 1024], BF16)
        Vw, b_Vw = sb("Vw", [128, 8 * 130], BF16)
        pw = [sb(f"pw{j}", [128, 512], BF16) for j in range(2)]
        ow_t, b_ow_t = sb("ow_t", [128, 512])
        coef, b_coef = sb("coef", [128, 8])
        zt, b_zt = sb("zt", [128, 16], BF16)
        C.op("dve", lambda e: e.memset(zt[:], 0.0), [], [b_zt])
        for g in range(2):
            C.dma(kcmp_d[g, :, 16384:16400], zt[:], reads=[b_zt], writes=[b_kcmp[g]])

        def wb(k, nm, off=0, w=128):
            o = k * NB + W_OFF[nm] + off
            return WB[:, o:o + w]

        ps_a, b_ps_a = PS[0]
        ps_b, b_ps_b = PS[1]
        ps_n, b_ps_n = PS[2]
        ps_v, b_ps_v = PS[3]
        ps_s = [PS[4], PS[3]]
        ps_o, b_ps_o = PS[6]
        fm_i = [0]

        cur_hT = [hT, b_hT]

        def fm_proj(nm, off, ncol_tile=512, col0=0):
            pt, bp = (ps_a, b_ps_a) if fm_i[0] % 2 == 0 else (ps_b, b_ps_b)
            fm_i[0] += 1
            for k in range(8):
                pe_mm(pt[:, 0:ncol_tile], wb(k, nm, off), cur_hT[0][:, k * 512 + col0:k * 512 + col0 + ncol_tile], k == 0, k == 7,
                      [b_WB, cur_hT[1]], [bp])
            return pt, bp

        ck('setupB')
        for t in range(32):
            if t == 1: ck('B1')
            if t == 2: ck('B2t')
            hT_c, b_hT_c = hT_l[t % 2]
            cur_hT[0], cur_hT[1] = hT_c, b_hT_c

            def emit_norm_a(tt_):
                if tt_ == 0:
                    for ch in range(4):
                        C.dma(xring[ch][0][:], x_d[ch * 128:(ch + 1) * 128, :], writes=[xring[ch][1]])
                for ch in range(4):
                    s = 4 * tt_ + ch
                    norm_chunk_a(xring[s % 4][0], xring[s % 4][1], ch)

            def emit_norm_b(tt_):
                hTn, b_hTn = hT_l[tt_ % 2]
                for ch in range(5):
                    if ch < 4:
                        s = 4 * tt_ + ch
                        norm_chunk_b1(xring[s % 4][0], xring[s % 4][1], ch, par=s % 2)
                    if ch >= 1:
                        norm_chunk_b2((ch - 1) * 128, hTn, b_hTn, par=(4 * tt_ + ch - 1) % 2)
                if tt_ + 1 < 32:
                    for ch in range(4):
                        s2 = 4 * (tt_ + 1) + ch
                        C.dma(xring[s2 % 4][0][:], x_d[s2 * 128:(s2 + 1) * 128, :], writes=[xring[s2 % 4][1]])

            if t == 0:
                emit_norm_a(0)
                emit_norm_b(0)
            if t + 1 < 32:
                emit_norm_a(t + 1)
            for g in range(2):
                pt, bp = fm_proj("kc0" if g == 0 else "kc1", 0)
                cp("act", cst[g][0][:], pt[:], [bp], [cst[g][1]])
                C.dma(kcmp_d[g, :, t * 512:(t + 1) * 512], cst[g][0][:], reads=[cst[g][1]], writes=[b_kcmp[g]])
            pt1, bp1 = fm_proj("ks", 0)
            head_norm1([(pt1[:, :], 0, 128)], 512, (nt_sq, nt_rt), b_nt, [bp1])
            pt2, bp2 = fm_proj("kw", 0)
            head_norm1([(pt2[:, :], 0, 128)], 512, (nt2_sq, nt2_rt), b_nt2, [bp2])
            head_norm2([(pt1[:, :], 0, 128)], kgain[:, 1:2], Ks[:, t * 512:(t + 1) * 512], 512, 1.0 / 64, EPS, ps_n, b_ps_n,
                       (nt_sq, nt_rt), b_nt, [bp1], [b_Ks])
            head_norm2([(pt2[:, :], 0, 128)], kgain[:, 2:3], Kw[:, (t % 2) * 512:(t % 2) * 512 + 512], 512, 1.0 / 64, EPS, ps_n,
                       b_ps_n, (nt2_sq, nt2_rt), b_nt2, [bp2], [b_Kw])
            for ch in range(4):
                s = 4 * t + ch
                pv_, b_pv_ = (ps_v, b_ps_v) if ch % 2 == 0 else PS[4]
                for k in range(8):
                    pe_mm(pv_[:, 0:256], hT_c[:, k * 512 + ch * 128:k * 512 + ch * 128 + 128], wb(k, "vs", 0, 256), k == 0, k == 7,
                          [b_hT_c, b_WB], [b_pv_])
                rs = s % 8
                for g in range(2):
                    cp("act", Vs[:, s * 130 + g * 65:s * 130 + g * 65 + 64], pv_[:, g * 64:(g + 1) * 64], [b_pv_], [b_Vs])
                    cp("dve", Vw[:, rs * 130 + g * 65:rs * 130 + g * 65 + 64], pv_[:, 128 + g * 64:128 + (g + 1) * 64],
                       [b_pv_], [b_Vw])
                    cp("dve", Vs[:, s * 130 + g * 65 + 64:s * 130 + g * 65 + 65], valid[:, s:s + 1], [b_valid], [b_Vs])
                    cp("dve", Vw[:, rs * 130 + g * 65 + 64:rs * 130 + g * 65 + 65], valid[:, s:s + 1], [b_valid], [b_Vw])
            own = (t % 2 == 1)
            qi = (t - 1) // 2
            if own:
                qps = []
                for g in range(2):
                    pt, bp = PS[6] if g == 0 else PS[4]
                    for r in range(4):
                        h = 4 * g + r
                        off = h * 64 - 64 * g
                        for k in range(8):
                            pe_mm(pt[:, r * 128:(r + 1) * 128], wb(k, "q", off), hT_c[:, k * 512 + 384:k * 512 + 512], k == 0, k == 7,
                                  [b_WB, b_hT_c], [bp])
                    qps.append((pt, bp))
                qn = q_all[:, qi * 512:(qi + 1) * 512]
                q_src = [(qps[0][0][0:64, :], 0, 64), (qps[1][0][64:128, :], 64, 128)]
                head_norm1(q_src, 512, (nt_sq, nt_rt), b_nt, [qps[0][1], qps[1][1]])
                for k in range(8):
                    pe_mm(ps_v[:, 256:280], hT_c[:, k * 512 + 384:k * 512 + 512], wb(k, "bg", 0, 24), k == 0, k == 7, [b_hT_c, b_WB], [b_ps_v])
                act(sg_all[:, qi * 24:(qi + 1) * 24], ps_v[:, 256:280], AF.Exp, [b_ps_v], [b_sg], scale=-1.0)
                ts("dve", sg_all[:, qi * 24:(qi + 1) * 24], sg_all[:, qi * 24:(qi + 1) * 24], 1.0, None, ALU.add, None, [b_sg], [b_sg])
                C.op("dve", lambda e: e.reciprocal(out=sg_all[:, qi * 24:(qi + 1) * 24], in_=sg_all[:, qi * 24:(qi + 1) * 24]), [b_sg], [b_sg])
            for ct in range(4):
                pt, bp = fm_proj("u", ct * 128)
                us_, b_us = ust[ct % 2]
                cp("act" if ct % 2 else "dve", us_[:], pt[:], [bp], [b_us])
                C.dma(u_d[ct, :, 3 + t * 512:3 + (t + 1) * 512], us_[:], reads=[b_us], writes=[b_ud[ct]])
            if own:
                head_norm2(q_src, qgain[:, 0:1], qn, 512, 1.0, 64 * EPS, ps_n, b_ps_n, (nt_sq, nt_rt), b_nt,
                           [qps[0][1], qps[1][1]], [b_q])
            if t + 1 < 32:
                emit_norm_b(t + 1)
            if not own:
                continue
            items = [(g, w) for g in range(2) for w in range(5)]
            po_l = [PS[6], PS[0]]

            def win_qk(idx):
                g, w = items[idx]
                lo, hi = g * 64, (g + 1) * 64
                rc = 3 + w
                pst, bps = ps_s[idx % 2]
                tab = {0: (TW0, b_TW0, 0), 3: (TPp, b_TPp, g * 512), 4: (TDp, b_TDp, g * 512)}.get(w)
                pe_mm(pst[:], Kw[lo:hi, rc * 128:(rc + 1) * 128], qn[lo:hi, :], True, tab is None, [b_Kw, b_q], [bps])
                if tab is not None:
                    pe_mm(pst[:], identf[:], tab[0][:, tab[2]:tab[2] + 512], False, True, [b_identf, tab[1]], [bps])

            win_qk(0)
            for idx in range(10):
                g, w = items[idx]
                rc = 3 + w
                if idx + 1 < 10:
                    win_qk(idx + 1)
                pst, bps = ps_s[idx % 2]
                pt_, bpt = pw[idx % 2]
                po, b_po = po_l[g]
                act(pt_[:], pst[:], AF.Exp, [bps], [bpt])
                for r in range(4):
                    pe_mm(po[:, r * 65:(r + 1) * 65], pt_[:, r * 128:(r + 1) * 128],
                          Vw[:, rc * 130 + g * 65:rc * 130 + g * 65 + 65], w == 0 and r == 0, w == 4, [bpt, b_Vw], [b_po])
                if w == 4:
                    for r in range(4):
                        C.op("dve", lambda e: e.reciprocal(out=coef[:, r:r + 1], in_=po[:, r * 65 + 64:r * 65 + 65]), [b_po], [b_coef])
                    tt("dve", coef[:, 4:8], coef[:, 0:4], sg_all[:, qi * 24 + 16 + 4 * g:qi * 24 + 20 + 4 * g], ALU.mult,
                       [b_coef, b_sg], [b_coef])
                    for r in range(4):
                        h = 4 * g + r
                        ts("dve", ow_t[:, h * 64:(h + 1) * 64], po[:, r * 65:r * 65 + 64], coef[:, 4 + r:5 + r], None, ALU.mult, None,
                           [b_po, b_coef], [b_ow_t])
            C.dma(ow_d[qi], ow_t[:], reads=[b_ow_t], writes=[b_ow[qi]])

        C.barrier()
        del stage[2:]
        phB.close()

        ck('B')
        phL = contextlib.ExitStack()
        cur_stack[0] = phL
        lmask, b_lmask = sb("lmask", [128, 1024], BF16)
        C.dma(lmask[:], c_lmask, writes=[b_lmask])
        cw, b_cw = sb("cw", [128, 16])
        lvec, b_lvec = sb("lvec", [128, 48])
        for ct in range(4):
            C.dma(cw[:, ct * 4:(ct + 1) * 4], conv_w[:, ct * 128:(ct + 1) * 128].rearrange("k p -> p k"), writes=[b_cw],
                  allow_slow_non_contiguous=True)
        for j, v in enumerate((conv_b, lru_ba, lru_bx, lru_lambda)):
            C.dma(lvec[:, j * 4:(j + 1) * 4], v.rearrange("(c p) -> p c", p=128), writes=[b_lvec], allow_slow_non_contiguous=True)
        act(lvec[:, 16:20], lvec[:, 12:16], AF.Exp, [b_lvec], [b_lvec], scale=-1.0)
        act(lvec[:, 20:24], lvec[:, 16:20], AF.Ln, [b_lvec], [b_lvec], bias=1.0)
        ts("dve", lvec[:, 24:28], lvec[:, 20:24], -4.0, None, ALU.mult, None, [b_lvec], [b_lvec])
        ts("dve", lvec[:, 28:32], lvec[:, 20:24], -8.0, None, ALU.mult, None, [b_lvec], [b_lvec])
        ts("dve", lvec[:, 32:40], lvec[:, 4:12], 0.5, None, ALU.mult, None, [b_lvec], [b_lvec])
        Wab, b_Wab = sb("Wab", [128, 512], BF16)
        Wxb, b_Wxb = sb("Wxb", [128, 512], BF16)
        for (Wd, bW, src) in ((Wab, b_Wab, lru_wa), (Wxb, b_Wxb, lru_wx)):
            st, b_st = stage[stage_i[0] % len(stage)]
            stage_i[0] += 1
            C.op("dve", lambda e: e.memset(st[:, 0:512], 0.0), [], [b_st])
            for ct in range(4):
                C.dma(st[0:64, ct * 128:ct * 128 + 64], src[2 * ct], writes=[b_st])
                C.dma(st[64:128, ct * 128 + 64:ct * 128 + 128], src[2 * ct + 1], writes=[b_st])
            cp("dve", Wd[:], st[:, 0:512], [b_st], [bW])
        LT = 512
        NR = 4
        ubL = [sb(f"ubL{j}", [128, LT + 3]) for j in range(NR)]
        ucL_ = [sb(f"ucL{j}", [128, LT]) for j in range(NR)]
        ucbL_ = [sb(f"ucbL{j}", [128, LT], BF16) for j in range(NR)]
        trL_ = [sb(f"trL{j}", [128, LT]) for j in range(NR)]
        tiL_ = [sb(f"tiL{j}", [128, LT]) for j in range(NR)]
        aL_ = [sb(f"aL{j}", [128, LT]) for j in range(NR)]
        qL_ = [sb(f"qL{j}", [128, LT]) for j in range(NR)]
        hL = [sb(f"hL{j}", [128, LT]) for j in range(NR)]
        hst = [sb(f"hstate{j}", [128, 1]) for j in range(4)]
        for j in range(4):
            C.op("dve", lambda e: e.memset(hst[j][0][:], 0.0), [], [hst[j][1]])
        pg = [PS[0], PS[1], PS[2], PS[3]]
        NIT = 4 * (S // LT)

        def l_load(it):
            ct, T2 = it % 4, it // 4
            C.dma(ubL[it % NR][0][:], u_d[ct, :, T2 * LT:T2 * LT + LT + 3], reads=[b_ud[ct]], writes=[ubL[it % NR][1]])

        def l_stageA(p):
            for it in (2 * p, 2 * p + 1):
                ct, T2 = it % 4, it // 4
                rr = it % NR
                ub, b_ub = ubL[rr]
                ucL, b_ucL = ucL_[rr]
                ucbL, b_ucbL = ucbL_[rr]
                trL, b_trL = trL_[rr]
                tiL, b_tiL = tiL_[rr]
                aL, b_aL = aL_[rr]
                qL, b_qL = qL_[rr]
                pa, bpa = pg[(it % 2) * 2]
                pb, bpb = pg[(it % 2) * 2 + 1]
                ts("dve", ucL[:], ub[:, 0:LT], cw[:, ct * 4:ct * 4 + 1], lvec[:, ct:ct + 1], ALU.mult, ALU.add,
                   [b_ub, b_cw, b_lvec], [b_ucL])
                for k in range(1, 4):
                    stt(ucL[:], ub[:, k:k + LT], cw[:, ct * 4 + k:ct * 4 + k + 1], ucL[:], ALU.mult, ALU.add, [b_ub, b_cw, b_ucL], [b_ucL])
                cp("act", ucbL[:], ucL[:], [b_ucL], [b_ucbL])
                pe_mm(pa[:], Wab[:, ct * 128:(ct + 1) * 128], ucbL[:], True, True, [b_Wab, b_ucbL], [bpa])
                pe_mm(pb[:], Wxb[:, ct * 128:(ct + 1) * 128], ucbL[:], True, True, [b_Wxb, b_ucbL], [bpb])
                act(trL[:], pa[:], AF.Tanh, [bpa, b_lvec], [b_trL], scale=0.5, bias=lvec[:, 32 + ct:33 + ct])
                act(tiL[:], pb[:], AF.Tanh, [bpb, b_lvec], [b_tiL], scale=0.5, bias=lvec[:, 36 + ct:37 + ct])
                act(aL[:], trL[:], AF.Exp, [b_trL, b_lvec], [b_aL], scale=lvec[:, 24 + ct:25 + ct], bias=lvec[:, 24 + ct:25 + ct])
                act(qL[:], trL[:], AF.Exp, [b_trL, b_lvec], [b_qL], scale=lvec[:, 28 + ct:29 + ct], bias=lvec[:, 28 + ct:29 + ct])
            for it in (2 * p, 2 * p + 1):
                qL, b_qL = qL_[it % NR]
                act(qL[:], qL[:], AF.Sqrt, [b_qL], [b_qL], scale=-1.0, bias=1.0)

        def l_stageB(p):
            for it in (2 * p, 2 * p + 1):
                ct, T2 = it % 4, it // 4
                rr = it % NR
                ho, b_ho = hL[rr]
                ucL, b_ucL = ucL_[rr]
                tiL, b_tiL = tiL_[rr]
                aL, b_aL = aL_[rr]
                qL, b_qL = qL_[rr]
                stt(tiL[:], tiL[:], 1.0, ucL[:], ALU.add, ALU.mult, [b_tiL, b_ucL], [b_tiL])
                stt(qL[:], qL[:], 0.5, tiL[:], ALU.mult, ALU.mult, [b_qL, b_tiL], [b_qL])
                if T2 < 2:
                    tt("dve", qL[:], qL[:], lmask[:, T2 * 512:(T2 + 1) * 512], ALU.mult, [b_qL, b_lmask], [b_qL])
                C.op("dve", lambda e: e.tensor_tensor_scan(out=ho[:], data0=aL[:], data1=qL[:], initial=hst[ct][0][:, 0:1],
                                                            op0=ALU.mult, op1=ALU.add), [b_aL, b_qL, hst[ct][1]], [b_ho])
                cp("act", hst[ct][0][:, 0:1], ho[:, LT - 1:LT], [b_ho], [hst[ct][1]])
                if T2 % 2 == 1:
                    qi_ = (T2 - 1) // 2
                    C.dma(hl_d[qi_, ct], ho[:, 384:512], reads=[b_ho], writes=[b_hl[qi_]])

        NP = NIT // 2
        for it in range(4):
            l_load(it)
        l_stageA(0)
        for p in range(NP):
            if p + 1 < NP:
                l_stageA(p + 1)
            if p + 2 < NP:
                l_load(2 * (p + 2))
                l_load(2 * (p + 2) + 1)
            l_stageB(p)
        C.barrier()
        phL.close()
        ck('L')
        phB2 = contextlib.ExitStack()
        cur_stack[0] = phB2
        for j in range(6):
            stage.append(sb(f"stageM{j}", [128, 512]))
        W1b, b_W1b = sb("W1b", [128, 32 * 256], BF16)
        for p0 in range(0, 32, 2):
            st, b_st = stage[stage_i[0] % len(stage)]
            stage_i[0] += 1
            for kv in range(2):
                for pp in range(2):
                    C.dma(st[kv * 64:(kv + 1) * 64, pp * 256:(pp + 1) * 256], cmp_w1[kv, (p0 + pp) * 64:(p0 + pp + 1) * 64, :],
                          writes=[b_st])
            cp("act" if (p0 // 2) % 2 else "dve", W1b[:, p0 * 256:(p0 + 2) * 256], st[:, 0:512], [b_st], [b_W1b])
        W2kp, b_W2kp = sb("W2kp", [128, 384], BF16)
        W2v, b_W2v = sb("W2v", [128, 128], BF16)
        st, b_st = stage[stage_i[0] % len(stage)]
        stage_i[0] += 1
        C.op("dve", lambda e: e.memset(st[:, 0:384], 0.0), [], [b_st])
        for hh in range(2):
            C.dma(st[:, hh * 192 + 64:hh * 192 + 128], cmp_w2[0, hh * 128:(hh + 1) * 128, :], writes=[b_st])
        cp("dve", W2kp[:], st[:, 0:384], [b_st], [b_W2kp])
        st, b_st = stage[stage_i[0] % len(stage)]
        stage_i[0] += 1
        for hh in range(2):
            C.dma(st[:, hh * 64:(hh + 1) * 64], cmp_w2[1, hh * 128:(hh + 1) * 128, :], writes=[b_st])
        cp("dve", W2v[:], st[:, 0:128], [b_st], [b_W2v])
        peT, b_peT = sb("peT", [128, 32], BF16)
        st, b_st = stage[stage_i[0] % len(stage)]
        stage_i[0] += 1
        for kv in range(2):
            C.dma(st[kv * 64:(kv + 1) * 64, 0:32], cmp_pe[kv].rearrange("p d -> d p"), writes=[b_st], allow_slow_non_contiguous=True)
        cp("dve", peT[:], st[:, 0:32], [b_st], [b_peT])
        b1t, b_b1t = sb("b1t", [128, 4])
        for kv in range(2):
            for hh in range(2):
                C.dma(b1t[:, kv * 2 + hh:kv * 2 + hh + 1], cmp_b1[kv, hh * 128:(hh + 1) * 128].rearrange("(h o) -> h o", o=1),
                      writes=[b_b1t], allow_slow_non_contiguous=True)
        cvec, b_cvec = sb("cvec", [128, 4])
        ps_a, b_ps_a = PS[0]
        ps_b, b_ps_b = PS[1]
        ps_n, b_ps_n = PS[2]
        ps_v, b_ps_v = PS[3]
        ps_o, b_ps_o = PS[6]
        for kv in range(2):
            lo, hi = kv * 64, (kv + 1) * 64
            for hh in range(2):
                j = kv * 2 + hh
                pcv, bpcv = (ps_n, b_ps_n) if kv == 0 else (ps_v, b_ps_v)
                for p in range(32):
                    pe_mm(pcv[:, hh:hh + 1], W1b[lo:hi, p * 256 + hh * 128:p * 256 + hh * 128 + 128], peT[lo:hi, p:p + 1], p == 0, p == 31,
                          [b_W1b, b_peT], [bpcv])
        tt("dve", cvec[:, 0:2], ps_n[:, 0:2], b1t[:, 0:2], ALU.add, [b_ps_n, b_b1t], [b_cvec])
        tt("dve", cvec[:, 2:4], ps_v[:, 0:2], b1t[:, 2:4], ALU.add, [b_ps_v, b_b1t], [b_cvec])
        cvalid, b_cvalid = sb("cvalid", [128, 8])
        C.dma(cvalid[:], c_cvalid, writes=[b_cvalid])
        cin = [sb(f"cin{g}", [128, 2064], BF16) for g in range(2)]
        hact, b_hact = sb("hact", [128, 1024], BF16)
        n2_sq, _ = sb("n2_sq", [128, 128], BF16)
        n2_rt, _ = sb("n2_rt", [128, 128])
        b_n2 = Buf()
        for Cc in range(8):
            for g in range(2):
                C.dma(cin[g][0][:], kcmp_d[g, :, Cc * 2048:Cc * 2048 + 2064], reads=[b_kcmp[g]], writes=[cin[g][1]])
            for kv in range(2):
                lo, hi = kv * 64, (kv + 1) * 64
                psh, bph = (ps_a, b_ps_a) if kv == 0 else (ps_b, b_ps_b)
                for g in range(2):
                    for hh in range(2):
                        o = (g * 2 + hh) * 128
                        for p in range(32):
                            pe_mm(psh[:, o:o + 128], W1b[lo:hi, p * 256 + hh * 128:p * 256 + hh * 128 + 128],
                                  cin[g][0][lo:hi, p:p + 2033:16], p == 0, p == 31, [b_W1b, cin[g][1]], [bph])
                for g in range(2):
                    for hh in range(2):
                        o = (g * 2 + hh) * 128
                        ho = ((kv * 2 + g) * 2 + hh) * 128
                        act(hact[:, ho:ho + 128], psh[:, o:o + 128], AF.Silu, [bph, b_cvec], [b_hact],
                            bias=cvec[:, kv * 2 + hh:kv * 2 + hh + 1])
            n_ = 0
            for g in range(2):
                for hh in range(2):
                    ho = ((0 * 2 + g) * 2 + hh) * 128
                    c0 = hh * 192 + (64 if g == 0 else 0)
                    pe_mm(ps_v[:, 0:128], W2kp[:, c0:c0 + 128], hact[:, ho:ho + 128], n_ == 0, n_ == 3, [b_W2kp, b_hact], [b_ps_v])
                    n_ += 1
            head_norm([(ps_v[:, 0:128], 0, 128)], kgain[:, 0:1], kcT[:, Cc * 128:(Cc + 1) * 128], 128, 1.0 / 64, EPS, ps_n, b_ps_n,
                      (n2_sq, n2_rt), b_n2, [b_ps_v], [b_kcT])
            for g in range(2):
                for hh in range(2):
                    ho = ((1 * 2 + g) * 2 + hh) * 128
                    pe_mm(ps_o[:, g * 64:(g + 1) * 64], hact[:, ho:ho + 128], W2v[:, hh * 64:(hh + 1) * 64], hh == 0, hh == 1,
                          [b_hact, b_W2v], [b_ps_o])
            for g in range(2):
                o = Cc * 130 + g * 65
                ts("dve", vcx[:, o:o + 64], ps_o[:, g * 64:(g + 1) * 64], cvalid[:, Cc:Cc + 1], None, ALU.mult, None,
                   [b_ps_o, b_cvalid], [b_vcx])
                cp("dve", vcx[:, o + 64:o + 65], cvalid[:, Cc:Cc + 1], [b_cvalid], [b_vcx])
        if os.environ.get("KDEBUG", ""):
            dbg_kc = nc.dram_tensor("dbg_kc", [128, 1024], BF16, kind="ExternalOutput").ap()
            dbg_vc = nc.dram_tensor("dbg_vc", [128, 1040], BF16, kind="ExternalOutput").ap()
            C.dma(dbg_kc, kcT[:], reads=[b_kcT], writes=[Buf()])
            C.dma(dbg_vc, vcx[:], reads=[b_vcx], writes=[Buf()])
        C.barrier()
        del stage[2:]
        phB2.close()

        ck('Bm')
        phC = contextlib.ExitStack()
        cur_stack[0] = phC
        GCp, b_GCp = sb("GCp", [128, 1024])
        BigSh, b_BigSh = sb("BigSh", [128, 384])
        KX1, b_KX1 = sb("KX1", [128, S], BF16)
        Pool_, b_Pool = sb("Pool_", [128, 2048], BF16)
        wrel, b_wrel = sb("wrel", [128, 512])
        wcore, b_wcore = sb("wcore", [128, 256])
        C.dma(GCp[:], c_gc, writes=[b_GCp])
        C.dma(BigSh[:], c_bigsh, writes=[b_BigSh])
        for j in range(4):
            cp(("act", "dve", "act", "dve")[j], KX1[64:128, j * 4096:(j + 1) * 4096], Ks[64:128, j * 4096:(j + 1) * 4096], [b_Ks], [b_KX1])
        for j in range(4):
            C.dma(KX1[0:64, j * 4096:(j + 1) * 4096], c_ind, writes=[b_KX1])
            C.dma(Ks[64:128, j * 4096:(j + 1) * 4096], c_ind, writes=[b_Ks])
        KX = [Ks, KX1]
        b_KX = [b_Ks, b_KX1]
        C.dma(Pool_[:], c_pool, writes=[b_Pool])
        C.dma(wrel[:], c_wrel, writes=[b_wrel])
        C.dma(wcore[:], c_wcore, writes=[b_wcore])
        for h in range(8):
            ts("dve", GCp[:, h * 128:(h + 1) * 128], GCp[:, h * 128:(h + 1) * 128], b31[:, h:h + 1], None, ALU.subtract, None,
               [b_GCp, b_b31], [b_GCp])
        pcs, b_pcs = sb("pcs", [128, 8 * 512], BF16)
        pring = [sb(f"pring{j}", [128, 512], BF16) for j in range(3)]
        NegMp, b_NegM = sb("NegMp", [128, 320], BF16)
        C.op("dve", lambda e: e.memset(NegMp[:, 0:64], 0.0), [], [b_NegM])
        NegM = NegMp[:, 64:320]
        mr1, b_mr1 = sb("mr1", [128, 128], BF16)
        qx = [sb(f"qx{g}", [128, 4 * 512], BF16) for g in range(2)]
        imp, b_imp = sb("imp", [128, 256])
        score, b_score = sb("score", [128, 256])
        sc2, b_sc2 = sb("sc2", [128, 256])
        mf, b_mf = sb("mf", [128, 256])
        m8, b_m8 = sb("m8", [128, 16])
        thr, b_thr = sb("thr", [128, 1])
        oacc_l = [sb(f"oacc{j}", [128, 512]) for j in range(2)]
        coefc, b_coefc = sb("coefc", [128, 8])
        rden, b_rden = sb("rden", [128, 4])
        coef2, b_coef2 = sb("coef2", [128, 8])
        rden2, b_rden2 = sb("rden2", [128, 4])
        ps_c = [PS[3], PS[4]]
        ps_s3 = [PS[0], PS[1], PS[6]]
        ps_oc, b_ps_oc = PS[2]
        psu = [PS[3], PS[4]]
        ps_os, b_ps_os = PS[5]

        def geom(n):
            i, g = n // 2, n % 2
            M = 8 * i + 7
            return i, g, M, i // 2 + 1, g * 64, (g + 1) * 64

        def prep_load(n):
            i, g, M, nC, lo, hi = geom(n)
            if g == 0:
                oacc, b_oacc = oacc_l[i % 2]
                C.dma(oacc[:], ow_d[i], reads=[b_ow[i]], writes=[b_oacc])
            qn = q_all[:, i * 512:(i + 1) * 512]
            qxt, b_qxt = qx[g]
            for s_ in range(M // 32 + 1):
                cp("dve" if s_ % 2 else "act", qxt[lo:hi, s_ * 512:(s_ + 1) * 512], qn[lo:hi, :], [b_q], [b_qxt])

        def prep_qk(n, Cc):
            i, g, M, nC, lo, hi = geom(n)
            qn = q_all[:, i * 512:(i + 1) * 512]
            pst, bps = ps_c[Cc % 2]
            delta = 128 * Cc - 8 * M + 64
            near = delta > -128
            pe_mm(pst[:], kcT[lo:hi, Cc * 128:(Cc + 1) * 128], qn[lo:hi, :], True, not near, [b_kcT, b_q], [bps])
            if near:
                pe_mm(pst[:], BigSh[:, 128 + delta:256 + delta], GCp[:, g * 512:(g + 1) * 512], False, True,
                      [b_BigSh, b_GCp], [bps])
            act(pcs[:, Cc * 512:(Cc + 1) * 512], pst[:], AF.Exp, [bps], [b_pcs])

        def prep_pv(n, Cc):
            i, g, M, nC, lo, hi = geom(n)
            for r in range(4):
                pe_mm(ps_oc[:, r * 65:(r + 1) * 65], pcs[:, Cc * 512 + r * 128:Cc * 512 + (r + 1) * 128],
                      vcx[:, Cc * 130 + g * 65:Cc * 130 + g * 65 + 65], Cc == 0 and r == 0, Cc == nC - 1, [b_pcs, b_vcx], [b_ps_oc])

        def prep_imp(n):
            i, g, M, nC, lo, hi = geom(n)
            oacc, b_oacc = oacc_l[i % 2]
            for r in range(4):
                pu, bpu = psu[r // 2]
                for Cc in range(nC):
                    pe_mm(pu[:, (r % 2) * 256:(r % 2) * 256 + 256], pcs[:, Cc * 512 + r * 128:Cc * 512 + (r + 1) * 128],
                          Pool_[:, Cc * 256:(Cc + 1) * 256], Cc == 0, Cc == nC - 1, [b_pcs, b_Pool], [bpu])
            for r in range(4):
                ts("dve", rden[:, r:r + 1], ps_oc[:, r * 65 + 64:r * 65 + 65], 1e-30, None, ALU.add, None, [b_ps_oc], [b_rden])
                C.op("dve", lambda e: e.reciprocal(out=rden[:, r:r + 1], in_=rden[:, r:r + 1]), [b_rden], [b_rden])
            ts("dve", imp[:], psu[0][0][:, 0:256], rden[:, 0:1], None, ALU.mult, None, [psu[0][1], b_rden], [b_imp])
            for r in range(1, 4):
                pu, bpu = psu[r // 2]
                stt(imp[:], pu[:, (r % 2) * 256:(r % 2) * 256 + 256], rden[:, r:r + 1], imp[:], ALU.mult, ALU.add,
                    [bpu, b_rden, b_imp], [b_imp])
            tt("dve", coefc[:, 0:4], rden[:], sg_all[:, i * 24 + 4 * g:i * 24 + 4 * g + 4], ALU.mult, [b_rden, b_sg], [b_coefc])
            for r in range(4):
                h = 4 * g + r
                stt(oacc[:, h * 64:(h + 1) * 64], ps_oc[:, r * 65:r * 65 + 64], coefc[:, r:r + 1], oacc[:, h * 64:(h + 1) * 64],
                    ALU.mult, ALU.add, [b_ps_oc, b_coefc, b_oacc], [b_oacc])
            tt("dve", score[:], imp[:], wrel[:, 256 - 2 * M:512 - 2 * M], ALU.add, [b_imp, b_wrel], [b_score])
            tt("dve", score[:], score[:], wcore[:], ALU.add, [b_score, b_wcore], [b_score])
            C.op("dve", lambda e: e.max(out=m8[:, 0:8], in_=score[:]), [b_score], [b_m8])
            C.op("dve", lambda e: e.match_replace(out=sc2[:], in_to_replace=m8[:, 0:8], in_values=score[:], imm_value=-1e30),
                 [b_score, b_m8], [b_sc2])
            C.op("dve", lambda e: e.max(out=m8[:, 8:16], in_=sc2[:]), [b_sc2], [b_m8])
            ts("dve", thr[:], m8[:, 15:16], -0.5, None, ALU.max, None, [b_m8], [b_thr])
            ts("dve", mf[:], score[:], thr[:, 0:1], None, ALU.is_ge, None, [b_score, b_thr], [b_mf])
            ts("dve", NegM, mf[:], -1.0, 30000.0, ALU.add, ALU.mult, [b_mf], [b_NegM])

        b_tpm = [Buf(), Buf()]

        def prep_mask(n):
            i, g, M, nC, lo, hi = geom(n)
            qxt, b_qxt = qx[g]
            mlo, mhi = (64, 128) if g == 0 else (0, 64)
            for s_ in range(M // 32 + 1):
                c0 = 0
                if g == 1:
                    C.op("pe", lambda e: e.transpose(out=ps_tp[0:64, c0:c0 + 128], in_=NegMp[:, 64 + 64 * s_:128 + 64 * s_],
                                                     identity=identb[:]), [b_NegM, b_identb], [b_ps_tp])
                else:
                    C.op("pe", lambda e: e.transpose(out=ps_tp[:, c0:c0 + 128], in_=NegMp[:, 64 * s_:64 * s_ + 128],
                                                     identity=identb[:]), [b_NegM, b_identb], [b_ps_tp])
                for r in range(4):
                    cp("act" if r < 2 else "dve", qxt[mlo:mhi, s_ * 512 + r * 128:s_ * 512 + (r + 1) * 128],
                       ps_tp[mlo:mhi, c0:c0 + 128], [b_ps_tp], [b_qxt])

        def slc_qk(n, kc):
            i, g, M, nC, lo, hi = geom(n)
            s_ = kc // 32
            pst, bps = ps_s3[kc % 3]
            near = kc >= M - 1
            pe_mm(pst[:], KX[g][:, kc * 128:(kc + 1) * 128], qx[g][0][:, s_ * 512:(s_ + 1) * 512], True, not near,
                  [b_KX[g], qx[g][1]], [bps])
            if kc == M:
                pe_mm(pst[:], identf[:], TDp[:, g * 512:(g + 1) * 512], False, True, [b_identf, b_TDp], [bps])
            elif kc == M - 1:
                pe_mm(pst[:], identf[:], TPp[:, g * 512:(g + 1) * 512], False, True, [b_identf, b_TPp], [bps])

        def slc_final(n):
            i, g, M, nC, lo, hi = geom(n)
            oacc, b_oacc = oacc_l[i % 2]
            for r in range(4):
                C.op("dve", lambda e: e.reciprocal(out=rden2[:, r:r + 1], in_=ps_os[:, r * 65 + 64:r * 65 + 65]), [b_ps_os], [b_rden2])
            tt("dve", coef2[:, 4:8], rden2[:], sg_all[:, i * 24 + 8 + 4 * g:i * 24 + 12 + 4 * g], ALU.mult, [b_rden2, b_sg], [b_coef2])
            for r in range(4):
                h = 4 * g + r
                stt(oacc[:, h * 64:(h + 1) * 64], ps_os[:, r * 65:r * 65 + 64], coef2[:, 4 + r:5 + r], oacc[:, h * 64:(h + 1) * 64],
                    ALU.mult, ALU.add, [b_ps_os, b_coef2, b_oacc], [b_oacc])
            if g == 1:
                C.dma(oa_d[i], oacc[:], reads=[b_oacc], writes=[b_oa[i]])

        def prep_all(n):
            i, g, M, nC, lo, hi = geom(n)
            prep_load(n)
            for Cc in range(nC):
                prep_qk(n, Cc)
                prep_pv(n, Cc)
            prep_imp(n)
            prep_mask(n)

        prep_all(0)
        for n in range(32):
            i, g, M, nC, lo, hi = geom(n)
            if n == 2: ck('C1')
            nxt = n + 1 if n + 1 < 32 else None
            nCn = geom(nxt)[3] if nxt is not None else 0
            slc_qk(n, 0)
            if M >= 1:
                slc_qk(n, 1)
            if nxt is not None:
                prep_load(nxt)
            for kc in range(M + 1):
                if kc + 2 <= M:
                    slc_qk(n, kc + 2)
                if nxt is not None:
                    if kc < nCn:
                        prep_qk(nxt, kc)
                    if 1 <= kc <= nCn:
                        prep_pv(nxt, kc - 1)
                    if kc == nCn + 1:
                        prep_imp(nxt)
                pst, bps = ps_s3[kc % 3]
                pt_, bpt = pring[kc % 3]
                act(pt_[:], pst[:], AF.Exp, [bps], [bpt])
                for r in range(4):
                    pe_mm(ps_os[:, r * 65:(r + 1) * 65], pt_[:, r * 128:(r + 1) * 128],
                          Vs[:, kc * 130 + g * 65:kc * 130 + g * 65 + 65], kc == 0 and r == 0, kc == M, [bpt, b_Vs], [b_ps_os])
            if nxt is not None:
                if nCn + 1 > M:
                    prep_imp(nxt)
                prep_mask(nxt)
            slc_final(n)
        C.barrier()
        phC.close()
        sc1.close()

        ck('C')
        phD = contextlib.ExitStack()
        cur_stack[0] = phD
        for j in range(6):
            stage.append(sb(f"stageD{j}", [128, 512]))
        Wg, b_Wg = sb("Wg", [128, 8 * 512], BF16)
        Wm, b_Wm = sb("Wm", [128, 8 * 2048], BF16)
        Wpa, b_Wpa = sb("Wpa", [128, 4 * 1024], BF16)
        Wpb, b_Wpb = sb("Wpb", [128, 4 * 1024], BF16)
        Wo, b_Wo = sb("Wo", [128, 8 * 1024], BF16)
        Wgl, b_Wgl = sb("Wgl", [128, 8 * 512], BF16)
        hraw, b_hraw = sb("hraw", [128, 512])
        for k in range(8):
            load_w_bf16(Wg[:, k * 512:(k + 1) * 512], w_in[k * 128:(k + 1) * 128, 1280:1792], 512, b_Wg)
            load_w_bf16(Wgl[:, k * 512:(k + 1) * 512], w_in[k * 128:(k + 1) * 128, 2328:2840], 512, b_Wgl)
            for pc in range(4):
                load_w_bf16(Wm[:, k * 2048 + pc * 512:k * 2048 + (pc + 1) * 512],
                            w_in[k * 128:(k + 1) * 128, 2840 + pc * 512:2840 + (pc + 1) * 512], 512, b_Wm)
            for pc in range(2):
                load_w_bf16(Wo[:, k * 1024 + pc * 512:k * 1024 + (pc + 1) * 512], w_out[k * 128:(k + 1) * 128, pc * 512:(pc + 1) * 512],
                            512, b_Wo)
        for k in range(4):
            for pc in range(2):
                load_w_bf16(Wpa[:, k * 1024 + pc * 512:k * 1024 + (pc + 1) * 512],
                            w_proj_a[k * 128:(k + 1) * 128, pc * 512:(pc + 1) * 512], 512, b_Wpa)
                load_w_bf16(Wpb[:, k * 1024 + pc * 512:k * 1024 + (pc + 1) * 512],
                            w_proj_b[k * 128:(k + 1) * 128, pc * 512:(pc + 1) * 512], 512, b_Wpb)
        xd = [sb(f"xd{j}", [128, 1024]) for j in range(4)]
        oa_t, b_oa_t = sb("oa_t", [128, 512])
        sgn, b_sgn = sb("sgn", [128, 512])
        ya, b_ya = sb("ya", [128, 512], BF16)
        yaT, b_yaT = sb("yaT", [128, 4 * 512], BF16)
        hlT, b_hlT = sb("hlT", [128, 4 * 512], BF16)
        mT, b_mT = sb("mT", [128, 8 * 512], BF16)
        sga, b_sga = sb("sga", [128, 512])
        sgb, b_sgb = sb("sgb", [128, 512])
        m1, b_m1 = sb("m1", [128, 512])
        ot = [sb(f"ot{j}", [128, 512]) for j in range(2)]
        pA, b_pA = PS[1]
        pGA, b_pGA = PS[2]
        pB, b_pB = PS[3]
        pGB, b_pGB = PS[4]
        for grp in range(4):
            for j in range(4):
                i = grp * 4 + j
                slot = 8 * i + 7
                C.dma(xd[j][0][:], x_d[slot * 128:(slot + 1) * 128, :], writes=[xd[j][1]])
                norm_chunk_a(xd[j][0], xd[j][1], j)
            for j in range(4):
                norm_chunk_b(xd[j][0], xd[j][1], j * 128, j, None, None, par=0)
            for j in range(4):
                i = grp * 4 + j
                for k in range(8):
                    pe_mm(PS[0][0][:], hT[:, k * 512 + j * 128:k * 512 + (j + 1) * 128], Wg[:, k * 512:(k + 1) * 512], k == 0, k == 7,
                          [b_hT, b_Wg], [PS[0][1]])
                act(sgn[:], PS[0][0][:], AF.Silu, [PS[0][1]], [b_sgn])
                C.dma(oa_t[:], oa_d[i], reads=[b_oa[i]], writes=[b_oa_t])
                tt("dve", ya[:], sgn[:], oa_t[:], ALU.mult, [b_sgn, b_oa_t], [b_ya])
                for kc in range(4):
                    C.op("pe", lambda e: e.transpose(out=ps_tp[:, kc * 128:(kc + 1) * 128], in_=ya[:, kc * 128:(kc + 1) * 128],
                                                     identity=identb[:]), [b_ya, b_identb], [b_ps_tp])
                for kc in range(4):
                    cp("act" if kc % 2 else "dve", yaT[:, kc * 512 + j * 128:kc * 512 + (j + 1) * 128], ps_tp[:, kc * 128:(kc + 1) * 128],
                       [b_ps_tp], [b_yaT])
            for ct in range(4):
                for k in range(8):
                    pe_mm(PS[0][0][:], Wgl[:, k * 512 + ct * 128:k * 512 + (ct + 1) * 128], hT[:, k * 512:(k + 1) * 512], k == 0, k == 7,
                          [b_Wgl, b_hT], [PS[0][1]])
                act(sgn[:], PS[0][0][:], AF.Silu, [PS[0][1]], [b_sgn])
                for j in range(4):
                    C.dma(hraw[:, j * 128:(j + 1) * 128], hl_d[grp * 4 + j, ct], reads=[b_hl[grp * 4 + j]], writes=[b_hraw])
                tt("dve", hlT[:, ct * 512:(ct + 1) * 512], sgn[:], hraw[:], ALU.mult, [b_sgn, b_hraw], [b_hlT])
            for f in range(8):
                for k in range(8):
                    pe_mm(pGA[:], Wm[:, k * 2048 + f * 128:k * 2048 + (f + 1) * 128], hT[:, k * 512:(k + 1) * 512], k == 0, k == 7,
                          [b_Wm, b_hT], [b_pGA])
                for k in range(8):
                    pe_mm(pGB[:], Wm[:, k * 2048 + 1024 + f * 128:k * 2048 + 1024 + (f + 1) * 128], hT[:, k * 512:(k + 1) * 512],
                          k == 0, k == 7, [b_Wm, b_hT], [b_pGB])
                for kc in range(4):
                    pe_mm(pA[:], Wpa[:, kc * 1024 + f * 128:kc * 1024 + (f + 1) * 128], yaT[:, kc * 512:(kc + 1) * 512], kc == 0, kc == 3,
                          [b_Wpa, b_yaT], [b_pA])
                for kc in range(4):
                    pe_mm(pB[:], Wpb[:, kc * 1024 + f * 128:kc * 1024 + (f + 1) * 128], hlT[:, kc * 512:(kc + 1) * 512], kc == 0, kc == 3,
                          [b_Wpb, b_hlT], [b_pB])
                act(sga[:], pGA[:], AF.Sigmoid, [b_pGA], [b_sga])
                act(sgb[:], pGB[:], AF.Sigmoid, [b_pGB], [b_sgb])
                tt("dve", m1[:], sga[:], pA[:], ALU.mult, [b_sga, b_pA], [b_m1])
                tt("dve", sgb[:], sgb[:], pB[:], ALU.mult, [b_sgb, b_pB], [b_sgb])
                tt("dve", mT[:, f * 512:(f + 1) * 512], m1[:], sgb[:], ALU.add, [b_m1, b_sgb], [b_mT])
            for j in range(4):
                i = grp * 4 + j
                for half in range(2):
                    pO, b_pO = PS[5 + half]
                    for f in range(8):
                        pe_mm(pO[:], mT[:, f * 512 + j * 128:f * 512 + (j + 1) * 128], Wo[:, f * 1024 + half * 512:f * 1024 + (half + 1) * 512],
                              f == 0, f == 7, [b_mT, b_Wo], [b_pO])
                    tt("dve", ot[half][0][:], pO[:], xd[j][0][:, half * 512:(half + 1) * 512], ALU.add, [b_pO, xd[j][1]], [ot[half][1]])
                    C.dma(out_d[i, :, half * 512:(half + 1) * 512], ot[half][0][:], reads=[ot[half][1]], writes=[b_out[i]])
        C.barrier()
        phD.close()
        phB_done = True
    except _Stop:
        C.barrier()
    return nc, C, None


def _bf(a):
    return np.asarray(a, np.float32).astype(ml_dtypes.bfloat16)


def host_constants(rel_bias):
    rb = np.asarray(rel_bias, np.float32)
    cst = {}
    cst["c_identb"] = _bf(np.eye(128))
    cst["c_identf"] = np.eye(128, dtype=np.float32)
    p = np.arange(128)
    cst["c_onesbd"] = _bf((p[:, None] // 64) == (p[None, :] // 64))
    k = np.arange(128)[:, None]
    i = np.arange(128)[None, :]
    d = i - k
    td = np.where(d[:, None, :] >= 0, rb[t5_bucket_np(d)].transpose(0, 2, 1), np.float32(NEG))
    cst["c_td"] = np.ascontiguousarray(td.reshape(128, 1024), np.float32)
    d = i - k + 128
    tp = rb[t5_bucket_np(d)].transpose(0, 2, 1)
    cst["c_tp"] = np.ascontiguousarray(tp.reshape(128, 1024), np.float32)
    tw0 = np.where(i < k, np.float32(0), np.float32(NEG)).astype(np.float32)
    cst["c_tw0"] = np.ascontiguousarray(np.tile(tw0, (1, 4)), np.float32)
    r = np.arange(128)[:, None]
    d = i - 16 * (r - 64) - 31
    gc = np.where(d[:, None, :] >= 0, rb[t5_bucket_np(d)].transpose(0, 2, 1), np.float32(NEG))
    gc[127] = NEG
    cst["c_gc"] = np.ascontiguousarray(gc.reshape(128, 1024), np.float32)
    cst["c_b31"] = np.ascontiguousarray(np.tile(rb[31][None, :], (128, 1)), np.float32)
    xx = np.arange(384)[None, :] - 128
    bs = (r == xx).astype(np.float32)
    bs[127] = (xx[0] >= 127).astype(np.float32)
    cst["c_bigsh"] = bs
    pos = np.arange(4096)[None, :]
    cst["c_ind"] = _bf((pos // 64) == np.arange(64)[:, None])
    pool = np.zeros((128, 8, 256), np.float32)
    for Cc in range(8):
        cb = 128 * Cc + np.arange(128)
        for j in range(256):
            pool[:, Cc, j] = (cb >= 4 * j - 1) & (cb <= 4 * j + 3)
    cst["c_pool"] = _bf(pool.reshape(128, 2048))
    ii = np.arange(128)[:, None]
    hi = (ii >= 64).astype(np.int64)
    xr = np.arange(512)[None, :] - 256
    wrel = np.where(xr > hi, np.float32(-1.0), np.where((xr == hi) | (xr == hi - 1), np.float32(1e9), np.float32(0)))
    cst["c_wrel"] = wrel.astype(np.float32)
    return cst


def core_constants(c):
    sh = 7 - c
    d = {}
    chv = (np.arange(128) >= sh).astype(np.float32)
    d["c_valid"] = np.ascontiguousarray(np.tile(chv[None, :], (128, 1)), np.float32)
    cb = np.arange(128)[:, None] + 128 * np.arange(8)[None, :]
    d["c_cvalid"] = (cb >= 8 * sh).astype(np.float32)
    posn = np.arange(1024)
    d["c_lmask"] = _bf(np.tile((posn >= 128 * sh)[None, :], (128, 1)))
    j = np.arange(256)
    wc = np.where(j < 2 * sh, np.float32(-3e9), np.where(j == 2 * sh, np.float32(1e9), np.float32(0)))
    d["c_wcore"] = np.ascontiguousarray(np.tile(wc[None, :], (128, 1)), np.float32)
    return d


_PROG = {}


def kernel(x, norm_gain, w_in, q_norm_gain, k_norm_gain, cmp_pe, cmp_w1, cmp_b1, cmp_w2, rel_bias,
           conv_w, conv_b, lru_wa, lru_ba, lru_wx, lru_bx, lru_lambda, w_proj_a, w_proj_b, w_out):
    if "nc" not in _PROG:
        _PROG["nc"] = build_program()[0]
    nc = _PROG["nc"]
    f = lambda a: np.ascontiguousarray(np.asarray(a), np.float32)
    shared = dict(norm_gain=f(norm_gain), w_in=f(w_in), q_norm_gain=f(q_norm_gain), k_norm_gain=f(k_norm_gain),
                  cmp_pe=f(cmp_pe), cmp_w1=f(cmp_w1), cmp_b1=f(cmp_b1), cmp_w2=f(cmp_w2), conv_w=f(conv_w),
                  conv_b=f(conv_b), lru_wa=f(lru_wa), lru_ba=f(lru_ba), lru_wx=f(lru_wx), lru_bx=f(lru_bx),
                  lru_lambda=f(lru_lambda), w_proj_a=f(w_proj_a), w_proj_b=f(w_proj_b), w_out=f(w_out))
    shared.update(host_constants(rel_bias))
    xs = f(x)[0]
    in_maps = []
    for c in range(8):
        sh = 7 - c
        xc = np.zeros((S, 1024), np.float32)
        xc[sh * 128:] = xs[:S - sh * 128]
        m = dict(shared)
        m["x"] = xc
        m.update(core_constants(c))
        in_maps.append(m)
    res = run_bass_kernel_spmd(nc, in_maps, core_ids=list(range(8)))
    out = np.zeros((1, S, 1024), np.float32)
    for c in range(8):
        o = np.asarray(res.results[c]["out"])
        for i in range(16):
            m_ = 8 * i + c
            out[0, m_ * 128:(m_ + 1) * 128] = o[i]
    _PROG["last"] = res
    return out
```

```python
import numpy as np
import ml_dtypes
import concourse.bass as bass
import concourse.mybir as mybir
from concourse.bass_utils import run_bass_kernel_spmd

F32 = mybir.dt.float32
BF16 = mybir.dt.bfloat16
AF = mybir.ActivationFunctionType
ALU = mybir.AluOpType
NEG = -30000.0
S = 16384
NCHUNK = 128
EPS = 1e-6


class Buf:
    __slots__ = ("w", "r")

    def __init__(self):
        self.w = None
        self.r = {}


class Eng:
    def __init__(self, name, handle, sem):
        self.name = name
        self.h = handle
        self.sem = sem
        self.count = 0
        self.waited = {}


class Ctx:
    NDSEM = 24

    def __init__(self, nc):
        self.nc = nc
        self.E = {}
        for name, h in (("pe", nc.tensor), ("act", nc.scalar), ("dve", nc.vector), ("pool", nc.gpsimd), ("sp", nc.sync)):
            self.E[name] = Eng(name, h, nc.alloc_semaphore(name="sem_" + name))
        self.dsems = [nc.alloc_semaphore(name=f"dsem{i}") for i in range(self.NDSEM)]
        self.dcnt = [0] * self.NDSEM
        self.dnext = 0

    def _wait(self, E, tok):
        sem, val, owner = tok
        if owner == "pe" and E.name == "pe":
            return
        key = id(sem)
        if E.waited.get(key, 0) >= val:
            return
        E.h.wait_ge(sem, val)
        E.waited[key] = val

    def _deps(self, E, reads, writes):
        for b in reads:
            if b.w is not None:
                self._wait(E, b.w)
        for b in writes:
            if b.w is not None:
                self._wait(E, b.w)
            for t in b.r.values():
                self._wait(E, t)

    def _mark(self, tok, reads, writes):
        key = tok[2] if tok[2] != "dma" else id(tok[0])
        for b in reads:
            b.r[key] = tok
        for b in writes:
            b.w = tok
            b.r = {}

    def op(self, eng, fn, reads=(), writes=()):
        E = self.E[eng]
        self._deps(E, reads, writes)
        ins = fn(E.h)
        E.count += 1
        ins.then_inc(E.sem, 1)
        self._mark((E.sem, E.count, eng), reads, writes)

    def dma(self, out, in_, reads=(), writes=(), queue="sp", **kw):
        E = self.E[queue]
        k = self.dnext
        self.dnext = (self.dnext + 1) % self.NDSEM
        sem = self.dsems[k]
        if self.dcnt[k] > 0:
            self._wait(E, (sem, 16 * self.dcnt[k], "dma"))
        self._deps(E, reads, writes)
        ins = E.h.dma_start(out=out, in_=in_, **kw)
        self.dcnt[k] += 1
        ins.then_inc(sem, 16)
        self._mark((sem, 16 * self.dcnt[k], "dma"), reads, writes)

    def barrier(self):
        toks = [(e.sem, e.count, e.name) for e in self.E.values() if e.count > 0]
        toks += [(self.dsems[k], 16 * self.dcnt[k], "dma") for k in range(self.NDSEM) if self.dcnt[k] > 0]
        for e in self.E.values():
            for t in toks:
                if t[2] == e.name:
                    continue
                key = id(t[0])
                if e.waited.get(key, 0) >= t[1]:
                    continue
                e.h.wait_ge(t[0], t[1])
                e.waited[key] = t[1]


def t5_bucket_np(dist):
    n = np.maximum(dist, 0)
    nf = np.maximum(n, 1).astype(np.float32)
    large = 16 + (np.log(nf / np.float32(16)) / np.float32(np.log(128 / 16)) * np.float32(16)).astype(np.int32)
    large = np.minimum(large, 31)
    return np.where(n < 16, n, large)


W_SEGS = [("kc0", 512, 64), ("vc0", 640, 64), ("kc1", 576, 64), ("vc1", 704, 64), ("ks", 768, 128),
          ("kw", 1024, 128), ("vs", 896, 128), ("vw", 1152, 128), ("u", 1816, 512),
          ("q", 0, 512), ("bg", 1792, 24)]
W_OFF = {}
_o = 0
for _n, _c, _w in W_SEGS:
    W_OFF[_n] = _o
    _o += _w
NB = _o


def build_program():
    nc = bass.Bass("TRN2", target_bir_lowering=False)
    C = Ctx(nc)

    import os
    STOP = os.environ.get("KSTOP", "")

    class _Stop(Exception):
        pass

    def ck(name):
        if STOP == name:
            raise _Stop()

    try:
        def din(name, shape, dt=F32):
            return nc.dram_tensor(name, list(shape), dt, kind="ExternalInput").ap()

        x_d = din("x", [S, 1024])
        w_in = din("w_in", [1024, 4888])
        norm_gain = din("norm_gain", [1024])
        q_norm_gain = din("q_norm_gain", [64])
        k_norm_gain = din("k_norm_gain", [3, 64])
        cmp_pe = din("cmp_pe", [2, 32, 64])
        cmp_w1 = din("cmp_w1", [2, 2048, 256])
        cmp_b1 = din("cmp_b1", [2, 256])
        cmp_w2 = din("cmp_w2", [2, 256, 64])
        conv_w = din("conv_w", [4, 512])
        conv_b = din("conv_b", [512])
        lru_wa = din("lru_wa", [8, 64, 64])
        lru_ba = din("lru_ba", [512])
        lru_wx = din("lru_wx", [8, 64, 64])
        lru_bx = din("lru_bx", [512])
        lru_lambda = din("lru_lambda", [512])
        w_proj_a = din("w_proj_a", [512, 1024])
        w_proj_b = din("w_proj_b", [512, 1024])
        w_out = din("w_out", [1024, 1024])
        c_identb = din("c_identb", [128, 128], BF16)
        c_identf = din("c_identf", [128, 128])
        c_onesbd = din("c_onesbd", [128, 128], BF16)
        c_td = din("c_td", [128, 1024])
        c_tp = din("c_tp", [128, 1024])
        c_tw0 = din("c_tw0", [128, 512])
        c_gc = din("c_gc", [128, 1024])
        c_b31 = din("c_b31", [128, 8])
        c_bigsh = din("c_bigsh", [128, 384])
        c_ind = din("c_ind", [64, 4096], BF16)
        c_pool = din("c_pool", [128, 2048], BF16)
        c_wrel = din("c_wrel", [128, 512])
        c_wcore = din("c_wcore", [128, 256])
        c_valid = din("c_valid", [128, 128])
        c_cvalid = din("c_cvalid", [128, 8])
        c_lmask = din("c_lmask", [128, 1024], BF16)
        out_d = nc.dram_tensor("out", [16, 128, 1024], F32, kind="ExternalOutput").ap()
        kcmp_d = nc.dram_tensor("kcmp_scr", [2, 128, 16400], BF16, kind="Internal").ap()
        _dk = "ExternalOutput" if os.environ.get("KDEBUG", "") else "Internal"
        ow_d = nc.dram_tensor("ow_scr", [16, 128, 512], F32, kind=_dk).ap()
        oa_d = nc.dram_tensor("oa_scr", [16, 128, 512], F32, kind=_dk).ap()
        hl_d = nc.dram_tensor("hl_scr", [16, 4, 128, 128], F32, kind="Internal").ap()
        u_d = nc.dram_tensor("u_scr", [4, 128, 16387], F32, kind="Internal").ap()
        b_ud = [Buf() for _ in range(4)]
        b_kcmp = [Buf(), Buf()]
        b_ow = [Buf() for _ in range(16)]
        b_oa = [Buf() for _ in range(16)]
        b_hl = [Buf() for _ in range(16)]
        b_out = [Buf() for _ in range(16)]

        import contextlib
        cur_stack = [None]

        def sb(name, shape, dt=F32):
            if cur_stack[0] is None:
                return nc.alloc_sbuf_tensor(name, list(shape), dt), Buf()
            return cur_stack[0].enter_context(nc.sbuf_tensor(name, list(shape), dt)), Buf()

        def pe_mm(out, lhsT, rhs, start, stop, reads, writes):
            C.op("pe", lambda e: e.matmul(out, lhsT=lhsT, rhs=rhs, start=start, stop=stop, skip_group_check=True), reads, writes)

        def act(out, in_, func, reads, writes, **kw):
            C.op("act", lambda e: e.activation(out=out, in_=in_, func=func, **kw), reads, writes)

        def ts(eng, out, in0, s1, s2, op0, op1, reads, writes):
            if op1 is None:
                C.op(eng, lambda e: e.tensor_scalar(out=out, in0=in0, scalar1=s1, scalar2=None, op0=op0), reads, writes)
            else:
                C.op(eng, lambda e: e.tensor_scalar(out=out, in0=in0, scalar1=s1, scalar2=s2, op0=op0, op1=op1), reads, writes)

        def stt(out, in0, scalar, in1, op0, op1, reads, writes):
            C.op("dve", lambda e: e.scalar_tensor_tensor(out=out, in0=in0, scalar=scalar, in1=in1, op0=op0, op1=op1), reads, writes)

        def tt(eng, out, in0, in1, op, reads, writes):
            C.op(eng, lambda e: e.tensor_tensor(out=out, in0=in0, in1=in1, op=op), reads, writes)

        def cp(eng, out, in_, reads, writes):
            if eng == "act":
                C.op("act", lambda e: e.copy(out=out, in_=in_), reads, writes)
            else:
                C.op(eng, lambda e: e.tensor_copy(out=out, in_=in_), reads, writes)

        ps_tp = nc.alloc_psum_tensor("ps_tp", [128, 1024], BF16)
        b_ps_tp = Buf()
        PS = []
        for j in range(7):
            PS.append((nc.alloc_psum_tensor(f"ps{j}", [128, 512], F32), Buf()))

        identb, b_identb = sb("identb", [128, 128], BF16)
        identf, b_identf = sb("identf", [128, 128])
        onesbd, b_onesbd = sb("onesbd", [128, 128], BF16)
        gain_bc, b_gain = sb("gain_bc", [128, 1024])
        sg_all, b_sg = sb("sg_all", [128, 16 * 24])
        kgain, b_kgain = sb("kgain", [128, 3])
        qgain, b_qgain = sb("qgain", [128, 1])
        stage = [sb(f"stage{j}", [128, 512]) for j in range(2)]
        stage_i = [0]
        xring = [sb(f"xr{j}", [128, 1024]) for j in range(4)]
        xn, b_xn = sb("xn", [128, 1024], BF16)
        xn2, b_xn2 = sb("xn2", [128, 1024], BF16)
        xn_l = [(xn, b_xn), (xn2, b_xn2)]
        tp_l = [(ps_tp, b_ps_tp), (ps_tp, b_ps_tp)]
        sqjunk, b_sqjunk = sb("sqjunk", [128, 1024], BF16)
        hT, b_hT = sb("hT", [128, 8 * 512], BF16)
        smalls_l = [sb(f"smalls{j}", [128, 4]) for j in range(4)]

        C.dma(identb[:], c_identb, writes=[b_identb])
        C.dma(identf[:], c_identf, writes=[b_identf])
        C.dma(onesbd[:], c_onesbd, writes=[b_onesbd])
        C.dma(gain_bc[:], norm_gain.partition_broadcast(128), writes=[b_gain])
        for half in range(2):
            C.dma(kgain[half * 64:(half + 1) * 64, :], k_norm_gain.rearrange("j d -> d j"), writes=[b_kgain],
                  allow_slow_non_contiguous=True)
            C.dma(qgain[half * 64:(half + 1) * 64, :], q_norm_gain.rearrange("(d o) -> d o", o=1), writes=[b_qgain],
                  allow_slow_non_contiguous=True)

        def load_w_bf16(dst, src, ncols, b_dst):
            st, b_st = stage[stage_i[0] % len(stage)]
            stage_i[0] += 1
            C.dma(st[:, 0:ncols], src, writes=[b_st])
            eng = "dve" if stage_i[0] % 2 else "act"
            cp(eng, dst, st[:, 0:ncols], [b_st], [b_dst])

        def norm_chunk_a(xt, b_xt, slot):
            sm, b_sm = smalls_l[slot]
            act(sqjunk[:], xt[:], AF.Square, [b_xt], [b_sqjunk, b_sm], accum_out=sm[:, 0:1])
            act(sm[:, 1:2], sm[:, 0:1], AF.Ln, [b_sm], [b_sm], scale=1.0 / 1024, bias=EPS)
            act(sm[:, 2:3], sm[:, 1:2], AF.Exp, [b_sm], [b_sm], scale=-0.5)

        def norm_chunk_b1(xt, b_xt, slot, par=0):
            sm, b_sm = smalls_l[slot]
            xn_, b_xn_ = xn_l[par]
            tp_, b_tp_ = tp_l[par]
            stt(xn_[:], xt[:], sm[:, 2:3], gain_bc[:], ALU.mult, ALU.mult, [b_xt, b_sm, b_gain], [b_xn_])
            for k in range(8):
                C.op("pe", lambda e: e.transpose(out=tp_[:, k * 128:(k + 1) * 128], in_=xn_[:, k * 128:(k + 1) * 128],
                                                 identity=identb[:]), [b_xn_, b_identb], [b_tp_])

        def norm_chunk_b2(col0, hT_=None, b_hT_=None, par=0):
            if hT_ is None:
                hT_, b_hT_ = hT, b_hT
            tp_, b_tp_ = tp_l[par]
            for k in range(8):
                cp("act" if k % 2 else "dve", hT_[:, k * 512 + col0: k * 512 + col0 + 128], tp_[:, k * 128:(k + 1) * 128],
                   [b_tp_], [b_hT_])

        def norm_chunk_b(xt, b_xt, col0, slot, hT_=None, b_hT_=None, par=0):
            norm_chunk_b1(xt, b_xt, slot, par)
            norm_chunk_b2(col0, hT_, b_hT_, par)

        def norm_chunk(xt, b_xt, col0, hT_=None, b_hT_=None, par=0):
            norm_chunk_a(xt, b_xt, par)
            norm_chunk_b(xt, b_xt, col0, par, hT_, b_hT_, par)

        def head_norm1(src_aps, n, tmp, b_tmp, reads):
            sqb, rt = tmp
            for (src, lo, hi) in src_aps:
                act(sqb[lo:hi, 0:n], src, AF.Square, reads, [b_tmp])

        def head_norm2(src_aps, gain_ap, out_ap, n, sc, bs, ps_n, b_ps_n, tmp, b_tmp, reads, writes):
            sqb, rt = tmp
            pe_mm(ps_n[:, 0:n], onesbd[:], sqb[:, 0:n], True, True, [b_onesbd, b_tmp], [b_ps_n])
            act(rt[:, 0:n], ps_n[:, 0:n], AF.Ln, [b_ps_n], [b_tmp], scale=sc, bias=bs)
            act(rt[:, 0:n], rt[:, 0:n], AF.Exp, [b_tmp], [b_tmp], scale=-0.5)
            for (src, lo, hi) in src_aps:
                stt(out_ap[lo:hi, :], src, gain_ap[lo:hi, :], rt[lo:hi, 0:n], ALU.mult, ALU.mult,
                    reads + [b_tmp, b_kgain, b_qgain], writes)

        def head_norm(src_aps, gain_ap, out_ap, n, sc, bs, ps_n, b_ps_n, tmp, b_tmp, reads, writes):
            head_norm1(src_aps, n, tmp, b_tmp, reads)
            head_norm2(src_aps, gain_ap, out_ap, n, sc, bs, ps_n, b_ps_n, tmp, b_tmp, reads, writes)

        sc1 = contextlib.ExitStack()
        cur_stack[0] = sc1
        Ks, b_Ks = sb("Ks", [128, S], BF16)
        Vs, b_Vs = sb("Vs", [128, NCHUNK * 130], BF16)
        kcT, b_kcT = sb("kcT", [128, 1024], BF16)
        vcx, b_vcx = sb("vcx", [128, 8 * 130], BF16)
        q_all, b_q = sb("q_all", [128, 16 * 512], BF16)
        TDp, b_TDp = sb("TDp", [128, 1024])
        TPp, b_TPp = sb("TPp", [128, 1024])
        b31, b_b31 = sb("b31", [128, 8])
        valid, b_valid = sb("valid", [128, 128])
        C.dma(TDp[:], c_td, writes=[b_TDp])
        C.dma(TPp[:], c_tp, writes=[b_TPp])
        C.dma(b31[:], c_b31, writes=[b_b31])
        C.dma(valid[:], c_valid, writes=[b_valid])
        for h in range(8):
            for (T_, bT) in ((TDp, b_TDp), (TPp, b_TPp)):
                ts("dve", T_[:, h * 128:(h + 1) * 128], T_[:, h * 128:(h + 1) * 128], b31[:, h:h + 1], None, ALU.subtract, None,
                   [bT, b_b31], [bT])

        ck('setup0')
        phB = contextlib.ExitStack()
        cur_stack[0] = phB
        for j in range(4):
            stage.append(sb(f"stageB{j}", [128, 512]))
        WB, b_WB = sb("WB", [128, 8 * NB], BF16)
        for k in range(8):
            for (nm, c0, w) in W_SEGS:
                o = k * NB + W_OFF[nm]
                load_w_bf16(WB[:, o:o + w], w_in[k * 128:(k + 1) * 128, c0:c0 + w], w, b_WB)
        TW0, b_TW0 = sb("TW0", [128, 512])
        C.dma(TW0[:], c_tw0, writes=[b_TW0])
        hT2, b_hT2 = sb("hT2", [128, 8 * 512], BF16)
        hT_l = [(hT, b_hT), (hT2, b_hT2)]
        tp2 = PS[5][0][:].bitcast(BF16)
        tp_l[1] = (tp2, PS[5][1])
        ust = [sb(f"ust{j}", [128, 512]) for j in range(2)]
        zt3, b_zt3 = sb("zt3", [128, 3])
        C.op("dve", lambda e: e.memset(zt3[:], 0.0), [], [b_zt3])
        for ct in range(4):
            C.dma(u_d[ct, :, 0:3], zt3[:], reads=[b_zt3], writes=[b_ud[ct]])
        nt_sq, _ = sb("nt_sq", [128, 512], BF16)
        nt_rt, _ = sb("nt_rt", [128, 512])
        b_nt = Buf()
        nt2_sq, _ = sb("nt2_sq", [128, 512], BF16)
        nt2_rt, _ = sb("nt2_rt", [128, 512])
        b_nt2 = Buf()
        cst = [sb(f"cst{g}", [128, 512], BF16) for g in range(2)]
        Kw, b_Kw = sb("Kw", [128, 1024], BF16)
        Vw, b_Vw = sb("Vw", [128, 8 * 130], BF16)
        pw = [sb(f"pw{j}", [128, 512], BF16) for j in range(2)]
        ow_t, b_ow_t = sb("ow_t", [128, 512])
        coef, b_coef = sb("coef", [128, 8])
        zt, b_zt = sb("zt", [128, 16], BF16)
        C.op("dve", lambda e: e.memset(zt[:], 0.0), [], [b_zt])
        for g in range(2):
            C.dma(kcmp_d[g, :, 16384:16400], zt[:], reads=[b_zt], writes=[b_kcmp[g]])

        def wb(k, nm, off=0, w=128):
            o = k * NB + W_OFF[nm] + off
            return WB[:, o:o + w]

        ps_a, b_ps_a = PS[0]
        ps_b, b_ps_b = PS[1]
        ps_n, b_ps_n = PS[2]
        ps_v, b_ps_v = PS[3]
        ps_s = [PS[4], PS[3]]
        ps_o, b_ps_o = PS[6]
        fm_i = [0]

        cur_hT = [hT, b_hT]

        def fm_proj(nm, off, ncol_tile=512, col0=0):
            pt, bp = (ps_a, b_ps_a) if fm_i[0] % 2 == 0 else (ps_b, b_ps_b)
            fm_i[0] += 1
            for k in range(8):
                pe_mm(pt[:, 0:ncol_tile], wb(k, nm, off), cur_hT[0][:, k * 512 + col0:k * 512 + col0 + ncol_tile], k == 0, k == 7,
                      [b_WB, cur_hT[1]], [bp])
            return pt, bp

        ck('setupB')
        for t in range(32):
            if t == 1: ck('B1')
            if t == 2: ck('B2t')
            hT_c, b_hT_c = hT_l[t % 2]
            cur_hT[0], cur_hT[1] = hT_c, b_hT_c

            def emit_norm_a(tt_):
                if tt_ == 0:
                    for ch in range(4):
                        C.dma(xring[ch][0][:], x_d[ch * 128:(ch + 1) * 128, :], writes=[xring[ch][1]])
                for ch in range(4):
                    s = 4 * tt_ + ch
                    norm_chunk_a(xring[s % 4][0], xring[s % 4][1], ch)

            def emit_norm_b(tt_):
                hTn, b_hTn = hT_l[tt_ % 2]
                for ch in range(5):
                    if ch < 4:
                        s = 4 * tt_ + ch
                        norm_chunk_b1(xring[s % 4][0], xring[s % 4][1], ch, par=s % 2)
                    if ch >= 1:
                        norm_chunk_b2((ch - 1) * 128, hTn, b_hTn, par=(4 * tt_ + ch - 1) % 2)
                if tt_ + 1 < 32:
                    for ch in range(4):
                        s2 = 4 * (tt_ + 1) + ch
                        C.dma(xring[s2 % 4][0][:], x_d[s2 * 128:(s2 + 1) * 128, :], writes=[xring[s2 % 4][1]])

            if t == 0:
                emit_norm_a(0)
                emit_norm_b(0)
            if t + 1 < 32:
                emit_norm_a(t + 1)
            for g in range(2):
                pt, bp = fm_proj("kc0" if g == 0 else "kc1", 0)
                cp("act", cst[g][0][:], pt[:], [bp], [cst[g][1]])
                C.dma(kcmp_d[g, :, t * 512:(t + 1) * 512], cst[g][0][:], reads=[cst[g][1]], writes=[b_kcmp[g]])
            pt1, bp1 = fm_proj("ks", 0)
            head_norm1([(pt1[:, :], 0, 128)], 512, (nt_sq, nt_rt), b_nt, [bp1])
            pt2, bp2 = fm_proj("kw", 0)
            head_norm1([(pt2[:, :], 0, 128)], 512, (nt2_sq, nt2_rt), b_nt2, [bp2])
            head_norm2([(pt1[:, :], 0, 128)], kgain[:, 1:2], Ks[:, t * 512:(t + 1) * 512], 512, 1.0 / 64, EPS, ps_n, b_ps_n,
                       (nt_sq, nt_rt), b_nt, [bp1], [b_Ks])
            head_norm2([(pt2[:, :], 0, 128)], kgain[:, 2:3], Kw[:, (t % 2) * 512:(t % 2) * 512 + 512], 512, 1.0 / 64, EPS, ps_n,
                       b_ps_n, (nt2_sq, nt2_rt), b_nt2, [bp2], [b_Kw])
            for ch in range(4):
                s = 4 * t + ch
                pv_, b_pv_ = (ps_v, b_ps_v) if ch % 2 == 0 else PS[4]
                for k in range(8):
                    pe_mm(pv_[:, 0:256], hT_c[:, k * 512 + ch * 128:k * 512 + ch * 128 + 128], wb(k, "vs", 0, 256), k == 0, k == 7,
                          [b_hT_c, b_WB], [b_pv_])
                rs = s % 8
                for g in range(2):
                    cp("act", Vs[:, s * 130 + g * 65:s * 130 + g * 65 + 64], pv_[:, g * 64:(g + 1) * 64], [b_pv_], [b_Vs])
                    cp("dve", Vw[:, rs * 130 + g * 65:rs * 130 + g * 65 + 64], pv_[:, 128 + g * 64:128 + (g + 1) * 64],
                       [b_pv_], [b_Vw])
                    cp("dve", Vs[:, s * 130 + g * 65 + 64:s * 130 + g * 65 + 65], valid[:, s:s + 1], [b_valid], [b_Vs])
                    cp("dve", Vw[:, rs * 130 + g * 65 + 64:rs * 130 + g * 65 + 65], valid[:, s:s + 1], [b_valid], [b_Vw])
            own = (t % 2 == 1)
            qi = (t - 1) // 2
            if own:
                qps = []
                for g in range(2):
                    pt, bp = PS[6] if g == 0 else PS[4]
                    for r in range(4):
                        h = 4 * g + r
                        off = h * 64 - 64 * g
                        for k in range(8):
                            pe_mm(pt[:, r * 128:(r + 1) * 128], wb(k, "q", off), hT_c[:, k * 512 + 384:k * 512 + 512], k == 0, k == 7,
                                  [b_WB, b_hT_c], [bp])
                    qps.append((pt, bp))
                qn = q_all[:, qi * 512:(qi + 1) * 512]
                q_src = [(qps[0][0][0:64, :], 0, 64), (qps[1][0][64:128, :], 64, 128)]
                head_norm1(q_src, 512, (nt_sq, nt_rt), b_nt, [qps[0][1], qps[1][1]])
                for k in range(8):
                    pe_mm(ps_v[:, 256:280], hT_c[:, k * 512 + 384:k * 512 + 512], wb(k, "bg", 0, 24), k == 0, k == 7, [b_hT_c, b_WB], [b_ps_v])
                act(sg_all[:, qi * 24:(qi + 1) * 24], ps_v[:, 256:280], AF.Exp, [b_ps_v], [b_sg], scale=-1.0)
                ts("dve", sg_all[:, qi * 24:(qi + 1) * 24], sg_all[:, qi * 24:(qi + 1) * 24], 1.0, None, ALU.add, None, [b_sg], [b_sg])
                C.op("dve", lambda e: e.reciprocal(out=sg_all[:, qi * 24:(qi + 1) * 24], in_=sg_all[:, qi * 24:(qi + 1) * 24]), [b_sg], [b_sg])
            for ct in range(4):
                pt, bp = fm_proj("u", ct * 128)
                us_, b_us = ust[ct % 2]
                cp("act" if ct % 2 else "dve", us_[:], pt[:], [bp], [b_us])
                C.dma(u_d[ct, :, 3 + t * 512:3 + (t + 1) * 512], us_[:], reads=[b_us], writes=[b_ud[ct]])
            if own:
                head_norm2(q_src, qgain[:, 0:1], qn, 512, 1.0, 64 * EPS, ps_n, b_ps_n, (nt_sq, nt_rt), b_nt,
                           [qps[0][1], qps[1][1]], [b_q])
            if t + 1 < 32:
                emit_norm_b(t + 1)
            if not own:
                continue
            items = [(g, w) for g in range(2) for w in range(5)]
            po_l = [PS[6], PS[0]]

            def win_qk(idx):
                g, w = items[idx]
                lo, hi = g * 64, (g + 1) * 64
                rc = 3 + w
                pst, bps = ps_s[idx % 2]
                tab = {0: (TW0, b_TW0, 0), 3: (TPp, b_TPp, g * 512), 4: (TDp, b_TDp, g * 512)}.get(w)
                pe_mm(pst[:], Kw[lo:hi, rc * 128:(rc + 1) * 128], qn[lo:hi, :], True, tab is None, [b_Kw, b_q], [bps])
                if tab is not None:
                    pe_mm(pst[:], identf[:], tab[0][:, tab[2]:tab[2] + 512], False, True, [b_identf, tab[1]], [bps])

            win_qk(0)
            for idx in range(10):
                g, w = items[idx]
                rc = 3 + w
                if idx + 1 < 10:
                    win_qk(idx + 1)
                pst, bps = ps_s[idx % 2]
                pt_, bpt = pw[idx % 2]
                po, b_po = po_l[g]
                act(pt_[:], pst[:], AF.Exp, [bps], [bpt])
                for r in range(4):
                    pe_mm(po[:, r * 65:(r + 1) * 65], pt_[:, r * 128:(r + 1) * 128],
                          Vw[:, rc * 130 + g * 65:rc * 130 + g * 65 + 65], w == 0 and r == 0, w == 4, [bpt, b_Vw], [b_po])
                if w == 4:
                    for r in range(4):
                        C.op("dve", lambda e: e.reciprocal(out=coef[:, r:r + 1], in_=po[:, r * 65 + 64:r * 65 + 65]), [b_po], [b_coef])
                    tt("dve", coef[:, 4:8], coef[:, 0:4], sg_all[:, qi * 24 + 16 + 4 * g:qi * 24 + 20 + 4 * g], ALU.mult,
                       [b_coef, b_sg], [b_coef])
                    for r in range(4):
                        h = 4 * g + r
                        ts("dve", ow_t[:, h * 64:(h + 1) * 64], po[:, r * 65:r * 65 + 64], coef[:, 4 + r:5 + r], None, ALU.mult, None,
                           [b_po, b_coef], [b_ow_t])
            C.dma(ow_d[qi], ow_t[:], reads=[b_ow_t], writes=[b_ow[qi]])

        C.barrier()
        del stage[2:]
        phB.close()

        ck('B')
        phL = contextlib.ExitStack()
        cur_stack[0] = phL
        lmask, b_lmask = sb("lmask", [128, 1024], BF16)
        C.dma(lmask[:], c_lmask, writes=[b_lmask])
        cw, b_cw = sb("cw", [128, 16])
        lvec, b_lvec = sb("lvec", [128, 48])
        for ct in range(4):
            C.dma(cw[:, ct * 4:(ct + 1) * 4], conv_w[:, ct * 128:(ct + 1) * 128].rearrange("k p -> p k"), writes=[b_cw],
                  allow_slow_non_contiguous=True)
        for j, v in enumerate((conv_b, lru_ba, lru_bx, lru_lambda)):
            C.dma(lvec[:, j * 4:(j + 1) * 4], v.rearrange("(c p) -> p c", p=128), writes=[b_lvec], allow_slow_non_contiguous=True)
        act(lvec[:, 16:20], lvec[:, 12:16], AF.Exp, [b_lvec], [b_lvec], scale=-1.0)
        act(lvec[:, 20:24], lvec[:, 16:20], AF.Ln, [b_lvec], [b_lvec], bias=1.0)
        ts("dve", lvec[:, 24:28], lvec[:, 20:24], -4.0, None, ALU.mult, None, [b_lvec], [b_lvec])
        ts("dve", lvec[:, 28:32], lvec[:, 20:24], -8.0, None, ALU.mult, None, [b_lvec], [b_lvec])
        ts("dve", lvec[:, 32:40], lvec[:, 4:12], 0.5, None, ALU.mult, None, [b_lvec], [b_lvec])
        Wab, b_Wab = sb("Wab", [128, 512], BF16)
        Wxb, b_Wxb = sb("Wxb", [128, 512], BF16)
        for (Wd, bW, src) in ((Wab, b_Wab, lru_wa), (Wxb, b_Wxb, lru_wx)):
            st, b_st = stage[stage_i[0] % len(stage)]
            stage_i[0] += 1
            C.op("dve", lambda e: e.memset(st[:, 0:512], 0.0), [], [b_st])
            for ct in range(4):
                C.dma(st[0:64, ct * 128:ct * 128 + 64], src[2 * ct], writes=[b_st])
                C.dma(st[64:128, ct * 128 + 64:ct * 128 + 128], src[2 * ct + 1], writes=[b_st])
            cp("dve", Wd[:], st[:, 0:512], [b_st], [bW])
        LT = 512
        NR = 4
        ubL = [sb(f"ubL{j}", [128, LT + 3]) for j in range(NR)]
        ucL_ = [sb(f"ucL{j}", [128, LT]) for j in range(NR)]
        ucbL_ = [sb(f"ucbL{j}", [128, LT], BF16) for j in range(NR)]
        trL_ = [sb(f"trL{j}", [128, LT]) for j in range(NR)]
        tiL_ = [sb(f"tiL{j}", [128, LT]) for j in range(NR)]
        aL_ = [sb(f"aL{j}", [128, LT]) for j in range(NR)]
        qL_ = [sb(f"qL{j}", [128, LT]) for j in range(NR)]
        hL = [sb(f"hL{j}", [128, LT]) for j in range(NR)]
        hst = [sb(f"hstate{j}", [128, 1]) for j in range(4)]
        for j in range(4):
            C.op("dve", lambda e: e.memset(hst[j][0][:], 0.0), [], [hst[j][1]])
        pg = [PS[0], PS[1], PS[2], PS[3]]
        NIT = 4 * (S // LT)

        def l_load(it):
            ct, T2 = it % 4, it // 4
            C.dma(ubL[it % NR][0][:], u_d[ct, :, T2 * LT:T2 * LT + LT + 3], reads=[b_ud[ct]], writes=[ubL[it % NR][1]])

        def l_stageA(p):
            for it in (2 * p, 2 * p + 1):
                ct, T2 = it % 4, it // 4
                rr = it % NR
                ub, b_ub = ubL[rr]
                ucL, b_ucL = ucL_[rr]
                ucbL, b_ucbL = ucbL_[rr]
                trL, b_trL = trL_[rr]
                tiL, b_tiL = tiL_[rr]
                aL, b_aL = aL_[rr]
                qL, b_qL = qL_[rr]
                pa, bpa = pg[(it % 2) * 2]
                pb, bpb = pg[(it % 2) * 2 + 1]
                ts("dve", ucL[:], ub[:, 0:LT], cw[:, ct * 4:ct * 4 + 1], lvec[:, ct:ct + 1], ALU.mult, ALU.add,
                   [b_ub, b_cw, b_lvec], [b_ucL])
                for k in range(1, 4):
                    stt(ucL[:], ub[:, k:k + LT], cw[:, ct * 4 + k:ct * 4 + k + 1], ucL[:], ALU.mult, ALU.add, [b_ub, b_cw, b_ucL], [b_ucL])
                cp("act", ucbL[:], ucL[:], [b_ucL], [b_ucbL])
                pe_mm(pa[:], Wab[:, ct * 128:(ct + 1) * 128], ucbL[:], True, True, [b_Wab, b_ucbL], [bpa])
                pe_mm(pb[:], Wxb[:, ct * 128:(ct + 1) * 128], ucbL[:], True, True, [b_Wxb, b_ucbL], [bpb])
                act(trL[:], pa[:], AF.Tanh, [bpa, b_lvec], [b_trL], scale=0.5, bias=lvec[:, 32 + ct:33 + ct])
                act(tiL[:], pb[:], AF.Tanh, [bpb, b_lvec], [b_tiL], scale=0.5, bias=lvec[:, 36 + ct:37 + ct])
                act(aL[:], trL[:], AF.Exp, [b_trL, b_lvec], [b_aL], scale=lvec[:, 24 + ct:25 + ct], bias=lvec[:, 24 + ct:25 + ct])
                act(qL[:], trL[:], AF.Exp, [b_trL, b_lvec], [b_qL], scale=lvec[:, 28 + ct:29 + ct], bias=lvec[:, 28 + ct:29 + ct])
            for it in (2 * p, 2 * p + 1):
                qL, b_qL = qL_[it % NR]
                act(qL[:], qL[:], AF.Sqrt, [b_qL], [b_qL], scale=-1.0, bias=1.0)

        def l_stageB(p):
            for it in (2 * p, 2 * p + 1):
                ct, T2 = it % 4, it // 4
                rr = it % NR
                ho, b_ho = hL[rr]
                ucL, b_ucL = ucL_[rr]
                tiL, b_tiL = tiL_[rr]
                aL, b_aL = aL_[rr]
                qL, b_qL = qL_[rr]
                stt(tiL[:], tiL[:], 1.0, ucL[:], ALU.add, ALU.mult, [b_tiL, b_ucL], [b_tiL])
                stt(qL[:], qL[:], 0.5, tiL[:], ALU.mult, ALU.mult, [b_qL, b_tiL], [b_qL])
                if T2 < 2:
                    tt("dve", qL[:], qL[:], lmask[:, T2 * 512:(T2 + 1) * 512], ALU.mult, [b_qL, b_lmask], [b_qL])
                C.op("dve", lambda e: e.tensor_tensor_scan(out=ho[:], data0=aL[:], data1=qL[:], initial=hst[ct][0][:, 0:1],
                                                            op0=ALU.mult, op1=ALU.add), [b_aL, b_qL, hst[ct][1]], [b_ho])
                cp("act", hst[ct][0][:, 0:1], ho[:, LT - 1:LT], [b_ho], [hst[ct][1]])
                if T2 % 2 == 1:
                    qi_ = (T2 - 1) // 2
                    C.dma(hl_d[qi_, ct], ho[:, 384:512], reads=[b_ho], writes=[b_hl[qi_]])

        NP = NIT // 2
        for it in range(4):
            l_load(it)
        l_stageA(0)
        for p in range(NP):
            if p + 1 < NP:
                l_stageA(p + 1)
            if p + 2 < NP:
                l_load(2 * (p + 2))
                l_load(2 * (p + 2) + 1)
            l_stageB(p)
        C.barrier()
        phL.close()
        ck('L')
        phB2 = contextlib.ExitStack()
        cur_stack[0] = phB2
        W1b, b_W1b = sb("W1b", [128, 32 * 256], BF16)
        for p0 in range(0, 32, 2):
            st, b_st = stage[stage_i[0] % len(stage)]
            stage_i[0] += 1
            for kv in range(2):
                C.dma(st[kv * 64:(kv + 1) * 64, 0:512].rearrange("d (p h) -> d p h", p=2),
                      cmp_w1[kv, p0 * 64:(p0 + 2) * 64, :].rearrange("(p d) h -> d p h", d=64), writes=[b_st])
            cp("act" if (p0 // 2) % 2 else "dve", W1b[:, p0 * 256:(p0 + 2) * 256], st[:, 0:512], [b_st], [b_W1b])
        W2kp, b_W2kp = sb("W2kp", [128, 384], BF16)
        W2v, b_W2v = sb("W2v", [128, 128], BF16)
        st, b_st = stage[stage_i[0] % len(stage)]
        stage_i[0] += 1
        C.op("dve", lambda e: e.memset(st[:, 0:384], 0.0), [], [b_st])
        for hh in range(2):
            C.dma(st[:, hh * 192 + 64:hh * 192 + 128], cmp_w2[0, hh * 128:(hh + 1) * 128, :], writes=[b_st])
        cp("dve", W2kp[:], st[:, 0:384], [b_st], [b_W2kp])
        st, b_st = stage[stage_i[0] % len(stage)]
        stage_i[0] += 1
        for hh in range(2):
            C.dma(st[:, hh * 64:(hh + 1) * 64], cmp_w2[1, hh * 128:(hh + 1) * 128, :], writes=[b_st])
        cp("dve", W2v[:], st[:, 0:128], [b_st], [b_W2v])
        peT, b_peT = sb("peT", [128, 32], BF16)
        st, b_st = stage[stage_i[0] % len(stage)]
        stage_i[0] += 1
        for kv in range(2):
            C.dma(st[kv * 64:(kv + 1) * 64, 0:32], cmp_pe[kv].rearrange("p d -> d p"), writes=[b_st], allow_slow_non_contiguous=True)
        cp("dve", peT[:], st[:, 0:32], [b_st], [b_peT])
        b1t, b_b1t = sb("b1t", [128, 4])
        for kv in range(2):
            for hh in range(2):
                C.dma(b1t[:, kv * 2 + hh:kv * 2 + hh + 1], cmp_b1[kv, hh * 128:(hh + 1) * 128].rearrange("(h o) -> h o", o=1),
                      writes=[b_b1t], allow_slow_non_contiguous=True)
        cvec, b_cvec = sb("cvec", [128, 4])
        ps_a, b_ps_a = PS[0]
        ps_b, b_ps_b = PS[1]
        ps_n, b_ps_n = PS[2]
        ps_v, b_ps_v = PS[3]
        ps_o, b_ps_o = PS[6]
        for kv in range(2):
            lo, hi = kv * 64, (kv + 1) * 64
            for hh in range(2):
                j = kv * 2 + hh
                pcv, bpcv = (ps_n, b_ps_n) if kv == 0 else (ps_v, b_ps_v)
                for p in range(32):
                    pe_mm(pcv[:, hh:hh + 1], W1b[lo:hi, p * 256 + hh * 128:p * 256 + hh * 128 + 128], peT[lo:hi, p:p + 1], p == 0, p == 31,
                          [b_W1b, b_peT], [bpcv])
        tt("dve", cvec[:, 0:2], ps_n[:, 0:2], b1t[:, 0:2], ALU.add, [b_ps_n, b_b1t], [b_cvec])
        tt("dve", cvec[:, 2:4], ps_v[:, 0:2], b1t[:, 2:4], ALU.add, [b_ps_v, b_b1t], [b_cvec])
        cvalid, b_cvalid = sb("cvalid", [128, 8])
        C.dma(cvalid[:], c_cvalid, writes=[b_cvalid])
        cin = [sb(f"cin{g}", [128, 2064], BF16) for g in range(2)]
        hact, b_hact = sb("hact", [128, 1024], BF16)
        n2_sq, _ = sb("n2_sq", [128, 128], BF16)
        n2_rt, _ = sb("n2_rt", [128, 128])
        b_n2 = Buf()
        for Cc in range(8):
            for g in range(2):
                C.dma(cin[g][0][:], kcmp_d[g, :, Cc * 2048:Cc * 2048 + 2064], reads=[b_kcmp[g]], writes=[cin[g][1]])
            for kv in range(2):
                lo, hi = kv * 64, (kv + 1) * 64
                psh, bph = (ps_a, b_ps_a) if kv == 0 else (ps_b, b_ps_b)
                for g in range(2):
                    for hh in range(2):
                        o = (g * 2 + hh) * 128
                        for p in range(32):
                            pe_mm(psh[:, o:o + 128], W1b[lo:hi, p * 256 + hh * 128:p * 256 + hh * 128 + 128],
                                  cin[g][0][lo:hi, p:p + 2033:16], p == 0, p == 31, [b_W1b, cin[g][1]], [bph])
                for g in range(2):
                    for hh in range(2):
                        o = (g * 2 + hh) * 128
                        ho = ((kv * 2 + g) * 2 + hh) * 128
                        act(hact[:, ho:ho + 128], psh[:, o:o + 128], AF.Silu, [bph, b_cvec], [b_hact],
                            bias=cvec[:, kv * 2 + hh:kv * 2 + hh + 1])
            n_ = 0
            for g in range(2):
                for hh in range(2):
                    ho = ((0 * 2 + g) * 2 + hh) * 128
                    c0 = hh * 192 + (64 if g == 0 else 0)
                    pe_mm(ps_v[:, 0:128], W2kp[:, c0:c0 + 128], hact[:, ho:ho + 128], n_ == 0, n_ == 3, [b_W2kp, b_hact], [b_ps_v])
                    n_ += 1
            head_norm([(ps_v[:, 0:128], 0, 128)], kgain[:, 0:1], kcT[:, Cc * 128:(Cc + 1) * 128], 128, 1.0 / 64, EPS, ps_n, b_ps_n,
                      (n2_sq, n2_rt), b_n2, [b_ps_v], [b_kcT])
            for g in range(2):
                for hh in range(2):
                    ho = ((1 * 2 + g) * 2 + hh) * 128
                    pe_mm(ps_o[:, g * 64:(g + 1) * 64], hact[:, ho:ho + 128], W2v[:, hh * 64:(hh + 1) * 64], hh == 0, hh == 1,
                          [b_hact, b_W2v], [b_ps_o])
            for g in range(2):
                o = Cc * 130 + g * 65
                ts("dve", vcx[:, o:o + 64], ps_o[:, g * 64:(g + 1) * 64], cvalid[:, Cc:Cc + 1], None, ALU.mult, None,
                   [b_ps_o, b_cvalid], [b_vcx])
                cp("dve", vcx[:, o + 64:o + 65], cvalid[:, Cc:Cc + 1], [b_cvalid], [b_vcx])
        if os.environ.get("KDEBUG", ""):
            dbg_kc = nc.dram_tensor("dbg_kc", [128, 1024], BF16, kind="ExternalOutput").ap()
            dbg_vc = nc.dram_tensor("dbg_vc", [128, 1040], BF16, kind="ExternalOutput").ap()
            C.dma(dbg_kc, kcT[:], reads=[b_kcT], writes=[Buf()])
            C.dma(dbg_vc, vcx[:], reads=[b_vcx], writes=[Buf()])
        C.barrier()
        phB2.close()

        ck('Bm')
        phC = contextlib.ExitStack()
        cur_stack[0] = phC
        GCp, b_GCp = sb("GCp", [128, 1024])
        BigSh, b_BigSh = sb("BigSh", [128, 384])
        KX1, b_KX1 = sb("KX1", [128, S], BF16)
        Pool_, b_Pool = sb("Pool_", [128, 2048], BF16)
        wrel, b_wrel = sb("wrel", [128, 512])
        wcore, b_wcore = sb("wcore", [128, 256])
        C.dma(GCp[:], c_gc, writes=[b_GCp])
        C.dma(BigSh[:], c_bigsh, writes=[b_BigSh])
        for j in range(4):
            cp(("act", "dve", "act", "dve")[j], KX1[64:128, j * 4096:(j + 1) * 4096], Ks[64:128, j * 4096:(j + 1) * 4096], [b_Ks], [b_KX1])
        for j in range(4):
            C.dma(KX1[0:64, j * 4096:(j + 1) * 4096], c_ind, writes=[b_KX1])
            C.dma(Ks[64:128, j * 4096:(j + 1) * 4096], c_ind, writes=[b_Ks])
        KX = [Ks, KX1]
        b_KX = [b_Ks, b_KX1]
        C.dma(Pool_[:], c_pool, writes=[b_Pool])
        C.dma(wrel[:], c_wrel, writes=[b_wrel])
        C.dma(wcore[:], c_wcore, writes=[b_wcore])
        for h in range(8):
            ts("dve", GCp[:, h * 128:(h + 1) * 128], GCp[:, h * 128:(h + 1) * 128], b31[:, h:h + 1], None, ALU.subtract, None,
               [b_GCp, b_b31], [b_GCp])
        pcs, b_pcs = sb("pcs", [128, 8 * 512], BF16)
        pring = [sb(f"pring{j}", [128, 512], BF16) for j in range(3)]
        NegMp, b_NegM = sb("NegMp", [128, 320], BF16)
        C.op("dve", lambda e: e.memset(NegMp[:, 0:64], 0.0), [], [b_NegM])
        NegM = NegMp[:, 64:320]
        mr1, b_mr1 = sb("mr1", [128, 128], BF16)
        qx = [sb(f"qx{g}", [128, 4 * 512], BF16) for g in range(2)]
        imp, b_imp = sb("imp", [128, 256])
        score, b_score = sb("score", [128, 256])
        sc2, b_sc2 = sb("sc2", [128, 256])
        mf, b_mf = sb("mf", [128, 256])
        m8, b_m8 = sb("m8", [128, 16])
        thr, b_thr = sb("thr", [128, 1])
        oacc_l = [sb(f"oacc{j}", [128, 512]) for j in range(2)]
        coefc, b_coefc = sb("coefc", [128, 8])
        rden, b_rden = sb("rden", [128, 4])
        coef2, b_coef2 = sb("coef2", [128, 8])
        rden2, b_rden2 = sb("rden2", [128, 4])
        ps_c = [PS[3], PS[4]]
        ps_s3 = [PS[0], PS[1], PS[6]]
        ps_oc, b_ps_oc = PS[2]
        psu = [PS[3], PS[4]]
        ps_os, b_ps_os = PS[5]

        def geom(n):
            i, g = n // 2, n % 2
            M = 8 * i + 7
            return i, g, M, i // 2 + 1, g * 64, (g + 1) * 64

        def prep_load(n):
            i, g, M, nC, lo, hi = geom(n)
            if g == 0:
                oacc, b_oacc = oacc_l[i % 2]
                C.dma(oacc[:], ow_d[i], reads=[b_ow[i]], writes=[b_oacc])
            qn = q_all[:, i * 512:(i + 1) * 512]
            qxt, b_qxt = qx[g]
            for s_ in range(M // 32 + 1):
                cp("dve" if s_ % 2 else "act", qxt[lo:hi, s_ * 512:(s_ + 1) * 512], qn[lo:hi, :], [b_q], [b_qxt])

        def prep_qk(n, Cc):
            i, g, M, nC, lo, hi = geom(n)
            qn = q_all[:, i * 512:(i + 1) * 512]
            pst, bps = ps_c[Cc % 2]
            delta = 128 * Cc - 8 * M + 64
            near = delta > -128
            pe_mm(pst[:], kcT[lo:hi, Cc * 128:(Cc + 1) * 128], qn[lo:hi, :], True, not near, [b_kcT, b_q], [bps])
            if near:
                pe_mm(pst[:], BigSh[:, 128 + delta:256 + delta], GCp[:, g * 512:(g + 1) * 512], False, True,
                      [b_BigSh, b_GCp], [bps])
            act(pcs[:, Cc * 512:(Cc + 1) * 512], pst[:], AF.Exp, [bps], [b_pcs])

        def prep_pv(n, Cc):
            i, g, M, nC, lo, hi = geom(n)
            for r in range(4):
                pe_mm(ps_oc[:, r * 65:(r + 1) * 65], pcs[:, Cc * 512 + r * 128:Cc * 512 + (r + 1) * 128],
                      vcx[:, Cc * 130 + g * 65:Cc * 130 + g * 65 + 65], Cc == 0 and r == 0, Cc == nC - 1, [b_pcs, b_vcx], [b_ps_oc])

        def prep_imp(n):
            i, g, M, nC, lo, hi = geom(n)
            oacc, b_oacc = oacc_l[i % 2]
            for r in range(4):
                pu, bpu = psu[r // 2]
                for Cc in range(nC):
                    pe_mm(pu[:, (r % 2) * 256:(r % 2) * 256 + 256], pcs[:, Cc * 512 + r * 128:Cc * 512 + (r + 1) * 128],
                          Pool_[:, Cc * 256:(Cc + 1) * 256], Cc == 0, Cc == nC - 1, [b_pcs, b_Pool], [bpu])
            for r in range(4):
                ts("dve", rden[:, r:r + 1], ps_oc[:, r * 65 + 64:r * 65 + 65], 1e-30, None, ALU.add, None, [b_ps_oc], [b_rden])
                C.op("dve", lambda e: e.reciprocal(out=rden[:, r:r + 1], in_=rden[:, r:r + 1]), [b_rden], [b_rden])
            ts("dve", imp[:], psu[0][0][:, 0:256], rden[:, 0:1], None, ALU.mult, None, [psu[0][1], b_rden], [b_imp])
            for r in range(1, 4):
                pu, bpu = psu[r // 2]
                stt(imp[:], pu[:, (r % 2) * 256:(r % 2) * 256 + 256], rden[:, r:r + 1], imp[:], ALU.mult, ALU.add,
                    [bpu, b_rden, b_imp], [b_imp])
            tt("dve", coefc[:, 0:4], rden[:], sg_all[:, i * 24 + 4 * g:i * 24 + 4 * g + 4], ALU.mult, [b_rden, b_sg], [b_coefc])
            for r in range(4):
                h = 4 * g + r
                stt(oacc[:, h * 64:(h + 1) * 64], ps_oc[:, r * 65:r * 65 + 64], coefc[:, r:r + 1], oacc[:, h * 64:(h + 1) * 64],
                    ALU.mult, ALU.add, [b_ps_oc, b_coefc, b_oacc], [b_oacc])
            tt("dve", score[:], imp[:], wrel[:, 256 - 2 * M:512 - 2 * M], ALU.add, [b_imp, b_wrel], [b_score])
            tt("dve", score[:], score[:], wcore[:], ALU.add, [b_score, b_wcore], [b_score])
            C.op("dve", lambda e: e.max(out=m8[:, 0:8], in_=score[:]), [b_score], [b_m8])
            C.op("dve", lambda e: e.match_replace(out=sc2[:], in_to_replace=m8[:, 0:8], in_values=score[:], imm_value=-1e30),
                 [b_score, b_m8], [b_sc2])
            C.op("dve", lambda e: e.max(out=m8[:, 8:16], in_=sc2[:]), [b_sc2], [b_m8])
            ts("dve", thr[:], m8[:, 15:16], -0.5, None, ALU.max, None, [b_m8], [b_thr])
            ts("dve", mf[:], score[:], thr[:, 0:1], None, ALU.is_ge, None, [b_score, b_thr], [b_mf])
            ts("dve", NegM, mf[:], -1.0, 30000.0, ALU.add, ALU.mult, [b_mf], [b_NegM])

        b_tpm = [Buf(), Buf()]

        def prep_mask(n):
            i, g, M, nC, lo, hi = geom(n)
            qxt, b_qxt = qx[g]
            mlo, mhi = (64, 128) if g == 0 else (0, 64)
            for s_ in range(M // 32 + 1):
                c0 = 0
                if g == 1:
                    C.op("pe", lambda e: e.transpose(out=ps_tp[0:64, c0:c0 + 128], in_=NegMp[:, 64 + 64 * s_:128 + 64 * s_],
                                                     identity=identb[:]), [b_NegM, b_identb], [b_ps_tp])
                else:
                    C.op("pe", lambda e: e.transpose(out=ps_tp[:, c0:c0 + 128], in_=NegMp[:, 64 * s_:64 * s_ + 128],
                                                     identity=identb[:]), [b_NegM, b_identb], [b_ps_tp])
                for r in range(4):
                    cp("act" if r < 2 else "dve", qxt[mlo:mhi, s_ * 512 + r * 128:s_ * 512 + (r + 1) * 128],
                       ps_tp[mlo:mhi, c0:c0 + 128], [b_ps_tp], [b_qxt])

        def slc_qk(n, kc):
            i, g, M, nC, lo, hi = geom(n)
            s_ = kc // 32
            pst, bps = ps_s3[kc % 3]
            near = kc >= M - 1
            pe_mm(pst[:], KX[g][:, kc * 128:(kc + 1) * 128], qx[g][0][:, s_ * 512:(s_ + 1) * 512], True, not near,
                  [b_KX[g], qx[g][1]], [bps])
            if kc == M:
                pe_mm(pst[:], identf[:], TDp[:, g * 512:(g + 1) * 512], False, True, [b_identf, b_TDp], [bps])
            elif kc == M - 1:
                pe_mm(pst[:], identf[:], TPp[:, g * 512:(g + 1) * 512], False, True, [b_identf, b_TPp], [bps])

        def slc_final(n):
            i, g, M, nC, lo, hi = geom(n)
            oacc, b_oacc = oacc_l[i % 2]
            for r in range(4):
                C.op("dve", lambda e: e.reciprocal(out=rden2[:, r:r + 1], in_=ps_os[:, r * 65 + 64:r * 65 + 65]), [b_ps_os], [b_rden2])
            tt("dve", coef2[:, 4:8], rden2[:], sg_all[:, i * 24 + 8 + 4 * g:i * 24 + 12 + 4 * g], ALU.mult, [b_rden2, b_sg], [b_coef2])
            for r in range(4):
                h = 4 * g + r
                stt(oacc[:, h * 64:(h + 1) * 64], ps_os[:, r * 65:r * 65 + 64], coef2[:, 4 + r:5 + r], oacc[:, h * 64:(h + 1) * 64],
                    ALU.mult, ALU.add, [b_ps_os, b_coef2, b_oacc], [b_oacc])
            if g == 1:
                C.dma(oa_d[i], oacc[:], reads=[b_oacc], writes=[b_oa[i]])

        def prep_all(n):
            i, g, M, nC, lo, hi = geom(n)
            prep_load(n)
            for Cc in range(nC):
                prep_qk(n, Cc)
                prep_pv(n, Cc)
            prep_imp(n)
            prep_mask(n)

        prep_all(0)
        for n in range(32):
            i, g, M, nC, lo, hi = geom(n)
            if n == 2: ck('C1')
            nxt = n + 1 if n + 1 < 32 else None
            nCn = geom(nxt)[3] if nxt is not None else 0
            slc_qk(n, 0)
            if M >= 1:
                slc_qk(n, 1)
            if nxt is not None:
                prep_load(nxt)
            for kc in range(M + 1):
                if kc + 2 <= M:
                    slc_qk(n, kc + 2)
                if nxt is not None:
                    if kc < nCn:
                        prep_qk(nxt, kc)
                    if 1 <= kc <= nCn:
                        prep_pv(nxt, kc - 1)
                    if kc == nCn + 1:
                        prep_imp(nxt)
                pst, bps = ps_s3[kc % 3]
                pt_, bpt = pring[kc % 3]
                act(pt_[:], pst[:], AF.Exp, [bps], [bpt])
                for r in range(4):
                    pe_mm(ps_os[:, r * 65:(r + 1) * 65], pt_[:, r * 128:(r + 1) * 128],
                          Vs[:, kc * 130 + g * 65:kc * 130 + g * 65 + 65], kc == 0 and r == 0, kc == M, [bpt, b_Vs], [b_ps_os])
            if nxt is not None:
                if nCn + 1 > M:
                    prep_imp(nxt)
                prep_mask(nxt)
            slc_final(n)
        C.barrier()
        phC.close()
        sc1.close()

        ck('C')
        phD = contextlib.ExitStack()
        cur_stack[0] = phD
        for j in range(6):
            stage.append(sb(f"stageD{j}", [128, 512]))
        Wg, b_Wg = sb("Wg", [128, 8 * 512], BF16)
        Wm, b_Wm = sb("Wm", [128, 8 * 2048], BF16)
        Wpa, b_Wpa = sb("Wpa", [128, 4 * 1024], BF16)
        Wpb, b_Wpb = sb("Wpb", [128, 4 * 1024], BF16)
        Wo, b_Wo = sb("Wo", [128, 8 * 1024], BF16)
        Wgl, b_Wgl = sb("Wgl", [128, 8 * 512], BF16)
        hraw, b_hraw = sb("hraw", [128, 512])
        for k in range(8):
            load_w_bf16(Wg[:, k * 512:(k + 1) * 512], w_in[k * 128:(k + 1) * 128, 1280:1792], 512, b_Wg)
            load_w_bf16(Wgl[:, k * 512:(k + 1) * 512], w_in[k * 128:(k + 1) * 128, 2328:2840], 512, b_Wgl)
            for pc in range(4):
                load_w_bf16(Wm[:, k * 2048 + pc * 512:k * 2048 + (pc + 1) * 512],
                            w_in[k * 128:(k + 1) * 128, 2840 + pc * 512:2840 + (pc + 1) * 512], 512, b_Wm)
            for pc in range(2):
                load_w_bf16(Wo[:, k * 1024 + pc * 512:k * 1024 + (pc + 1) * 512], w_out[k * 128:(k + 1) * 128, pc * 512:(pc + 1) * 512],
                            512, b_Wo)
        for k in range(4):
            for pc in range(2):
                load_w_bf16(Wpa[:, k * 1024 + pc * 512:k * 1024 + (pc + 1) * 512],
                            w_proj_a[k * 128:(k + 1) * 128, pc * 512:(pc + 1) * 512], 512, b_Wpa)
                load_w_bf16(Wpb[:, k * 1024 + pc * 512:k * 1024 + (pc + 1) * 512],
                            w_proj_b[k * 128:(k + 1) * 128, pc * 512:(pc + 1) * 512], 512, b_Wpb)
        xd = [sb(f"xd{j}", [128, 1024]) for j in range(4)]
        oa_t, b_oa_t = sb("oa_t", [128, 512])
        sgn, b_sgn = sb("sgn", [128, 512])
        ya, b_ya = sb("ya", [128, 512], BF16)
        yaT, b_yaT = sb("yaT", [128, 4 * 512], BF16)
        hlT, b_hlT = sb("hlT", [128, 4 * 512], BF16)
        mT, b_mT = sb("mT", [128, 8 * 512], BF16)
        sga, b_sga = sb("sga", [128, 512])
        sgb, b_sgb = sb("sgb", [128, 512])
        m1, b_m1 = sb("m1", [128, 512])
        ot = [sb(f"ot{j}", [128, 512]) for j in range(2)]
        pA, b_pA = PS[1]
        pGA, b_pGA = PS[2]
        pB, b_pB = PS[3]
        pGB, b_pGB = PS[4]
        for grp in range(4):
            for j in range(4):
                i = grp * 4 + j
                slot = 8 * i + 7
                C.dma(xd[j][0][:], x_d[slot * 128:(slot + 1) * 128, :], writes=[xd[j][1]])
                norm_chunk_a(xd[j][0], xd[j][1], j)
            for j in range(4):
                norm_chunk_b(xd[j][0], xd[j][1], j * 128, j, None, None, par=0)
            for j in range(4):
                i = grp * 4 + j
                for k in range(8):
                    pe_mm(PS[0][0][:], hT[:, k * 512 + j * 128:k * 512 + (j + 1) * 128], Wg[:, k * 512:(k + 1) * 512], k == 0, k == 7,
                          [b_hT, b_Wg], [PS[0][1]])
                act(sgn[:], PS[0][0][:], AF.Silu, [PS[0][1]], [b_sgn])
                C.dma(oa_t[:], oa_d[i], reads=[b_oa[i]], writes=[b_oa_t])
                tt("dve", ya[:], sgn[:], oa_t[:], ALU.mult, [b_sgn, b_oa_t], [b_ya])
                for kc in range(4):
                    C.op("pe", lambda e: e.transpose(out=ps_tp[:, kc * 128:(kc + 1) * 128], in_=ya[:, kc * 128:(kc + 1) * 128],
                                                     identity=identb[:]), [b_ya, b_identb], [b_ps_tp])
                for kc in range(4):
                    cp("act" if kc % 2 else "dve", yaT[:, kc * 512 + j * 128:kc * 512 + (j + 1) * 128], ps_tp[:, kc * 128:(kc + 1) * 128],
                       [b_ps_tp], [b_yaT])
            for ct in range(4):
                for k in range(8):
                    pe_mm(PS[0][0][:], Wgl[:, k * 512 + ct * 128:k * 512 + (ct + 1) * 128], hT[:, k * 512:(k + 1) * 512], k == 0, k == 7,
                          [b_Wgl, b_hT], [PS[0][1]])
                act(sgn[:], PS[0][0][:], AF.Silu, [PS[0][1]], [b_sgn])
                C.dma(hraw[:, 0:512].rearrange("p (j c) -> p j c", j=4),
                      hl_d[grp * 4:(grp + 1) * 4, ct].rearrange("j p c -> p j c"),
                      reads=[b_hl[grp * 4 + j] for j in range(4)], writes=[b_hraw])
                tt("dve", hlT[:, ct * 512:(ct + 1) * 512], sgn[:], hraw[:], ALU.mult, [b_sgn, b_hraw], [b_hlT])
            for f in range(8):
                for k in range(8):
                    pe_mm(pGA[:], Wm[:, k * 2048 + f * 128:k * 2048 + (f + 1) * 128], hT[:, k * 512:(k + 1) * 512], k == 0, k == 7,
                          [b_Wm, b_hT], [b_pGA])
                for k in range(8):
                    pe_mm(pGB[:], Wm[:, k * 2048 + 1024 + f * 128:k * 2048 + 1024 + (f + 1) * 128], hT[:, k * 512:(k + 1) * 512],
                          k == 0, k == 7, [b_Wm, b_hT], [b_pGB])
                for kc in range(4):
                    pe_mm(pA[:], Wpa[:, kc * 1024 + f * 128:kc * 1024 + (f + 1) * 128], yaT[:, kc * 512:(kc + 1) * 512], kc == 0, kc == 3,
                          [b_Wpa, b_yaT], [b_pA])
                for kc in range(4):
                    pe_mm(pB[:], Wpb[:, kc * 1024 + f * 128:kc * 1024 + (f + 1) * 128], hlT[:, kc * 512:(kc + 1) * 512], kc == 0, kc == 3,
                          [b_Wpb, b_hlT], [b_pB])
                act(sga[:], pGA[:], AF.Sigmoid, [b_pGA], [b_sga])
                act(sgb[:], pGB[:], AF.Sigmoid, [b_pGB], [b_sgb])
                tt("dve", m1[:], sga[:], pA[:], ALU.mult, [b_sga, b_pA], [b_m1])
                tt("dve", sgb[:], sgb[:], pB[:], ALU.mult, [b_sgb, b_pB], [b_sgb])
                tt("dve", mT[:, f * 512:(f + 1) * 512], m1[:], sgb[:], ALU.add, [b_m1, b_sgb], [b_mT])
            for j in range(4):
                i = grp * 4 + j
                for half in range(2):
                    pO, b_pO = PS[5 + half]
                    for f in range(8):
                        pe_mm(pO[:], mT[:, f * 512 + j * 128:f * 512 + (j + 1) * 128], Wo[:, f * 1024 + half * 512:f * 1024 + (half + 1) * 512],
                              f == 0, f == 7, [b_mT, b_Wo], [b_pO])
                    tt("dve", ot[half][0][:], pO[:], xd[j][0][:, half * 512:(half + 1) * 512], ALU.add, [b_pO, xd[j][1]], [ot[half][1]])
                    C.dma(out_d[i, :, half * 512:(half + 1) * 512], ot[half][0][:], reads=[ot[half][1]], writes=[b_out[i]])
        C.barrier()
        phD.close()
        phB_done = True
    except _Stop:
        C.barrier()
    return nc, C, None


def _bf(a):
    return np.asarray(a, np.float32).astype(ml_dtypes.bfloat16)


def host_constants(rel_bias):
    rb = np.asarray(rel_bias, np.float32)
    cst = {}
    cst["c_identb"] = _bf(np.eye(128))
    cst["c_identf"] = np.eye(128, dtype=np.float32)
    p = np.arange(128)
    cst["c_onesbd"] = _bf((p[:, None] // 64) == (p[None, :] // 64))
    k = np.arange(128)[:, None]
    i = np.arange(128)[None, :]
    d = i - k
    td = np.where(d[:, None, :] >= 0, rb[t5_bucket_np(d)].transpose(0, 2, 1), np.float32(NEG))
    cst["c_td"] = np.ascontiguousarray(td.reshape(128, 1024), np.float32)
    d = i - k + 128
    tp = rb[t5_bucket_np(d)].transpose(0, 2, 1)
    cst["c_tp"] = np.ascontiguousarray(tp.reshape(128, 1024), np.float32)
    tw0 = np.where(i < k, np.float32(0), np.float32(NEG)).astype(np.float32)
    cst["c_tw0"] = np.ascontiguousarray(np.tile(tw0, (1, 4)), np.float32)
    r = np.arange(128)[:, None]
    d = i - 16 * (r - 64) - 31
    gc = np.where(d[:, None, :] >= 0, rb[t5_bucket_np(d)].transpose(0, 2, 1), np.float32(NEG))
    gc[127] = NEG
    cst["c_gc"] = np.ascontiguousarray(gc.reshape(128, 1024), np.float32)
    cst["c_b31"] = np.ascontiguousarray(np.tile(rb[31][None, :], (128, 1)), np.float32)
    xx = np.arange(384)[None, :] - 128
    bs = (r == xx).astype(np.float32)
    bs[127] = (xx[0] >= 127).astype(np.float32)
    cst["c_bigsh"] = bs
    pos = np.arange(4096)[None, :]
    cst["c_ind"] = _bf((pos // 64) == np.arange(64)[:, None])
    pool = np.zeros((128, 8, 256), np.float32)
    for Cc in range(8):
        cb = 128 * Cc + np.arange(128)
        for j in range(256):
            pool[:, Cc, j] = (cb >= 4 * j - 1) & (cb <= 4 * j + 3)
    cst["c_pool"] = _bf(pool.reshape(128, 2048))
    ii = np.arange(128)[:, None]
    hi = (ii >= 64).astype(np.int64)
    xr = np.arange(512)[None, :] - 256
    wrel = np.where(xr > hi, np.float32(-1.0), np.where((xr == hi) | (xr == hi - 1), np.float32(1e9), np.float32(0)))
    cst["c_wrel"] = wrel.astype(np.float32)
    return cst


def core_constants(c):
    sh = 7 - c
    d = {}
    chv = (np.arange(128) >= sh).astype(np.float32)
    d["c_valid"] = np.ascontiguousarray(np.tile(chv[None, :], (128, 1)), np.float32)
    cb = np.arange(128)[:, None] + 128 * np.arange(8)[None, :]
    d["c_cvalid"] = (cb >= 8 * sh).astype(np.float32)
    posn = np.arange(1024)
    d["c_lmask"] = _bf(np.tile((posn >= 128 * sh)[None, :], (128, 1)))
    j = np.arange(256)
    wc = np.where(j < 2 * sh, np.float32(-3e9), np.where(j == 2 * sh, np.float32(1e9), np.float32(0)))
    d["c_wcore"] = np.ascontiguousarray(np.tile(wc[None, :], (128, 1)), np.float32)
    return d


_PROG = {}


def kernel(x, norm_gain, w_in, q_norm_gain, k_norm_gain, cmp_pe, cmp_w1, cmp_b1, cmp_w2, rel_bias,
           conv_w, conv_b, lru_wa, lru_ba, lru_wx, lru_bx, lru_lambda, w_proj_a, w_proj_b, w_out):
    if "nc" not in _PROG:
        _PROG["nc"] = build_program()[0]
    nc = _PROG["nc"]
    f = lambda a: np.ascontiguousarray(np.asarray(a), np.float32)
    shared = dict(norm_gain=f(norm_gain), w_in=f(w_in), q_norm_gain=f(q_norm_gain), k_norm_gain=f(k_norm_gain),
                  cmp_pe=f(cmp_pe), cmp_w1=f(cmp_w1), cmp_b1=f(cmp_b1), cmp_w2=f(cmp_w2), conv_w=f(conv_w),
                  conv_b=f(conv_b), lru_wa=f(lru_wa), lru_ba=f(lru_ba), lru_wx=f(lru_wx), lru_bx=f(lru_bx),
                  lru_lambda=f(lru_lambda), w_proj_a=f(w_proj_a), w_proj_b=f(w_proj_b), w_out=f(w_out))
    shared.update(host_constants(rel_bias))
    xs = f(x)[0]
    in_maps = []
    for c in range(8):
        sh = 7 - c
        xc = np.zeros((S, 1024), np.float32)
        xc[sh * 128:] = xs[:S - sh * 128]
        m = dict(shared)
        m["x"] = xc
        m.update(core_constants(c))
        in_maps.append(m)
    res = run_bass_kernel_spmd(nc, in_maps, core_ids=list(range(8)))
    out = np.zeros((1, S, 1024), np.float32)
    for c in range(8):
        o = np.asarray(res.results[c]["out"])
        for i in range(16):
            m_ = 8 * i + c
            out[0, m_ * 128:(m_ + 1) * 128] = o[i]
    _PROG["last"] = res
    return out
```

```python
import numpy as np
import ml_dtypes
import concourse.bass as bass
import concourse.mybir as mybir
from concourse.bass_utils import run_bass_kernel_spmd

F32 = mybir.dt.float32
BF16 = mybir.dt.bfloat16
AF = mybir.ActivationFunctionType
ALU = mybir.AluOpType
NEG = -30000.0
S = 16384
NCHUNK = 128
EPS = 1e-6


class Buf:
    __slots__ = ("w", "r")

    def __init__(self):
        self.w = None
        self.r = {}


class Eng:
    def __init__(self, name, handle, sem):
        self.name = name
        self.h = handle
        self.sem = sem
        self.count = 0
        self.waited = {}


class Ctx:
    NDSEM = 24

    def __init__(self, nc):
        self.nc = nc
        self.E = {}
        for name, h in (("pe", nc.tensor), ("act", nc.scalar), ("dve", nc.vector), ("pool", nc.gpsimd), ("sp", nc.sync)):
            self.E[name] = Eng(name, h, nc.alloc_semaphore(name="sem_" + name))
        self.dsems = [nc.alloc_semaphore(name=f"dsem{i}") for i in range(self.NDSEM)]
        self.dcnt = [0] * self.NDSEM
        self.dnext = 0

    def _wait(self, E, tok):
        sem, val, owner = tok
        if owner == "pe" and E.name == "pe":
            return
        key = id(sem)
        if E.waited.get(key, 0) >= val:
            return
        E.h.wait_ge(sem, val)
        E.waited[key] = val

    def _deps(self, E, reads, writes):
        for b in reads:
            if b.w is not None:
                self._wait(E, b.w)
        for b in writes:
            if b.w is not None:
                self._wait(E, b.w)
            for t in b.r.values():
                self._wait(E, t)

    def _mark(self, tok, reads, writes):
        key = tok[2] if tok[2] != "dma" else id(tok[0])
        for b in reads:
            b.r[key] = tok
        for b in writes:
            b.w = tok
            b.r = {}

    def op(self, eng, fn, reads=(), writes=()):
        E = self.E[eng]
        self._deps(E, reads, writes)
        ins = fn(E.h)
        E.count += 1
        ins.then_inc(E.sem, 1)
        self._mark((E.sem, E.count, eng), reads, writes)

    def dma(self, out, in_, reads=(), writes=(), queue="sp", **kw):
        E = self.E[queue]
        k = self.dnext
        self.dnext = (self.dnext + 1) % self.NDSEM
        sem = self.dsems[k]
        if self.dcnt[k] > 0:
            self._wait(E, (sem, 16 * self.dcnt[k], "dma"))
        self._deps(E, reads, writes)
        ins = E.h.dma_start(out=out, in_=in_, **kw)
        self.dcnt[k] += 1
        ins.then_inc(sem, 16)
        self._mark((sem, 16 * self.dcnt[k], "dma"), reads, writes)

    def barrier(self):
        toks = [(e.sem, e.count, e.name) for e in self.E.values() if e.count > 0]
        toks += [(self.dsems[k], 16 * self.dcnt[k], "dma") for k in range(self.NDSEM) if self.dcnt[k] > 0]
        for e in self.E.values():
            for t in toks:
                if t[2] == e.name:
                    continue
                key = id(t[0])
                if e.waited.get(key, 0) >= t[1]:
                    continue
                e.h.wait_ge(t[0], t[1])
                e.waited[key] = t[1]


def t5_bucket_np(dist):
    n = np.maximum(dist, 0)
    nf = np.maximum(n, 1).astype(np.float32)
    large = 16 + (np.log(nf / np.float32(16)) / np.float32(np.log(128 / 16)) * np.float32(16)).astype(np.int32)
    large = np.minimum(large, 31)
    return np.where(n < 16, n, large)


W_SEGS = [("kc0", 512, 64), ("vc0", 640, 64), ("kc1", 576, 64), ("vc1", 704, 64), ("ks", 768, 128),
          ("kw", 1024, 128), ("vs", 896, 128), ("vw", 1152, 128), ("u", 1816, 512),
          ("q", 0, 512), ("bg", 1792, 24)]
W_OFF = {}
_o = 0
for _n, _c, _w in W_SEGS:
    W_OFF[_n] = _o
    _o += _w
NB = _o


def build_program():
    nc = bass.Bass("TRN2", target_bir_lowering=False)
    C = Ctx(nc)

    import os
    STOP = os.environ.get("KSTOP", "")

    class _Stop(Exception):
        pass

    def ck(name):
        if STOP == name:
            raise _Stop()

    try:
        def din(name, shape, dt=F32):
            return nc.dram_tensor(name, list(shape), dt, kind="ExternalInput").ap()

        x_d = din("x", [S, 1024])
        w_in = din("w_in", [1024, 4888])
        norm_gain = din("norm_gain", [1024])
        q_norm_gain = din("q_norm_gain", [64])
        k_norm_gain = din("k_norm_gain", [3, 64])
        cmp_pe = din("cmp_pe", [2, 32, 64])
        cmp_w1 = din("cmp_w1", [2, 2048, 256])
        cmp_b1 = din("cmp_b1", [2, 256])
        cmp_w2 = din("cmp_w2", [2, 256, 64])
        conv_w = din("conv_w", [4, 512])
        conv_b = din("conv_b", [512])
        lru_wa = din("lru_wa", [8, 64, 64])
        lru_ba = din("lru_ba", [512])
        lru_wx = din("lru_wx", [8, 64, 64])
        lru_bx = din("lru_bx", [512])
        lru_lambda = din("lru_lambda", [512])
        w_proj_a = din("w_proj_a", [512, 1024])
        w_proj_b = din("w_proj_b", [512, 1024])
        w_out = din("w_out", [1024, 1024])
        c_identb = din("c_identb", [128, 128], BF16)
        c_identf = din("c_identf", [128, 128])
        c_onesbd = din("c_onesbd", [128, 128], BF16)
        c_td = din("c_td", [128, 1024])
        c_tp = din("c_tp", [128, 1024])
        c_tw0 = din("c_tw0", [128, 512])
        c_gc = din("c_gc", [128, 1024])
        c_b31 = din("c_b31", [128, 8])
        c_bigsh = din("c_bigsh", [128, 384])
        c_ind = din("c_ind", [64, 4096], BF16)
        c_pool = din("c_pool", [128, 2048], BF16)
        c_wrel = din("c_wrel", [128, 512])
        c_wcore = din("c_wcore", [128, 256])
        c_valid = din("c_valid", [128, 128])
        c_cvalid = din("c_cvalid", [128, 8])
        c_lmask = din("c_lmask", [128, 1024], BF16)
        out_d = nc.dram_tensor("out", [16, 128, 1024], F32, kind="ExternalOutput").ap()
        kcmp_d = nc.dram_tensor("kcmp_scr", [2, 128, 16400], BF16, kind="Internal").ap()
        _dk = "ExternalOutput" if os.environ.get("KDEBUG", "") else "Internal"
        ow_d = nc.dram_tensor("ow_scr", [16, 128, 512], F32, kind=_dk).ap()
        oa_d = nc.dram_tensor("oa_scr", [16, 128, 512], F32, kind=_dk).ap()
        hl_d = nc.dram_tensor("hl_scr", [16, 4, 128, 128], F32, kind="Internal").ap()
        u_d = nc.dram_tensor("u_scr", [4, 128, 16387], F32, kind="Internal").ap()
        b_ud = [Buf() for _ in range(4)]
        b_kcmp = [Buf(), Buf()]
        b_ow = [Buf() for _ in range(16)]
        b_oa = [Buf() for _ in range(16)]
        b_hl = [Buf() for _ in range(16)]
        b_out = [Buf() for _ in range(16)]

        import contextlib
        cur_stack = [None]

        def sb(name, shape, dt=F32):
            if cur_stack[0] is None:
                return nc.alloc_sbuf_tensor(name, list(shape), dt), Buf()
            return cur_stack[0].enter_context(nc.sbuf_tensor(name, list(shape), dt)), Buf()

        def pe_mm(out, lhsT, rhs, start, stop, reads, writes):
            C.op("pe", lambda e: e.matmul(out, lhsT=lhsT, rhs=rhs, start=start, stop=stop, skip_group_check=True), reads, writes)

        def act(out, in_, func, reads, writes, **kw):
            C.op("act", lambda e: e.activation(out=out, in_=in_, func=func, **kw), reads, writes)

        def ts(eng, out, in0, s1, s2, op0, op1, reads, writes):
            if op1 is None:
                C.op(eng, lambda e: e.tensor_scalar(out=out, in0=in0, scalar1=s1, scalar2=None, op0=op0), reads, writes)
            else:
                C.op(eng, lambda e: e.tensor_scalar(out=out, in0=in0, scalar1=s1, scalar2=s2, op0=op0, op1=op1), reads, writes)

        def stt(out, in0, scalar, in1, op0, op1, reads, writes):
            C.op("dve", lambda e: e.scalar_tensor_tensor(out=out, in0=in0, scalar=scalar, in1=in1, op0=op0, op1=op1), reads, writes)

        def tt(eng, out, in0, in1, op, reads, writes):
            C.op(eng, lambda e: e.tensor_tensor(out=out, in0=in0, in1=in1, op=op), reads, writes)

        def cp(eng, out, in_, reads, writes):
            if eng == "act":
                C.op("act", lambda e: e.copy(out=out, in_=in_), reads, writes)
            else:
                C.op(eng, lambda e: e.tensor_copy(out=out, in_=in_), reads, writes)

        ps_tp = nc.alloc_psum_tensor("ps_tp", [128, 1024], BF16)
        b_ps_tp = Buf()
        PS = []
        for j in range(7):
            PS.append((nc.alloc_psum_tensor(f"ps{j}", [128, 512], F32), Buf()))

        identb, b_identb = sb("identb", [128, 128], BF16)
        identf, b_identf = sb("identf", [128, 128])
        onesbd, b_onesbd = sb("onesbd", [128, 128], BF16)
        gain_bc, b_gain = sb("gain_bc", [128, 1024])
        sg_all, b_sg = sb("sg_all", [128, 16 * 24])
        kgain, b_kgain = sb("kgain", [128, 3])
        qgain, b_qgain = sb("qgain", [128, 1])
        stage = [sb(f"stage{j}", [128, 512]) for j in range(2)]
        stage_i = [0]
        xring = [sb(f"xr{j}", [128, 1024]) for j in range(4)]
        xn, b_xn = sb("xn", [128, 1024], BF16)
        xn2, b_xn2 = sb("xn2", [128, 1024], BF16)
        xn_l = [(xn, b_xn), (xn2, b_xn2)]
        tp_l = [(ps_tp, b_ps_tp), (ps_tp, b_ps_tp)]
        sqjunk, b_sqjunk = sb("sqjunk", [128, 1024], BF16)
        hT, b_hT = sb("hT", [128, 8 * 512], BF16)
        smalls_l = [sb(f"smalls{j}", [128, 4]) for j in range(4)]

        C.dma(identb[:], c_identb, writes=[b_identb])
        C.dma(identf[:], c_identf, writes=[b_identf])
        C.dma(onesbd[:], c_onesbd, writes=[b_onesbd])
        C.dma(gain_bc[:], norm_gain.partition_broadcast(128), writes=[b_gain])
        for half in range(2):
            C.dma(kgain[half * 64:(half + 1) * 64, :], k_norm_gain.rearrange("j d -> d j"), writes=[b_kgain],
                  allow_slow_non_contiguous=True)
            C.dma(qgain[half * 64:(half + 1) * 64, :], q_norm_gain.rearrange("(d o) -> d o", o=1), writes=[b_qgain],
                  allow_slow_non_contiguous=True)

        def load_w_bf16(dst, src, ncols, b_dst):
            st, b_st = stage[stage_i[0] % len(stage)]
            stage_i[0] += 1
            C.dma(st[:, 0:ncols], src, writes=[b_st])
            eng = "dve" if stage_i[0] % 2 else "act"
            cp(eng, dst, st[:, 0:ncols], [b_st], [b_dst])

        def norm_chunk_a(xt, b_xt, slot):
            sm, b_sm = smalls_l[slot]
            act(sqjunk[:], xt[:], AF.Square, [b_xt], [b_sqjunk, b_sm], accum_out=sm[:, 0:1])
            act(sm[:, 1:2], sm[:, 0:1], AF.Ln, [b_sm], [b_sm], scale=1.0 / 1024, bias=EPS)
            act(sm[:, 2:3], sm[:, 1:2], AF.Exp, [b_sm], [b_sm], scale=-0.5)

        def norm_chunk_b1(xt, b_xt, slot, par=0):
            sm, b_sm = smalls_l[slot]
            xn_, b_xn_ = xn_l[par]
            tp_, b_tp_ = tp_l[par]
            stt(xn_[:], xt[:], sm[:, 2:3], gain_bc[:], ALU.mult, ALU.mult, [b_xt, b_sm, b_gain], [b_xn_])
            for k in range(8):
                C.op("pe", lambda e: e.transpose(out=tp_[:, k * 128:(k + 1) * 128], in_=xn_[:, k * 128:(k + 1) * 128],
                                                 identity=identb[:]), [b_xn_, b_identb], [b_tp_])

        def norm_chunk_b2(col0, hT_=None, b_hT_=None, par=0):
            if hT_ is None:
                hT_, b_hT_ = hT, b_hT
            tp_, b_tp_ = tp_l[par]
            for k in range(8):
                cp("act" if k % 2 else "dve", hT_[:, k * 512 + col0: k * 512 + col0 + 128], tp_[:, k * 128:(k + 1) * 128],
                   [b_tp_], [b_hT_])

        def norm_chunk_b(xt, b_xt, col0, slot, hT_=None, b_hT_=None, par=0):
            norm_chunk_b1(xt, b_xt, slot, par)
            norm_chunk_b2(col0, hT_, b_hT_, par)

        def norm_chunk(xt, b_xt, col0, hT_=None, b_hT_=None, par=0):
            norm_chunk_a(xt, b_xt, par)
            norm_chunk_b(xt, b_xt, col0, par, hT_, b_hT_, par)

        def head_norm1(src_aps, n, tmp, b_tmp, reads):
            sqb, rt = tmp
            for (src, lo, hi) in src_aps:
                act(sqb[lo:hi, 0:n], src, AF.Square, reads, [b_tmp])

        def head_norm2(src_aps, gain_ap, out_ap, n, sc, bs, ps_n, b_ps_n, tmp, b_tmp, reads, writes):
            sqb, rt = tmp
            pe_mm(ps_n[:, 0:n], onesbd[:], sqb[:, 0:n], True, True, [b_onesbd, b_tmp], [b_ps_n])
            act(rt[:, 0:n], ps_n[:, 0:n], AF.Ln, [b_ps_n], [b_tmp], scale=sc, bias=bs)
            act(rt[:, 0:n], rt[:, 0:n], AF.Exp, [b_tmp], [b_tmp], scale=-0.5)
            for (src, lo, hi) in src_aps:
                stt(out_ap[lo:hi, :], src, gain_ap[lo:hi, :], rt[lo:hi, 0:n], ALU.mult, ALU.mult,
                    reads + [b_tmp, b_kgain, b_qgain], writes)

        def head_norm(src_aps, gain_ap, out_ap, n, sc, bs, ps_n, b_ps_n, tmp, b_tmp, reads, writes):
            head_norm1(src_aps, n, tmp, b_tmp, reads)
            head_norm2(src_aps, gain_ap, out_ap, n, sc, bs, ps_n, b_ps_n, tmp, b_tmp, reads, writes)

        sc1 = contextlib.ExitStack()
        cur_stack[0] = sc1
        Ks, b_Ks = sb("Ks", [128, S], BF16)
        Vs, b_Vs = sb("Vs", [128, NCHUNK * 130], BF16)
        kcT, b_kcT = sb("kcT", [128, 1024], BF16)
        vcx, b_vcx = sb("vcx", [128, 8 * 130], BF16)
        q_all, b_q = sb("q_all", [128, 16 * 512], BF16)
        TDp, b_TDp = sb("TDp", [128, 1024])
        TPp, b_TPp = sb("TPp", [128, 1024])
        b31, b_b31 = sb("b31", [128, 8])
        valid, b_valid = sb("valid", [128, 128])
        C.dma(TDp[:], c_td, writes=[b_TDp])
        C.dma(TPp[:], c_tp, writes=[b_TPp])
        C.dma(b31[:], c_b31, writes=[b_b31])
        C.dma(valid[:], c_valid, writes=[b_valid])
        for h in range(8):
            for (T_, bT) in ((TDp, b_TDp), (TPp, b_TPp)):
                ts("dve", T_[:, h * 128:(h + 1) * 128], T_[:, h * 128:(h + 1) * 128], b31[:, h:h + 1], None, ALU.subtract, None,
                   [bT, b_b31], [bT])

        ck('setup0')
        phB = contextlib.ExitStack()
        cur_stack[0] = phB
        for j in range(4):
            stage.append(sb(f"stageB{j}", [128, 512]))
        WB, b_WB = sb("WB", [128, 8 * NB], BF16)
        for k in range(8):
            for (nm, c0, w) in W_SEGS:
                o = k * NB + W_OFF[nm]
                load_w_bf16(WB[:, o:o + w], w_in[k * 128:(k + 1) * 128, c0:c0 + w], w, b_WB)
        TW0, b_TW0 = sb("TW0", [128, 512])
        C.dma(TW0[:], c_tw0, writes=[b_TW0])
        hT2, b_hT2 = sb("hT2", [128, 8 * 512], BF16)
        hT_l = [(hT, b_hT), (hT2, b_hT2)]
        tp2 = PS[5][0][:].bitcast(BF16)
        tp_l[1] = (tp2, PS[5][1])
        ust = [sb(f"ust{j}", [128, 512]) for j in range(2)]
        zt3, b_zt3 = sb("zt3", [128, 3])
        C.op("dve", lambda e: e.memset(zt3[:], 0.0), [], [b_zt3])
        for ct in range(4):
            C.dma(u_d[ct, :, 0:3], zt3[:], reads=[b_zt3], writes=[b_ud[ct]])
        nt_sq, _ = sb("nt_sq", [128, 512], BF16)
        nt_rt, _ = sb("nt_rt", [128, 512])
        b_nt = Buf()
        nt2_sq, _ = sb("nt2_sq", [128, 512], BF16)
        nt2_rt, _ = sb("nt2_rt", [128, 512])
        b_nt2 = Buf()
        cst = [sb(f"cst{g}", [128, 512], BF16) for g in range(2)]
        Kw, b_Kw = sb("Kw", [128, 1024], BF16)
        Vw, b_Vw = sb("Vw", [128, 8 * 130], BF16)
        pw = [sb(f"pw{j}", [128, 512], BF16) for j in range(2)]
        ow_t, b_ow_t = sb("ow_t", [128, 512])
        coef, b_coef = sb("coef", [128, 8])
        zt, b_zt = sb("zt", [128, 16], BF16)
        C.op("dve", lambda e: e.memset(zt[:], 0.0), [], [b_zt])
        for g in range(2):
            C.dma(kcmp_d[g, :, 16384:16400], zt[:], reads=[b_zt], writes=[b_kcmp[g]])

        def wb(k, nm, off=0, w=128):
            o = k * NB + W_OFF[nm] + off
            return WB[:, o:o + w]

        ps_a, b_ps_a = PS[0]
        ps_b, b_ps_b = PS[1]
        ps_n, b_ps_n = PS[2]
        ps_v, b_ps_v = PS[3]
        ps_s = [PS[4], PS[3]]
        ps_o, b_ps_o = PS[6]
        fm_i = [0]

        cur_hT = [hT, b_hT]

        def fm_proj(nm, off, ncol_tile=512, col0=0):
            pt, bp = (ps_a, b_ps_a) if fm_i[0] % 2 == 0 else (ps_b, b_ps_b)
            fm_i[0] += 1
            for k in range(8):
                pe_mm(pt[:, 0:ncol_tile], wb(k, nm, off), cur_hT[0][:, k * 512 + col0:k * 512 + col0 + ncol_tile], k == 0, k == 7,
                      [b_WB, cur_hT[1]], [bp])
            return pt, bp

        ck('setupB')
        for t in range(32):
            if t == 1: ck('B1')
            if t == 2: ck('B2t')
            hT_c, b_hT_c = hT_l[t % 2]
            cur_hT[0], cur_hT[1] = hT_c, b_hT_c

            def emit_norm_a(tt_):
                if tt_ == 0:
                    for ch in range(4):
                        C.dma(xring[ch][0][:], x_d[ch * 128:(ch + 1) * 128, :], writes=[xring[ch][1]])
                for ch in range(4):
                    s = 4 * tt_ + ch
                    norm_chunk_a(xring[s % 4][0], xring[s % 4][1], ch)

            def emit_norm_b(tt_):
                hTn, b_hTn = hT_l[tt_ % 2]
                for ch in range(5):
                    if ch < 4:
                        s = 4 * tt_ + ch
                        norm_chunk_b1(xring[s % 4][0], xring[s % 4][1], ch, par=s % 2)
                    if ch >= 1:
                        norm_chunk_b2((ch - 1) * 128, hTn, b_hTn, par=(4 * tt_ + ch - 1) % 2)
                if tt_ + 1 < 32:
                    for ch in range(4):
                        s2 = 4 * (tt_ + 1) + ch
                        C.dma(xring[s2 % 4][0][:], x_d[s2 * 128:(s2 + 1) * 128, :], writes=[xring[s2 % 4][1]])

            if t == 0:
                emit_norm_a(0)
                emit_norm_b(0)
            if t + 1 < 32:
                emit_norm_a(t + 1)
            for g in range(2):
                pt, bp = fm_proj("kc0" if g == 0 else "kc1", 0)
                cp("act", cst[g][0][:], pt[:], [bp], [cst[g][1]])
                C.dma(kcmp_d[g, :, t * 512:(t + 1) * 512], cst[g][0][:], reads=[cst[g][1]], writes=[b_kcmp[g]])
            pt1, bp1 = fm_proj("ks", 0)
            head_norm1([(pt1[:, :], 0, 128)], 512, (nt_sq, nt_rt), b_nt, [bp1])
            pt2, bp2 = fm_proj("kw", 0)
            head_norm1([(pt2[:, :], 0, 128)], 512, (nt2_sq, nt2_rt), b_nt2, [bp2])
            head_norm2([(pt1[:, :], 0, 128)], kgain[:, 1:2], Ks[:, t * 512:(t + 1) * 512], 512, 1.0 / 64, EPS, ps_n, b_ps_n,
                       (nt_sq, nt_rt), b_nt, [bp1], [b_Ks])
            head_norm2([(pt2[:, :], 0, 128)], kgain[:, 2:3], Kw[:, (t % 2) * 512:(t % 2) * 512 + 512], 512, 1.0 / 64, EPS, ps_n,
                       b_ps_n, (nt2_sq, nt2_rt), b_nt2, [bp2], [b_Kw])
            for ch in range(4):
                s = 4 * t + ch
                pv_, b_pv_ = (ps_v, b_ps_v) if ch % 2 == 0 else PS[4]
                for k in range(8):
                    pe_mm(pv_[:, 0:256], hT_c[:, k * 512 + ch * 128:k * 512 + ch * 128 + 128], wb(k, "vs", 0, 256), k == 0, k == 7,
                          [b_hT_c, b_WB], [b_pv_])
                rs = s % 8
                for g in range(2):
                    cp("act", Vs[:, s * 130 + g * 65:s * 130 + g * 65 + 64], pv_[:, g * 64:(g + 1) * 64], [b_pv_], [b_Vs])
                    cp("dve", Vw[:, rs * 130 + g * 65:rs * 130 + g * 65 + 64], pv_[:, 128 + g * 64:128 + (g + 1) * 64],
                       [b_pv_], [b_Vw])
                    cp("dve", Vs[:, s * 130 + g * 65 + 64:s * 130 + g * 65 + 65], valid[:, s:s + 1], [b_valid], [b_Vs])
                    cp("dve", Vw[:, rs * 130 + g * 65 + 64:rs * 130 + g * 65 + 65], valid[:, s:s + 1], [b_valid], [b_Vw])
            own = (t % 2 == 1)
            qi = (t - 1) // 2
            if own:
                qps = []
                for g in range(2):
                    pt, bp = PS[6] if g == 0 else PS[4]
                    for r in range(4):
                        h = 4 * g + r
                        off = h * 64 - 64 * g
                        for k in range(8):
                            pe_mm(pt[:, r * 128:(r + 1) * 128], wb(k, "q", off), hT_c[:, k * 512 + 384:k * 512 + 512], k == 0, k == 7,
                                  [b_WB, b_hT_c], [bp])
                    qps.append((pt, bp))
                qn = q_all[:, qi * 512:(qi + 1) * 512]
                q_src = [(qps[0][0][0:64, :], 0, 64), (qps[1][0][64:128, :], 64, 128)]
                head_norm1(q_src, 512, (nt_sq, nt_rt), b_nt, [qps[0][1], qps[1][1]])
                for k in range(8):
                    pe_mm(ps_v[:, 256:280], hT_c[:, k * 512 + 384:k * 512 + 512], wb(k, "bg", 0, 24), k == 0, k == 7, [b_hT_c, b_WB], [b_ps_v])
                act(sg_all[:, qi * 24:(qi + 1) * 24], ps_v[:, 256:280], AF.Exp, [b_ps_v], [b_sg], scale=-1.0)
                ts("dve", sg_all[:, qi * 24:(qi + 1) * 24], sg_all[:, qi * 24:(qi + 1) * 24], 1.0, None, ALU.add, None, [b_sg], [b_sg])
                C.op("dve", lambda e: e.reciprocal(out=sg_all[:, qi * 24:(qi + 1) * 24], in_=sg_all[:, qi * 24:(qi + 1) * 24]), [b_sg], [b_sg])
            for ct in range(4):
                pt, bp = fm_proj("u", ct * 128)
                us_, b_us = ust[ct % 2]
                cp("act" if ct % 2 else "dve", us_[:], pt[:], [bp], [b_us])
                C.dma(u_d[ct, :, 3 + t * 512:3 + (t + 1) * 512], us_[:], reads=[b_us], writes=[b_ud[ct]])
            if own:
                head_norm2(q_src, qgain[:, 0:1], qn, 512, 1.0, 64 * EPS, ps_n, b_ps_n, (nt_sq, nt_rt), b_nt,
                           [qps[0][1], qps[1][1]], [b_q])
            if t + 1 < 32:
                emit_norm_b(t + 1)
            if not own:
                continue
            items = [(g, w) for g in range(2) for w in range(5)]
            po_l = [PS[6], PS[0]]

            def win_qk(idx):
                g, w = items[idx]
                lo, hi = g * 64, (g + 1) * 64
                rc = 3 + w
                pst, bps = ps_s[idx % 2]
                tab = {0: (TW0, b_TW0, 0), 3: (TPp, b_TPp, g * 512), 4: (TDp, b_TDp, g * 512)}.get(w)
                pe_mm(pst[:], Kw[lo:hi, rc * 128:(rc + 1) * 128], qn[lo:hi, :], True, tab is None, [b_Kw, b_q], [bps])
                if tab is not None:
                    pe_mm(pst[:], identf[:], tab[0][:, tab[2]:tab[2] + 512], False, True, [b_identf, tab[1]], [bps])

            win_qk(0)
            for idx in range(10):
                g, w = items[idx]
                rc = 3 + w
                if idx + 1 < 10:
                    win_qk(idx + 1)
                pst, bps = ps_s[idx % 2]
                pt_, bpt = pw[idx % 2]
                po, b_po = po_l[g]
                act(pt_[:], pst[:], AF.Exp, [bps], [bpt])
                for r in range(4):
                    pe_mm(po[:, r * 65:(r + 1) * 65], pt_[:, r * 128:(r + 1) * 128],
                          Vw[:, rc * 130 + g * 65:rc * 130 + g * 65 + 65], w == 0 and r == 0, w == 4, [bpt, b_Vw], [b_po])
                if w == 4:
                    for r in range(4):
                        C.op("dve", lambda e: e.reciprocal(out=coef[:, r:r + 1], in_=po[:, r * 65 + 64:r * 65 + 65]), [b_po], [b_coef])
                    tt("dve", coef[:, 4:8], coef[:, 0:4], sg_all[:, qi * 24 + 16 + 4 * g:qi * 24 + 20 + 4 * g], ALU.mult,
                       [b_coef, b_sg], [b_coef])
                    for r in range(4):
                        h = 4 * g + r
                        ts("dve", ow_t[:, h * 64:(h + 1) * 64], po[:, r * 65:r * 65 + 64], coef[:, 4 + r:5 + r], None, ALU.mult, None,
                           [b_po, b_coef], [b_ow_t])
            C.dma(ow_d[qi], ow_t[:], reads=[b_ow_t], writes=[b_ow[qi]])

        C.barrier()
        del stage[2:]
        phB.close()

        ck('B')
        phL = contextlib.ExitStack()
        cur_stack[0] = phL
        lmask, b_lmask = sb("lmask", [128, 1024], BF16)
        C.dma(lmask[:], c_lmask, writes=[b_lmask])
        cw, b_cw = sb("cw", [128, 16])
        lvec, b_lvec = sb("lvec", [128, 48])
        for ct in range(4):
            C.dma(cw[:, ct * 4:(ct + 1) * 4], conv_w[:, ct * 128:(ct + 1) * 128].rearrange("k p -> p k"), writes=[b_cw],
                  allow_slow_non_contiguous=True)
        for j, v in enumerate((conv_b, lru_ba, lru_bx, lru_lambda)):
            C.dma(lvec[:, j * 4:(j + 1) * 4], v.rearrange("(c p) -> p c", p=128), writes=[b_lvec], allow_slow_non_contiguous=True)
        act(lvec[:, 16:20], lvec[:, 12:16], AF.Exp, [b_lvec], [b_lvec], scale=-1.0)
        act(lvec[:, 20:24], lvec[:, 16:20], AF.Ln, [b_lvec], [b_lvec], bias=1.0)
        ts("dve", lvec[:, 24:28], lvec[:, 20:24], -4.0, None, ALU.mult, None, [b_lvec], [b_lvec])
        ts("dve", lvec[:, 28:32], lvec[:, 20:24], -8.0, None, ALU.mult, None, [b_lvec], [b_lvec])
        ts("dve", lvec[:, 32:40], lvec[:, 4:12], 0.5, None, ALU.mult, None, [b_lvec], [b_lvec])
        Wab, b_Wab = sb("Wab", [128, 512], BF16)
        Wxb, b_Wxb = sb("Wxb", [128, 512], BF16)
        for (Wd, bW, src) in ((Wab, b_Wab, lru_wa), (Wxb, b_Wxb, lru_wx)):
            st, b_st = stage[stage_i[0] % len(stage)]
            stage_i[0] += 1
            C.op("dve", lambda e: e.memset(st[:, 0:512], 0.0), [], [b_st])
            C.dma(st[0:64, 0:512].rearrange("d (c e) -> d c e", c=4)[:, :, 0:64],
                  src[0:8:2].rearrange("n d e -> d n e"), writes=[b_st])
            C.dma(st[64:128, 0:512].rearrange("d (c e) -> d c e", c=4)[:, :, 64:128],
                  src[1:8:2].rearrange("n d e -> d n e"), writes=[b_st])
            cp("dve", Wd[:], st[:, 0:512], [b_st], [bW])
        LT = 512
        NR = 4
        ubL = [sb(f"ubL{j}", [128, LT + 3]) for j in range(NR)]
        ucL_ = [sb(f"ucL{j}", [128, LT]) for j in range(NR)]
        ucbL_ = [sb(f"ucbL{j}", [128, LT], BF16) for j in range(NR)]
        trL_ = [sb(f"trL{j}", [128, LT]) for j in range(NR)]
        tiL_ = [sb(f"tiL{j}", [128, LT]) for j in range(NR)]
        aL_ = [sb(f"aL{j}", [128, LT]) for j in range(NR)]
        qL_ = [sb(f"qL{j}", [128, LT]) for j in range(NR)]
        hL = [sb(f"hL{j}", [128, LT]) for j in range(NR)]
        hst = [sb(f"hstate{j}", [128, 1]) for j in range(4)]
        for j in range(4):
            C.op("dve", lambda e: e.memset(hst[j][0][:], 0.0), [], [hst[j][1]])
        pg = [PS[0], PS[1], PS[2], PS[3]]
        NIT = 4 * (S // LT)

        def l_load(it):
            ct, T2 = it % 4, it // 4
            C.dma(ubL[it % NR][0][:], u_d[ct, :, T2 * LT:T2 * LT + LT + 3], reads=[b_ud[ct]], writes=[ubL[it % NR][1]])

        def l_stageA(p):
            for it in (2 * p, 2 * p + 1):
                ct, T2 = it % 4, it // 4
                rr = it % NR
                ub, b_ub = ubL[rr]
                ucL, b_ucL = ucL_[rr]
                ucbL, b_ucbL = ucbL_[rr]
                trL, b_trL = trL_[rr]
                tiL, b_tiL = tiL_[rr]
                aL, b_aL = aL_[rr]
                qL, b_qL = qL_[rr]
                pa, bpa = pg[(it % 2) * 2]
                pb, bpb = pg[(it % 2) * 2 + 1]
                ts("dve", ucL[:], ub[:, 0:LT], cw[:, ct * 4:ct * 4 + 1], lvec[:, ct:ct + 1], ALU.mult, ALU.add,
                   [b_ub, b_cw, b_lvec], [b_ucL])
                for k in range(1, 4):
                    stt(ucL[:], ub[:, k:k + LT], cw[:, ct * 4 + k:ct * 4 + k + 1], ucL[:], ALU.mult, ALU.add, [b_ub, b_cw, b_ucL], [b_ucL])
                cp("act", ucbL[:], ucL[:], [b_ucL], [b_ucbL])
                pe_mm(pa[:], Wab[:, ct * 128:(ct + 1) * 128], ucbL[:], True, True, [b_Wab, b_ucbL], [bpa])
                pe_mm(pb[:], Wxb[:, ct * 128:(ct + 1) * 128], ucbL[:], True, True, [b_Wxb, b_ucbL], [bpb])
                act(trL[:], pa[:], AF.Tanh, [bpa, b_lvec], [b_trL], scale=0.5, bias=lvec[:, 32 + ct:33 + ct])
                act(tiL[:], pb[:], AF.Tanh, [bpb, b_lvec], [b_tiL], scale=0.5, bias=lvec[:, 36 + ct:37 + ct])
                act(aL[:], trL[:], AF.Exp, [b_trL, b_lvec], [b_aL], scale=lvec[:, 24 + ct:25 + ct], bias=lvec[:, 24 + ct:25 + ct])
                act(qL[:], trL[:], AF.Exp, [b_trL, b_lvec], [b_qL], scale=lvec[:, 28 + ct:29 + ct], bias=lvec[:, 28 + ct:29 + ct])
            for it in (2 * p, 2 * p + 1):
                qL, b_qL = qL_[it % NR]
                act(qL[:], qL[:], AF.Sqrt, [b_qL], [b_qL], scale=-1.0, bias=1.0)

        def l_stageB(p):
            for it in (2 * p, 2 * p + 1):
                ct, T2 = it % 4, it // 4
                rr = it % NR
                ho, b_ho = hL[rr]
                ucL, b_ucL = ucL_[rr]
                tiL, b_tiL = tiL_[rr]
                aL, b_aL = aL_[rr]
                qL, b_qL = qL_[rr]
                stt(tiL[:], tiL[:], 1.0, ucL[:], ALU.add, ALU.mult, [b_tiL, b_ucL], [b_tiL])
                stt(qL[:], qL[:], 0.5, tiL[:], ALU.mult, ALU.mult, [b_qL, b_tiL], [b_qL])
                if T2 < 2:
                    tt("dve", qL[:], qL[:], lmask[:, T2 * 512:(T2 + 1) * 512], ALU.mult, [b_qL, b_lmask], [b_qL])
                C.op("dve", lambda e: e.tensor_tensor_scan(out=ho[:], data0=aL[:], data1=qL[:], initial=hst[ct][0][:, 0:1],
                                                            op0=ALU.mult, op1=ALU.add), [b_aL, b_qL, hst[ct][1]], [b_ho])
                cp("act", hst[ct][0][:, 0:1], ho[:, LT - 1:LT], [b_ho], [hst[ct][1]])
                if T2 % 2 == 1:
                    qi_ = (T2 - 1) // 2
                    C.dma(hl_d[qi_, ct], ho[:, 384:512], reads=[b_ho], writes=[b_hl[qi_]])

        NP = NIT // 2
        for it in range(4):
            l_load(it)
        l_stageA(0)
        for p in range(NP):
            if p + 1 < NP:
                l_stageA(p + 1)
            if p + 2 < NP:
                l_load(2 * (p + 2))
                l_load(2 * (p + 2) + 1)
            l_stageB(p)
        C.barrier()
        phL.close()
        ck('L')
        phB2 = contextlib.ExitStack()
        cur_stack[0] = phB2
        W1b, b_W1b = sb("W1b", [128, 32 * 256], BF16)
        for p0 in range(0, 32, 2):
            st, b_st = stage[stage_i[0] % len(stage)]
            stage_i[0] += 1
            for kv in range(2):
                C.dma(st[kv * 64:(kv + 1) * 64, 0:512].rearrange("d (p h) -> d p h", p=2),
                      cmp_w1[kv, p0 * 64:(p0 + 2) * 64, :].rearrange("(p d) h -> d p h", d=64), writes=[b_st])
            cp("act" if (p0 // 2) % 2 else "dve", W1b[:, p0 * 256:(p0 + 2) * 256], st[:, 0:512], [b_st], [b_W1b])
        W2kp, b_W2kp = sb("W2kp", [128, 384], BF16)
        W2v, b_W2v = sb("W2v", [128, 128], BF16)
        st, b_st = stage[stage_i[0] % len(stage)]
        stage_i[0] += 1
        C.op("dve", lambda e: e.memset(st[:, 0:384], 0.0), [], [b_st])
        for hh in range(2):
            C.dma(st[:, hh * 192 + 64:hh * 192 + 128], cmp_w2[0, hh * 128:(hh + 1) * 128, :], writes=[b_st])
        cp("dve", W2kp[:], st[:, 0:384], [b_st], [b_W2kp])
        st, b_st = stage[stage_i[0] % len(stage)]
        stage_i[0] += 1
        for hh in range(2):
            C.dma(st[:, hh * 64:(hh + 1) * 64], cmp_w2[1, hh * 128:(hh + 1) * 128, :], writes=[b_st])
        cp("dve", W2v[:], st[:, 0:128], [b_st], [b_W2v])
        peT, b_peT = sb("peT", [128, 32], BF16)
        st, b_st = stage[stage_i[0] % len(stage)]
        stage_i[0] += 1
        for kv in range(2):
            C.dma(st[kv * 64:(kv + 1) * 64, 0:32], cmp_pe[kv].rearrange("p d -> d p"), writes=[b_st], allow_slow_non_contiguous=True)
        cp("dve", peT[:], st[:, 0:32], [b_st], [b_peT])
        b1t, b_b1t = sb("b1t", [128, 4])
        for kv in range(2):
            for hh in range(2):
                C.dma(b1t[:, kv * 2 + hh:kv * 2 + hh + 1], cmp_b1[kv, hh * 128:(hh + 1) * 128].rearrange("(h o) -> h o", o=1),
                      writes=[b_b1t], allow_slow_non_contiguous=True)
        cvec, b_cvec = sb("cvec", [128, 4])
        ps_a, b_ps_a = PS[0]
        ps_b, b_ps_b = PS[1]
        ps_n, b_ps_n = PS[2]
        ps_v, b_ps_v = PS[3]
        ps_o, b_ps_o = PS[6]
        for kv in range(2):
            lo, hi = kv * 64, (kv + 1) * 64
            for hh in range(2):
                j = kv * 2 + hh
                pcv, bpcv = (ps_n, b_ps_n) if kv == 0 else (ps_v, b_ps_v)
                for p in range(32):
                    pe_mm(pcv[:, hh:hh + 1], W1b[lo:hi, p * 256 + hh * 128:p * 256 + hh * 128 + 128], peT[lo:hi, p:p + 1], p == 0, p == 31,
                          [b_W1b, b_peT], [bpcv])
        tt("dve", cvec[:, 0:2], ps_n[:, 0:2], b1t[:, 0:2], ALU.add, [b_ps_n, b_b1t], [b_cvec])
        tt("dve", cvec[:, 2:4], ps_v[:, 0:2], b1t[:, 2:4], ALU.add, [b_ps_v, b_b1t], [b_cvec])
        cvalid, b_cvalid = sb("cvalid", [128, 8])
        C.dma(cvalid[:], c_cvalid, writes=[b_cvalid])
        cin = [sb(f"cin{g}", [128, 2064], BF16) for g in range(2)]
        hact, b_hact = sb("hact", [128, 1024], BF16)
        n2_sq, _ = sb("n2_sq", [128, 128], BF16)
        n2_rt, _ = sb("n2_rt", [128, 128])
        b_n2 = Buf()
        for Cc in range(8):
            for g in range(2):
                C.dma(cin[g][0][:], kcmp_d[g, :, Cc * 2048:Cc * 2048 + 2064], reads=[b_kcmp[g]], writes=[cin[g][1]])
            for kv in range(2):
                lo, hi = kv * 64, (kv + 1) * 64
                psh, bph = (ps_a, b_ps_a) if kv == 0 else (ps_b, b_ps_b)
                for g in range(2):
                    for hh in range(2):
                        o = (g * 2 + hh) * 128
                        for p in range(32):
                            pe_mm(psh[:, o:o + 128], W1b[lo:hi, p * 256 + hh * 128:p * 256 + hh * 128 + 128],
                                  cin[g][0][lo:hi, p:p + 2033:16], p == 0, p == 31, [b_W1b, cin[g][1]], [bph])
                for g in range(2):
                    for hh in range(2):
                        o = (g * 2 + hh) * 128
                        ho = ((kv * 2 + g) * 2 + hh) * 128
                        act(hact[:, ho:ho + 128], psh[:, o:o + 128], AF.Silu, [bph, b_cvec], [b_hact],
                            bias=cvec[:, kv * 2 + hh:kv * 2 + hh + 1])
            n_ = 0
            for g in range(2):
                for hh in range(2):
                    ho = ((0 * 2 + g) * 2 + hh) * 128
                    c0 = hh * 192 + (64 if g == 0 else 0)
                    pe_mm(ps_v[:, 0:128], W2kp[:, c0:c0 + 128], hact[:, ho:ho + 128], n_ == 0, n_ == 3, [b_W2kp, b_hact], [b_ps_v])
                    n_ += 1
            head_norm([(ps_v[:, 0:128], 0, 128)], kgain[:, 0:1], kcT[:, Cc * 128:(Cc + 1) * 128], 128, 1.0 / 64, EPS, ps_n, b_ps_n,
                      (n2_sq, n2_rt), b_n2, [b_ps_v], [b_kcT])
            for g in range(2):
                for hh in range(2):
                    ho = ((1 * 2 + g) * 2 + hh) * 128
                    pe_mm(ps_o[:, g * 64:(g + 1) * 64], hact[:, ho:ho + 128], W2v[:, hh * 64:(hh + 1) * 64], hh == 0, hh == 1,
                          [b_hact, b_W2v], [b_ps_o])
            for g in range(2):
                o = Cc * 130 + g * 65
                ts("dve", vcx[:, o:o + 64], ps_o[:, g * 64:(g + 1) * 64], cvalid[:, Cc:Cc + 1], None, ALU.mult, None,
                   [b_ps_o, b_cvalid], [b_vcx])
                cp("dve", vcx[:, o + 64:o + 65], cvalid[:, Cc:Cc + 1], [b_cvalid], [b_vcx])
        if os.environ.get("KDEBUG", ""):
            dbg_kc = nc.dram_tensor("dbg_kc", [128, 1024], BF16, kind="ExternalOutput").ap()
            dbg_vc = nc.dram_tensor("dbg_vc", [128, 1040], BF16, kind="ExternalOutput").ap()
            C.dma(dbg_kc, kcT[:], reads=[b_kcT], writes=[Buf()])
            C.dma(dbg_vc, vcx[:], reads=[b_vcx], writes=[Buf()])
        C.barrier()
        phB2.close()

        ck('Bm')
        phC = contextlib.ExitStack()
        cur_stack[0] = phC
        GCp, b_GCp = sb("GCp", [128, 1024])
        BigSh, b_BigSh = sb("BigSh", [128, 384])
        KX1, b_KX1 = sb("KX1", [128, S], BF16)
        Pool_, b_Pool = sb("Pool_", [128, 2048], BF16)
        wrel, b_wrel = sb("wrel", [128, 512])
        wcore, b_wcore = sb("wcore", [128, 256])
        C.dma(GCp[:], c_gc, writes=[b_GCp])
        C.dma(BigSh[:], c_bigsh, writes=[b_BigSh])
        for j in range(4):
            cp(("act", "dve", "act", "dve")[j], KX1[64:128, j * 4096:(j + 1) * 4096], Ks[64:128, j * 4096:(j + 1) * 4096], [b_Ks], [b_KX1])
        for j in range(4):
            C.dma(KX1[0:64, j * 4096:(j + 1) * 4096], c_ind, writes=[b_KX1])
            C.dma(Ks[64:128, j * 4096:(j + 1) * 4096], c_ind, writes=[b_Ks])
        KX = [Ks, KX1]
        b_KX = [b_Ks, b_KX1]
        C.dma(Pool_[:], c_pool, writes=[b_Pool])
        C.dma(wrel[:], c_wrel, writes=[b_wrel])
        C.dma(wcore[:], c_wcore, writes=[b_wcore])
        for h in range(8):
            ts("dve", GCp[:, h * 128:(h + 1) * 128], GCp[:, h * 128:(h + 1) * 128], b31[:, h:h + 1], None, ALU.subtract, None,
               [b_GCp, b_b31], [b_GCp])
        pcs, b_pcs = sb("pcs", [128, 8 * 512], BF16)
        pring = [sb(f"pring{j}", [128, 512], BF16) for j in range(3)]
        NegMp, b_NegM = sb("NegMp", [128, 320], BF16)
        C.op("dve", lambda e: e.memset(NegMp[:, 0:64], 0.0), [], [b_NegM])
        NegM = NegMp[:, 64:320]
        mr1, b_mr1 = sb("mr1", [128, 128], BF16)
        qx = [sb(f"qx{g}", [128, 4 * 512], BF16) for g in range(2)]
        imp, b_imp = sb("imp", [128, 256])
        score, b_score = sb("score", [128, 256])
        sc2, b_sc2 = sb("sc2", [128, 256])
        mf, b_mf = sb("mf", [128, 256])
        m8, b_m8 = sb("m8", [128, 16])
        thr, b_thr = sb("thr", [128, 1])
        oacc_l = [sb(f"oacc{j}", [128, 512]) for j in range(2)]
        coefc, b_coefc = sb("coefc", [128, 8])
        rden, b_rden = sb("rden", [128, 4])
        coef2, b_coef2 = sb("coef2", [128, 8])
        rden2, b_rden2 = sb("rden2", [128, 4])
        ps_c = [PS[3], PS[4]]
        ps_s3 = [PS[0], PS[1], PS[6]]
        ps_oc, b_ps_oc = PS[2]
        psu = [PS[3], PS[4]]
        ps_os, b_ps_os = PS[5]

        def geom(n):
            i, g = n // 2, n % 2
            M = 8 * i + 7
            return i, g, M, i // 2 + 1, g * 64, (g + 1) * 64

        def prep_load(n):
            i, g, M, nC, lo, hi = geom(n)
            if g == 0:
                oacc, b_oacc = oacc_l[i % 2]
                C.dma(oacc[:], ow_d[i], reads=[b_ow[i]], writes=[b_oacc])
            qn = q_all[:, i * 512:(i + 1) * 512]
            qxt, b_qxt = qx[g]
            for s_ in range(M // 32 + 1):
                cp("dve" if s_ % 2 else "act", qxt[lo:hi, s_ * 512:(s_ + 1) * 512], qn[lo:hi, :], [b_q], [b_qxt])

        def prep_qk(n, Cc):
            i, g, M, nC, lo, hi = geom(n)
            qn = q_all[:, i * 512:(i + 1) * 512]
            pst, bps = ps_c[Cc % 2]
            delta = 128 * Cc - 8 * M + 64
            near = delta > -128
            pe_mm(pst[:], kcT[lo:hi, Cc * 128:(Cc + 1) * 128], qn[lo:hi, :], True, not near, [b_kcT, b_q], [bps])
            if near:
                pe_mm(pst[:], BigSh[:, 128 + delta:256 + delta], GCp[:, g * 512:(g + 1) * 512], False, True,
                      [b_BigSh, b_GCp], [bps])
            act(pcs[:, Cc * 512:(Cc + 1) * 512], pst[:], AF.Exp, [bps], [b_pcs])

        def prep_pv(n, Cc):
            i, g, M, nC, lo, hi = geom(n)
            for r in range(4):
                pe_mm(ps_oc[:, r * 65:(r + 1) * 65], pcs[:, Cc * 512 + r * 128:Cc * 512 + (r + 1) * 128],
                      vcx[:, Cc * 130 + g * 65:Cc * 130 + g * 65 + 65], Cc == 0 and r == 0, Cc == nC - 1, [b_pcs, b_vcx], [b_ps_oc])

        def prep_imp(n):
            i, g, M, nC, lo, hi = geom(n)
            oacc, b_oacc = oacc_l[i % 2]
            for r in range(4):
                pu, bpu = psu[r // 2]
                for Cc in range(nC):
                    pe_mm(pu[:, (r % 2) * 256:(r % 2) * 256 + 256], pcs[:, Cc * 512 + r * 128:Cc * 512 + (r + 1) * 128],
                          Pool_[:, Cc * 256:(Cc + 1) * 256], Cc == 0, Cc == nC - 1, [b_pcs, b_Pool], [bpu])
            for r in range(4):
                ts("dve", rden[:, r:r + 1], ps_oc[:, r * 65 + 64:r * 65 + 65], 1e-30, None, ALU.add, None, [b_ps_oc], [b_rden])
                C.op("dve", lambda e: e.reciprocal(out=rden[:, r:r + 1], in_=rden[:, r:r + 1]), [b_rden], [b_rden])
            ts("dve", imp[:], psu[0][0][:, 0:256], rden[:, 0:1], None, ALU.mult, None, [psu[0][1], b_rden], [b_imp])
            for r in range(1, 4):
                pu, bpu = psu[r // 2]
                stt(imp[:], pu[:, (r % 2) * 256:(r % 2) * 256 + 256], rden[:, r:r + 1], imp[:], ALU.mult, ALU.add,
                    [bpu, b_rden, b_imp], [b_imp])
            tt("dve", coefc[:, 0:4], rden[:], sg_all[:, i * 24 + 4 * g:i * 24 + 4 * g + 4], ALU.mult, [b_rden, b_sg], [b_coefc])
            for r in range(4):
                h = 4 * g + r
                stt(oacc[:, h * 64:(h + 1) * 64], ps_oc[:, r * 65:r * 65 + 64], coefc[:, r:r + 1], oacc[:, h * 64:(h + 1) * 64],
                    ALU.mult, ALU.add, [b_ps_oc, b_coefc, b_oacc], [b_oacc])
            tt("dve", score[:], imp[:], wrel[:, 256 - 2 * M:512 - 2 * M], ALU.add, [b_imp, b_wrel], [b_score])
            tt("dve", score[:], score[:], wcore[:], ALU.add, [b_score, b_wcore], [b_score])
            C.op("dve", lambda e: e.max(out=m8[:, 0:8], in_=score[:]), [b_score], [b_m8])
            C.op("dve", lambda e: e.match_replace(out=sc2[:], in_to_replace=m8[:, 0:8], in_values=score[:], imm_value=-1e30),
                 [b_score, b_m8], [b_sc2])
            C.op("dve", lambda e: e.max(out=m8[:, 8:16], in_=sc2[:]), [b_sc2], [b_m8])
            ts("dve", thr[:], m8[:, 15:16], -0.5, None, ALU.max, None, [b_m8], [b_thr])
            ts("dve", mf[:], score[:], thr[:, 0:1], None, ALU.is_ge, None, [b_score, b_thr], [b_mf])
            ts("dve", NegM, mf[:], -1.0, 30000.0, ALU.add, ALU.mult, [b_mf], [b_NegM])

        b_tpm = [Buf(), Buf()]

        def prep_mask(n):
            i, g, M, nC, lo, hi = geom(n)
            qxt, b_qxt = qx[g]
            mlo, mhi = (64, 128) if g == 0 else (0, 64)
            for s_ in range(M // 32 + 1):
                c0 = 0
                if g == 1:
                    C.op("pe", lambda e: e.transpose(out=ps_tp[0:64, c0:c0 + 128], in_=NegMp[:, 64 + 64 * s_:128 + 64 * s_],
                                                     identity=identb[:]), [b_NegM, b_identb], [b_ps_tp])
                else:
                    C.op("pe", lambda e: e.transpose(out=ps_tp[:, c0:c0 + 128], in_=NegMp[:, 64 * s_:64 * s_ + 128],
                                                     identity=identb[:]), [b_NegM, b_identb], [b_ps_tp])
                for r in range(4):
                    cp("act" if r < 2 else "dve", qxt[mlo:mhi, s_ * 512 + r * 128:s_ * 512 + (r + 1) * 128],
                       ps_tp[mlo:mhi, c0:c0 + 128], [b_ps_tp], [b_qxt])

        def slc_qk(n, kc):
            i, g, M, nC, lo, hi = geom(n)
            s_ = kc // 32
            pst, bps = ps_s3[kc % 3]
            near = kc >= M - 1
            pe_mm(pst[:], KX[g][:, kc * 128:(kc + 1) * 128], qx[g][0][:, s_ * 512:(s_ + 1) * 512], True, not near,
                  [b_KX[g], qx[g][1]], [bps])
            if kc == M:
                pe_mm(pst[:], identf[:], TDp[:, g * 512:(g + 1) * 512], False, True, [b_identf, b_TDp], [bps])
            elif kc == M - 1:
                pe_mm(pst[:], identf[:], TPp[:, g * 512:(g + 1) * 512], False, True, [b_identf, b_TPp], [bps])

        def slc_final(n):
            i, g, M, nC, lo, hi = geom(n)
            oacc, b_oacc = oacc_l[i % 2]
            for r in range(4):
                C.op("dve", lambda e: e.reciprocal(out=rden2[:, r:r + 1], in_=ps_os[:, r * 65 + 64:r * 65 + 65]), [b_ps_os], [b_rden2])
            tt("dve", coef2[:, 4:8], rden2[:], sg_all[:, i * 24 + 8 + 4 * g:i * 24 + 12 + 4 * g], ALU.mult, [b_rden2, b_sg], [b_coef2])
            for r in range(4):
                h = 4 * g + r
                stt(oacc[:, h * 64:(h + 1) * 64], ps_os[:, r * 65:r * 65 + 64], coef2[:, 4 + r:5 + r], oacc[:, h * 64:(h + 1) * 64],
                    ALU.mult, ALU.add, [b_ps_os, b_coef2, b_oacc], [b_oacc])
            if g == 1:
                C.dma(oa_d[i], oacc[:], reads=[b_oacc], writes=[b_oa[i]])

        def prep_all(n):
            i, g, M, nC, lo, hi = geom(n)
            prep_load(n)
            for Cc in range(nC):
                prep_qk(n, Cc)
                prep_pv(n, Cc)
            prep_imp(n)
            prep_mask(n)

        prep_all(0)
        for n in range(32):
            i, g, M, nC, lo, hi = geom(n)
            if n == 2: ck('C1')
            nxt = n + 1 if n + 1 < 32 else None
            nCn = geom(nxt)[3] if nxt is not None else 0
            slc_qk(n, 0)
            if M >= 1:
                slc_qk(n, 1)
            if nxt is not None:
                prep_load(nxt)
            for kc in range(M + 1):
                if kc + 2 <= M:
                    slc_qk(n, kc + 2)
                if nxt is not None:
                    if kc < nCn:
                        prep_qk(nxt, kc)
                    if 1 <= kc <= nCn:
                        prep_pv(nxt, kc - 1)
                    if kc == nCn + 1:
                        prep_imp(nxt)
                pst, bps = ps_s3[kc % 3]
                pt_, bpt = pring[kc % 3]
                act(pt_[:], pst[:], AF.Exp, [bps], [bpt])
                for r in range(4):
                    pe_mm(ps_os[:, r * 65:(r + 1) * 65], pt_[:, r * 128:(r + 1) * 128],
                          Vs[:, kc * 130 + g * 65:kc * 130 + g * 65 + 65], kc == 0 and r == 0, kc == M, [bpt, b_Vs], [b_ps_os])
            if nxt is not None:
                if nCn + 1 > M:
                    prep_imp(nxt)
                prep_mask(nxt)
            slc_final(n)
        C.barrier()
        phC.close()
        sc1.close()

        ck('C')
        phD = contextlib.ExitStack()
        cur_stack[0] = phD
        for j in range(6):
            stage.append(sb(f"stageD{j}", [128, 512]))
        Wg, b_Wg = sb("Wg", [128, 8 * 512], BF16)
        Wm, b_Wm = sb("Wm", [128, 8 * 2048], BF16)
        Wpa, b_Wpa = sb("Wpa", [128, 4 * 1024], BF16)
        Wpb, b_Wpb = sb("Wpb", [128, 4 * 1024], BF16)
        Wo, b_Wo = sb("Wo", [128, 8 * 1024], BF16)
        Wgl, b_Wgl = sb("Wgl", [128, 8 * 512], BF16)
        hraw, b_hraw = sb("hraw", [128, 512])
        for k in range(8):
            load_w_bf16(Wg[:, k * 512:(k + 1) * 512], w_in[k * 128:(k + 1) * 128, 1280:1792], 512, b_Wg)
            load_w_bf16(Wgl[:, k * 512:(k + 1) * 512], w_in[k * 128:(k + 1) * 128, 2328:2840], 512, b_Wgl)
            for pc in range(4):
                load_w_bf16(Wm[:, k * 2048 + pc * 512:k * 2048 + (pc + 1) * 512],
                            w_in[k * 128:(k + 1) * 128, 2840 + pc * 512:2840 + (pc + 1) * 512], 512, b_Wm)
            for pc in range(2):
                load_w_bf16(Wo[:, k * 1024 + pc * 512:k * 1024 + (pc + 1) * 512], w_out[k * 128:(k + 1) * 128, pc * 512:(pc + 1) * 512],
                            512, b_Wo)
        for k in range(4):
            for pc in range(2):
                load_w_bf16(Wpa[:, k * 1024 + pc * 512:k * 1024 + (pc + 1) * 512],
                            w_proj_a[k * 128:(k + 1) * 128, pc * 512:(pc + 1) * 512], 512, b_Wpa)
                load_w_bf16(Wpb[:, k * 1024 + pc * 512:k * 1024 + (pc + 1) * 512],
                            w_proj_b[k * 128:(k + 1) * 128, pc * 512:(pc + 1) * 512], 512, b_Wpb)
        xd = [sb(f"xd{j}", [128, 1024]) for j in range(4)]
        oa_t, b_oa_t = sb("oa_t", [128, 512])
        sgn, b_sgn = sb("sgn", [128, 512])
        ya, b_ya = sb("ya", [128, 512], BF16)
        yaT, b_yaT = sb("yaT", [128, 4 * 512], BF16)
        hlT, b_hlT = sb("hlT", [128, 4 * 512], BF16)
        mT, b_mT = sb("mT", [128, 8 * 512], BF16)
        sga, b_sga = sb("sga", [128, 512])
        sgb, b_sgb = sb("sgb", [128, 512])
        m1, b_m1 = sb("m1", [128, 512])
        ot = [sb(f"ot{j}", [128, 512]) for j in range(2)]
        pA, b_pA = PS[1]
        pGA, b_pGA = PS[2]
        pB, b_pB = PS[3]
        pGB, b_pGB = PS[4]
        for grp in range(4):
            for j in range(4):
                i = grp * 4 + j
                slot = 8 * i + 7
                C.dma(xd[j][0][:], x_d[slot * 128:(slot + 1) * 128, :], writes=[xd[j][1]])
                norm_chunk_a(xd[j][0], xd[j][1], j)
            for j in range(4):
                norm_chunk_b(xd[j][0], xd[j][1], j * 128, j, None, None, par=0)
            for j in range(4):
                i = grp * 4 + j
                for k in range(8):
                    pe_mm(PS[0][0][:], hT[:, k * 512 + j * 128:k * 512 + (j + 1) * 128], Wg[:, k * 512:(k + 1) * 512], k == 0, k == 7,
                          [b_hT, b_Wg], [PS[0][1]])
                act(sgn[:], PS[0][0][:], AF.Silu, [PS[0][1]], [b_sgn])
                C.dma(oa_t[:], oa_d[i], reads=[b_oa[i]], writes=[b_oa_t])
                tt("dve", ya[:], sgn[:], oa_t[:], ALU.mult, [b_sgn, b_oa_t], [b_ya])
                for kc in range(4):
                    C.op("pe", lambda e: e.transpose(out=ps_tp[:, kc * 128:(kc + 1) * 128], in_=ya[:, kc * 128:(kc + 1) * 128],
                                                     identity=identb[:]), [b_ya, b_identb], [b_ps_tp])
                for kc in range(4):
                    cp("act" if kc % 2 else "dve", yaT[:, kc * 512 + j * 128:kc * 512 + (j + 1) * 128], ps_tp[:, kc * 128:(kc + 1) * 128],
                       [b_ps_tp], [b_yaT])
            for ct in range(4):
                for k in range(8):
                    pe_mm(PS[0][0][:], Wgl[:, k * 512 + ct * 128:k * 512 + (ct + 1) * 128], hT[:, k * 512:(k + 1) * 512], k == 0, k == 7,
                          [b_Wgl, b_hT], [PS[0][1]])
                act(sgn[:], PS[0][0][:], AF.Silu, [PS[0][1]], [b_sgn])
                C.dma(hraw[:, 0:512].rearrange("p (j c) -> p j c", j=4),
                      hl_d[grp * 4:(grp + 1) * 4, ct].rearrange("j p c -> p j c"),
                      reads=[b_hl[grp * 4 + j] for j in range(4)], writes=[b_hraw])
                tt("dve", hlT[:, ct * 512:(ct + 1) * 512], sgn[:], hraw[:], ALU.mult, [b_sgn, b_hraw], [b_hlT])
            for f in range(8):
                for k in range(8):
                    pe_mm(pGA[:], Wm[:, k * 2048 + f * 128:k * 2048 + (f + 1) * 128], hT[:, k * 512:(k + 1) * 512], k == 0, k == 7,
                          [b_Wm, b_hT], [b_pGA])
                for k in range(8):
                    pe_mm(pGB[:], Wm[:, k * 2048 + 1024 + f * 128:k * 2048 + 1024 + (f + 1) * 128], hT[:, k * 512:(k + 1) * 512],
                          k == 0, k == 7, [b_Wm, b_hT], [b_pGB])
                for kc in range(4):
                    pe_mm(pA[:], Wpa[:, kc * 1024 + f * 128:kc * 1024 + (f + 1) * 128], yaT[:, kc * 512:(kc + 1) * 512], kc == 0, kc == 3,
                          [b_Wpa, b_yaT], [b_pA])
                for kc in range(4):
                    pe_mm(pB[:], Wpb[:, kc * 1024 + f * 128:kc * 1024 + (f + 1) * 128], hlT[:, kc * 512:(kc + 1) * 512], kc == 0, kc == 3,
                          [b_Wpb, b_hlT], [b_pB])
                act(sga[:], pGA[:], AF.Sigmoid, [b_pGA], [b_sga])
                act(sgb[:], pGB[:], AF.Sigmoid, [b_pGB], [b_sgb])
                tt("dve", m1[:], sga[:], pA[:], ALU.mult, [b_sga, b_pA], [b_m1])
                tt("dve", sgb[:], sgb[:], pB[:], ALU.mult, [b_sgb, b_pB], [b_sgb])
                tt("dve", mT[:, f * 512:(f + 1) * 512], m1[:], sgb[:], ALU.add, [b_m1, b_sgb], [b_mT])
            for j in range(4):
                i = grp * 4 + j
                for half in range(2):
                    pO, b_pO = PS[5 + half]
                    for f in range(8):
                        pe_mm(pO[:], mT[:, f * 512 + j * 128:f * 512 + (j + 1) * 128], Wo[:, f * 1024 + half * 512:f * 1024 + (half + 1) * 512],
                              f == 0, f == 7, [b_mT, b_Wo], [b_pO])
                    tt("dve", ot[half][0][:], pO[:], xd[j][0][:, half * 512:(half + 1) * 512], ALU.add, [b_pO, xd[j][1]], [ot[half][1]])
                    C.dma(out_d[i, :, half * 512:(half + 1) * 512], ot[half][0][:], reads=[ot[half][1]], writes=[b_out[i]])
        C.barrier()
        phD.close()
        phB_done = True
    except _Stop:
        C.barrier()
    return nc, C, None


def _bf(a):
    return np.asarray(a, np.float32).astype(ml_dtypes.bfloat16)


def host_constants(rel_bias):
    rb = np.asarray(rel_bias, np.float32)
    cst = {}
    cst["c_identb"] = _bf(np.eye(128))
    cst["c_identf"] = np.eye(128, dtype=np.float32)
    p = np.arange(128)
    cst["c_onesbd"] = _bf((p[:, None] // 64) == (p[None, :] // 64))
    k = np.arange(128)[:, None]
    i = np.arange(128)[None, :]
    d = i - k
    td = np.where(d[:, None, :] >= 0, rb[t5_bucket_np(d)].transpose(0, 2, 1), np.float32(NEG))
    cst["c_td"] = np.ascontiguousarray(td.reshape(128, 1024), np.float32)
    d = i - k + 128
    tp = rb[t5_bucket_np(d)].transpose(0, 2, 1)
    cst["c_tp"] = np.ascontiguousarray(tp.reshape(128, 1024), np.float32)
    tw0 = np.where(i < k, np.float32(0), np.float32(NEG)).astype(np.float32)
    cst["c_tw0"] = np.ascontiguousarray(np.tile(tw0, (1, 4)), np.float32)
    r = np.arange(128)[:, None]
    d = i - 16 * (r - 64) - 31
    gc = np.where(d[:, None, :] >= 0, rb[t5_bucket_np(d)].transpose(0, 2, 1), np.float32(NEG))
    gc[127] = NEG
    cst["c_gc"] = np.ascontiguousarray(gc.reshape(128, 1024), np.float32)
    cst["c_b31"] = np.ascontiguousarray(np.tile(rb[31][None, :], (128, 1)), np.float32)
    xx = np.arange(384)[None, :] - 128
    bs = (r == xx).astype(np.float32)
    bs[127] = (xx[0] >= 127).astype(np.float32)
    cst["c_bigsh"] = bs
    pos = np.arange(4096)[None, :]
    cst["c_ind"] = _bf((pos // 64) == np.arange(64)[:, None])
    pool = np.zeros((128, 8, 256), np.float32)
    for Cc in range(8):
        cb = 128 * Cc + np.arange(128)
        for j in range(256):
            pool[:, Cc, j] = (cb >= 4 * j - 1) & (cb <= 4 * j + 3)
    cst["c_pool"] = _bf(pool.reshape(128, 2048))
    ii = np.arange(128)[:, None]
    hi = (ii >= 64).astype(np.int64)
    xr = np.arange(512)[None, :] - 256
    wrel = np.where(xr > hi, np.float32(-1.0), np.where((xr == hi) | (xr == hi - 1), np.float32(1e9), np.float32(0)))
    cst["c_wrel"] = wrel.astype(np.float32)
    return cst


def core_constants(c):
    sh = 7 - c
    d = {}
    chv = (np.arange(128) >= sh).astype(np.float32)
    d["c_valid"] = np.ascontiguousarray(np.tile(chv[None, :], (128, 1)), np.float32)
    cb = np.arange(128)[:, None] + 128 * np.arange(8)[None, :]
    d["c_cvalid"] = (cb >= 8 * sh).astype(np.float32)
    posn = np.arange(1024)
    d["c_lmask"] = _bf(np.tile((posn >= 128 * sh)[None, :], (128, 1)))
    j = np.arange(256)
    wc = np.where(j < 2 * sh, np.float32(-3e9), np.where(j == 2 * sh, np.float32(1e9), np.float32(0)))
    d["c_wcore"] = np.ascontiguousarray(np.tile(wc[None, :], (128, 1)), np.float32)
    return d


_PROG = {}


def kernel(x, norm_gain, w_in, q_norm_gain, k_norm_gain, cmp_pe, cmp_w1, cmp_b1, cmp_w2, rel_bias,
           conv_w, conv_b, lru_wa, lru_ba, lru_wx, lru_bx, lru_lambda, w_proj_a, w_proj_b, w_out):
    if "nc" not in _PROG:
        _PROG["nc"] = build_program()[0]
    nc = _PROG["nc"]
    f = lambda a: np.ascontiguousarray(np.asarray(a), np.float32)
    shared = dict(norm_gain=f(norm_gain), w_in=f(w_in), q_norm_gain=f(q_norm_gain), k_norm_gain=f(k_norm_gain),
                  cmp_pe=f(cmp_pe), cmp_w1=f(cmp_w1), cmp_b1=f(cmp_b1), cmp_w2=f(cmp_w2), conv_w=f(conv_w),
                  conv_b=f(conv_b), lru_wa=f(lru_wa), lru_ba=f(lru_ba), lru_wx=f(lru_wx), lru_bx=f(lru_bx),
                  lru_lambda=f(lru_lambda), w_proj_a=f(w_proj_a), w_proj_b=f(w_proj_b), w_out=f(w_out))
    shared.update(host_constants(rel_bias))
    xs = f(x)[0]
    in_maps = []
    for c in range(8):
        sh = 7 - c
        xc = np.zeros((S, 1024), np.float32)
        xc[sh * 128:] = xs[:S - sh * 128]
        m = dict(shared)
        m["x"] = xc
        m.update(core_constants(c))
        in_maps.append(m)
    res = run_bass_kernel_spmd(nc, in_maps, core_ids=list(range(8)))
    out = np.zeros((1, S, 1024), np.float32)
    for c in range(8):
        o = np.asarray(res.results[c]["out"])
        for i in range(16):
            m_ = 8 * i + c
            out[0, m_ * 128:(m_ + 1) * 128] = o[i]
    _PROG["last"] = res
    return out
```
